# Optimizing a Trainium2 kernel written in Bass

```python
import jax, jax.numpy as jnp
from jax import lax
import numpy as np

D_MODEL = 2048
BATCH = 2
SEQ = 16384
DEPTH = 1
DEC_BATCH = 4
DEC_SEQ = 8192
PAST_LEN = 128

N_HEADS_A = 8
DK_A = 128
DV_A = 256
W_QK = N_HEADS_A * DK_A
W_V = N_HEADS_A * DV_A
N_GATES = 4 * N_HEADS_A
CONV_W = 5
CHUNK = 128
N_GROUPS_B = 8
GROUP_B = 128
W_B = N_GROUPS_B * GROUP_B
D_FF = ((8 * D_MODEL // 3 + 255) // 256) * 256
OFF_Q = 0
OFF_K = OFF_Q + W_QK
OFF_V = OFF_K + W_QK
OFF_O = OFF_V + W_V
OFF_G = OFF_O + W_V
OFF_B = OFF_G + N_GATES
OFF_M = OFF_B + W_B
W_IN = OFF_M + 2 * D_MODEL
EPS = 1e-6

kernel_name = "hybrid_mlstm_fnet_gated_encoder"


def rmsnorm(x, g):
    xf = x.astype(jnp.float32)
    y = xf * lax.rsqrt(jnp.mean(xf * xf, axis=-1, keepdims=True) + EPS) * g.astype(jnp.float32)
    return y.astype(x.dtype)


def short_conv(u, w):
    c = u.shape[-1]
    return lax.conv_general_dilated(
        u, w[:, None, :].astype(u.dtype), window_strides=(1,),
        padding=[(CONV_W // 2, CONV_W // 2)],
        dimension_numbers=("NWC", "WIO", "NWC"), feature_group_count=c)


def mlstm_chunkwise(q, k, v, log_i, log_f):
    B, T, H, _ = q.shape
    nc = T // CHUNK

    def chunks(a):
        return a.reshape((B, nc, CHUNK) + a.shape[2:]).transpose(1, 0, 3, 2, 4)

    def gchunks(a):
        return a.reshape(B, nc, CHUNK, H).transpose(1, 0, 3, 2)

    tril = jnp.tril(jnp.ones((CHUNK, CHUNK), dtype=bool))

    def step(carry, inp):
        C, n, m = carry
        qc, kc, vc, li, lf = inp
        b = jnp.cumsum(lf, axis=-1)
        dmat = b[..., :, None] - b[..., None, :] + li[..., None, :]
        dmat = jnp.where(tril, dmat, -jnp.inf)
        inter = b + m[..., None]
        m_t = jnp.maximum(inter, jnp.max(dmat, axis=-1))
        w_inter = jnp.exp(inter - m_t)
        s = jnp.einsum("bhtd,bhsd->bhts", qc, kc) * jnp.exp(dmat - m_t[..., None])
        num = (w_inter[..., None] * jnp.einsum("bhvd,bhtd->bhtv", C, qc)
               + jnp.einsum("bhts,bhsv->bhtv", s, vc))
        nq = w_inter * jnp.einsum("bhd,bhtd->bht", n, qc) + jnp.sum(s, axis=-1)
        h = num / jnp.maximum(jnp.abs(nq), jnp.exp(-m_t))[..., None]
        b_last = b[..., -1]
        dec = b_last[..., None] - b + li
        m_new = jnp.maximum(b_last + m, jnp.max(dec, axis=-1))
        w_c = jnp.exp(b_last + m - m_new)
        w_s = jnp.exp(dec - m_new[..., None])
        C_new = w_c[..., None, None] * C + jnp.einsum("bhs,bhsv,bhsd->bhvd", w_s, vc, kc)
        n_new = w_c[..., None] * n + jnp.einsum("bhs,bhsd->bhd", w_s, kc)
        return (C_new, n_new, m_new), h

    init = (jnp.zeros((B, H, DV_A, DK_A), jnp.float32),
            jnp.zeros((B, H, DK_A), jnp.float32),
            jnp.zeros((B, H), jnp.float32))
    _, hs = lax.scan(step, init, (chunks(q), chunks(k), chunks(v), gchunks(log_i), gchunks(log_f)))
    return hs.transpose(1, 0, 3, 2, 4).reshape(B, T, H, DV_A)


def encoder_layer(x, g_pre_mix, w_in, conv_w, b_gates, g_head, w_a_out, w_b_out,
                  b_merge, w_out, g_post_mix, g_pre_ffn, w_ffn_in, w_ffn_out, g_post_ffn):
    B, T, _ = x.shape
    dt = x.dtype
    h = rmsnorm(x, g_pre_mix)
    z = h @ w_in
    qk_raw = z[..., OFF_Q:OFF_V]
    v_raw = z[..., OFF_V:OFF_O]
    o_pre = z[..., OFF_O:OFF_G]
    gate_pre = z[..., OFF_G:OFF_B]
    u_b = z[..., OFF_B:OFF_M]
    merge_pre = z[..., OFF_M:]

    qk = jax.nn.silu(short_conv(qk_raw, conv_w)).astype(jnp.float32)
    q = qk[..., :W_QK].reshape(B, T, N_HEADS_A, DK_A)
    k = qk[..., W_QK:].reshape(B, T, N_HEADS_A, DK_A) * (DK_A ** -0.5)
    v = v_raw.astype(jnp.float32).reshape(B, T, N_HEADS_A, DV_A)
    g = gate_pre.astype(jnp.float32) + b_gates.astype(jnp.float32)
    li_f, f_f, li_b, f_b = jnp.split(g, 4, axis=-1)
    lf_f = jax.nn.log_sigmoid(f_f)
    lf_b = jax.nn.log_sigmoid(f_b)
    h_fwd = mlstm_chunkwise(q, k, v, li_f, lf_f)
    h_bwd = jnp.flip(mlstm_chunkwise(jnp.flip(q, 1), jnp.flip(k, 1), jnp.flip(v, 1),
                                     jnp.flip(li_b, 1), jnp.flip(lf_b, 1)), axis=1)
    ha = h_fwd + h_bwd
    ha = ha * lax.rsqrt(jnp.mean(ha * ha, axis=-1, keepdims=True) + EPS)
    ha = (ha.reshape(B, T, W_V) * g_head.astype(jnp.float32)).astype(dt) * jax.nn.sigmoid(o_pre)
    y_a = ha @ w_a_out

    ub = u_b.astype(jnp.float32).reshape(B, T, N_GROUPS_B, GROUP_B)
    fb = jnp.fft.fft2(ub, axes=(1, 3), norm="ortho").real
    y_b = fb.astype(dt).reshape(B, T, W_B) @ w_b_out

    gates = jax.nn.sigmoid(merge_pre + b_merge)
    g_a = gates[..., :D_MODEL]
    g_b = gates[..., D_MODEL:]
    mix = (g_a * y_a + g_b * y_b) @ w_out
    x = x + rmsnorm(mix, g_post_mix)

    h2 = rmsnorm(x, g_pre_ffn)
    gu = h2 @ w_ffn_in
    ff = (jax.nn.silu(gu[..., :D_FF]) * gu[..., D_FF:]) @ w_ffn_out
    return x + rmsnorm(ff, g_post_ffn)


def run_trunk(x, g_pre_mix, w_in, conv_w, b_gates, g_head, w_a_out, w_b_out,
              b_merge, w_out, g_post_mix, g_pre_ffn, w_ffn_in, w_ffn_out, g_post_ffn):
    for l in range(DEPTH):
        x = encoder_layer(x, g_pre_mix[l], w_in[l], conv_w[l], b_gates[l], g_head[l],
                          w_a_out[l], w_b_out[l], b_merge[l], w_out[l], g_post_mix[l],
                          g_pre_ffn[l], w_ffn_in[l], w_ffn_out[l], g_post_ffn[l])
    return x


def setup_inputs(seed: int = 0) -> dict:
    key = jax.random.key(seed)
    ks = jax.random.split(key, 20)
    f32 = jnp.float32

    def nrm(k, shape, fan_in):
        return jax.random.normal(k, shape, f32) * (fan_in ** -0.5)

    def gain(k, shape):
        return 1.0 + 0.02 * jax.random.normal(k, shape, f32)

    i_bias = 0.1 * jax.random.normal(ks[14], (DEPTH, 2, N_HEADS_A), f32)
    f_bias = (jnp.linspace(3.0, 6.0, N_HEADS_A, dtype=f32)[None, None, :]
              + 0.1 * jax.random.normal(ks[15], (DEPTH, 2, N_HEADS_A), f32))
    b_gates = jnp.stack([i_bias, f_bias], axis=2).reshape(DEPTH, N_GATES)

    return {
        "x_prompt": jax.random.normal(ks[0], (BATCH, SEQ, D_MODEL), f32),
        "x_sample": jax.random.normal(ks[1], (DEC_BATCH, DEC_SEQ, D_MODEL), f32),
        "g_pre_mix": gain(ks[2], (DEPTH, D_MODEL)),
        "w_in": nrm(ks[3], (DEPTH, D_MODEL, W_IN), D_MODEL),
        "conv_w": nrm(ks[4], (DEPTH, CONV_W, 2 * W_QK), CONV_W),
        "b_gates": b_gates,
        "g_head": gain(ks[5], (DEPTH, W_V)),
        "w_a_out": nrm(ks[6], (DEPTH, W_V, D_MODEL), W_V),
        "w_b_out": nrm(ks[7], (DEPTH, W_B, D_MODEL), W_B),
        "b_merge": 0.02 * jax.random.normal(ks[8], (DEPTH, 2 * D_MODEL), f32),
        "w_out": nrm(ks[9], (DEPTH, D_MODEL, D_MODEL), D_MODEL),
        "g_post_mix": gain(ks[10], (DEPTH, D_MODEL)),
        "g_pre_ffn": gain(ks[11], (DEPTH, D_MODEL)),
        "w_ffn_in": nrm(ks[12], (DEPTH, D_MODEL, 2 * D_FF), D_MODEL),
        "w_ffn_out": nrm(ks[13], (DEPTH, D_FF, D_MODEL), D_FF),
        "g_post_ffn": gain(ks[16], (DEPTH, D_MODEL)),
    }


def reference(x_prompt, x_sample, g_pre_mix, w_in, conv_w, b_gates, g_head, w_a_out, w_b_out,
              b_merge, w_out, g_post_mix, g_pre_ffn, w_ffn_in, w_ffn_out, g_post_ffn):
    y_prompt = run_trunk(x_prompt, g_pre_mix, w_in, conv_w, b_gates, g_head, w_a_out, w_b_out,
                         b_merge, w_out, g_post_mix, g_pre_ffn, w_ffn_in, w_ffn_out, g_post_ffn)
    y_sample = run_trunk(x_sample, g_pre_mix, w_in, conv_w, b_gates, g_head, w_a_out, w_b_out,
                         b_merge, w_out, g_post_mix, g_pre_ffn, w_ffn_in, w_ffn_out, g_post_ffn)
    return (y_prompt, y_sample)
```

```python
import contextlib
import numpy as np
import concourse.bass as bass
import concourse.mybir as mybir
from concourse.bass_utils import run_bass_kernel_spmd

F32 = mybir.dt.float32
BF16 = mybir.dt.bfloat16
AF = mybir.ActivationFunctionType
ALU = mybir.AluOpType

D = 2048
NH = 8
DK = 128
DV = 256
W_QK = 1024
W_V = 2048
OFF_Q = 0
OFF_K = 1024
OFF_V = 2048
OFF_O = 4096
OFF_G = 6144
OFF_B = 6176
OFF_M = 7200
W_IN = 11296
D_FF = 5632
EPS = 1e-6
LNC = float(np.log(128.0 ** -0.5))

ENGS = ("pe", "act", "dve", "pool", "sp")


class T:
    __slots__ = ("name", "w", "r", "untracked")

    def __init__(self, name="", untracked=False):
        self.name = name
        self.w = None
        self.r = []
        self.untracked = untracked


class Op:
    __slots__ = ("eng", "emit", "waits", "signal", "sidx", "dma")

    def __init__(self, eng, emit):
        self.eng = eng
        self.emit = emit
        self.waits = []
        self.signal = False
        self.sidx = 0
        self.dma = None


class Prog:
    def __init__(self, nc):
        self.nc = nc
        self.ops = {e: [] for e in ENGS}
        self.dma_cnt = {}
        self.waited = {e: {} for e in ENGS}

    def _need(self, op, ev):
        if ev is None:
            return
        e = op.eng
        if ev[0] == "e":
            _, eng2, idx = ev
            if eng2 == "pe" and e == "pe":
                return
            key = ("e", eng2)
            if self.waited[e].get(key, -1) >= idx:
                return
            self.waited[e][key] = idx
            self.ops[eng2][idx].signal = True
            op.waits.append(ev)
        else:
            _, sk, val = ev
            key = ("d", sk)
            if self.waited[e].get(key, -1) >= val:
                return
            self.waited[e][key] = val
            op.waits.append(ev)

    def _track(self, op, ev, reads, writes):
        reads = [t for t in reads if not t.untracked]
        writes = [t for t in writes if not t.untracked]
        for t in reads:
            self._need(op, t.w)
        for t in writes:
            self._need(op, t.w)
            for r in t.r:
                self._need(op, r)
        for t in reads:
            t.r.append(ev)
            if len(t.r) > 64:
                t.r = t.r[-48:]
        for t in writes:
            t.w = ev
            t.r = []

    def op(self, eng, emit, reads=(), writes=()):
        o = Op(eng, emit)
        ev = ("e", eng, len(self.ops[eng]))
        self._track(o, ev, reads, writes)
        self.ops[eng].append(o)
        return ev

    def x(self, eng, name, reads=(), writes=(), args=(), **kw):
        return self.op(eng, lambda e, name=name, args=args, kw=kw: getattr(e, name)(*args, **kw), reads, writes)

    def dma(self, q, out, in_, reads=(), writes=(), sem="dma", **kw):
        o = Op(q, lambda e: e.dma_start(out=out, in_=in_, **kw))
        n = self.dma_cnt.get(sem, 0) + 1
        self.dma_cnt[sem] = n
        ev = ("d", sem, 16 * n)
        o.dma = (sem, 16 * n)
        if n > 1:
            self._need(o, ("d", sem, 16 * (n - 1)))
        self._track(o, ev, reads, writes)
        self.ops[q].append(o)
        return ev

    def barrier(self):
        evs = []
        for e in ENGS:
            for i in range(len(self.ops[e]) - 1, -1, -1):
                o = self.ops[e][i]
                if o.dma is None and o.emit is not None:
                    evs.append(("e", e, i))
                    break
        for sk, n in self.dma_cnt.items():
            evs.append(("d", sk, 16 * n))
        for e in ENGS:
            o = Op(e, None)
            for ev in evs:
                if ev[0] == "e" and ev[1] == e:
                    continue
                if ev[0] == "e" and ev[1] == "pe" and e == "pe":
                    continue
                self._need(o, ev)
            if o.waits:
                self.ops[e].append(o)

    def finish(self):
        o = Op("sp", None)
        for sk, n in self.dma_cnt.items():
            self._need(o, ("d", sk, 16 * n))
        self.ops["sp"].append(o)

    def emit(self):
        nc = self.nc
        for e in ENGS:
            c = 0
            for o in self.ops[e]:
                if o.signal:
                    c += 1
                o.sidx = c
        with contextlib.ExitStack() as st:
            esem = {e: st.enter_context(nc.semaphore("S_" + e)) for e in ENGS}
            dsem = {sk: st.enter_context(nc.semaphore("D_%s" % (sk,))) for sk in self.dma_cnt}
            block = st.enter_context(nc.Block())

            def run(ename):
                def body(eng):
                    for o in self.ops[ename]:
                        for ev in o.waits:
                            if ev[0] == "e":
                                eng.wait_ge(esem[ev[1]], self.ops[ev[1]][ev[2]].sidx)
                            else:
                                eng.wait_ge(dsem[ev[1]], ev[2])
                        if o.emit is None:
                            continue
                        ins = o.emit(eng)
                        if o.dma is not None:
                            ins.then_inc(dsem[o.dma[0]], 16)
                        elif o.signal:
                            ins.then_inc(esem[ename], 1)
                return body

            block.tensor(run("pe"))
            block.scalar(run("act"))
            block.vector(run("dve"))
            block.gpsimd(run("pool"))
            block.sync(run("sp"))


DEBUG_OUT = ()


def build(NB):
    SEG = NB * 128
    NT = NB // 4
    nc = bass.Bass("TRN2", target_bir_lowering=False)
    P = Prog(nc)

    def din(name, shape, dt=F32):
        return nc.dram_tensor(name, list(shape), dt, kind="ExternalInput").ap()

    def dscr(name, shape, dt):
        return nc.dram_tensor(name, list(shape), dt, kind=("ExternalOutput" if name in DEBUG_OUT else "Internal")).ap()

    x_own = din("x_own", [SEG, D])
    x_oth = din("x_oth", [SEG, D])
    x_halo = din("x_halo", [128, D])
    w_in = din("w_in", [D, W_IN])
    w_a_out = din("w_a_out", [W_V, D])
    w_b_out = din("w_b_out", [1024, D])
    w_out = din("w_out", [D, D])
    w_ffn_in = din("w_ffn_in", [D, 2 * D_FF])
    w_ffn_out = din("w_ffn_out", [D_FF, D])
    w_go = din("w_go", [D, 16])
    vecs = din("vecs", [128, 16 * 4 + 32])
    convw = din("convw", [128, 16, 5])
    rep = din("rep", [128, 2048 + 32 + 16 + 4])
    cst = din("cst", [128, 5, 128])
    dftc = din("dftc", [128, 3, 256])
    gtw = din("gtw", [128, 128 * 2 * NB])
    y_out = nc.dram_tensor("y_out", [SEG, D], F32, kind="ExternalOutput").ap()

    WB = {}

    def add_wb(key, src, c0, ncols):
        K = src.shape[0]
        KC = K // 128
        scr = dscr("wb_%s" % key, [128, KC * ncols], BF16)
        WB[key] = dict(src=src, c0=c0, n=ncols, KC=KC, scr=scr, t=T("wb_" + key))

    for i in range(22):
        add_wb("in%d" % i, w_in, 512 * i if i < 12 else 0, 512)
    for i in range(2):
        WB["in%d" % (12 + i)]["c0"] = OFF_B + 512 * i
    for i in range(8):
        WB["in%d" % (14 + i)]["c0"] = OFF_M + 512 * i
    add_wb("gate", w_in, OFF_G, 32)
    add_wb("go", w_go, 0, 16)
    for i in range(4):
        add_wb("ao%d" % i, w_a_out, 512 * i, 512)
        add_wb("bo%d" % i, w_b_out, 512 * i, 512)
        add_wb("wo%d" % i, w_out, 512 * i, 512)
    for i in range(22):
        add_wb("fi%d" % i, w_ffn_in, 512 * i, 512)
    for i in range(16):
        add_wb("fo%d" % i, w_ffn_out, 128 * i, 128)

    q_s = dscr("q_s", [NH, 128, SEG + 512], BF16)
    k_s = dscr("k_s", [NH, 128, SEG + 512], BF16)
    ko_s = dscr("ko_s", [NH, 128, SEG + 512], BF16)
    v_s = dscr("v_s", [SEG, W_V], BF16)
    vo_s = dscr("vo_s", [SEG, W_V], BF16)
    sgo_s = dscr("sgo_s", [SEG, W_V], F32)
    V_s = dscr("V_s", [16, 2 * SEG, 128], BF16)
    hf_s = dscr("hf_s", [SEG, W_V], F32)
    ha_s = dscr("ha_s", [SEG, W_V], BF16)
    Y_s = dscr("Y_s", [SEG, 1024], BF16)
    gtw_s = dscr("gtw_s", [128, 128 * 2 * NB], BF16)
    dft_s = dscr("dft_s", [128, 3 * 256], BF16)
    Tq_s, Tk_s, Tko_s, Tv_s, Tvo_s, Tsgo_s, TV_s, Thf_s, Tha_s, TY_s = [T(untracked=True) for _ in range(10)]
    Tgtw_s, Tdft_s = T(), T()

    st = contextlib.ExitStack()
    with st:
        def sb(name, shape, dt, stack=st):
            return stack.enter_context(nc.sbuf_tensor(name, list(shape), dt))

        psb = [st.enter_context(nc.psum_tensor("ps%d" % i, [128, 512], F32)) for i in range(8)]
        Tps = [T("ps%d" % i) for i in range(8)]
        ps_rr = [0]

        ps_live = [False] * 8
        ps_mod = [7]

        def ps_next():
            i = ps_rr[0] % ps_mod[0]
            ps_rr[0] = (i + 1) % ps_mod[0]
            assert (not ps_live[i]) or len(Tps[i].r) > 0, "PSUM bank %d re-allocated before its consumer was recorded" % i
            ps_live[i] = True
            return psb[i], Tps[i]

        cst_t = sb("cst_t", [128, 5, 128], F32)
        Tcst = T()
        vec_t = sb("vec_t", [128, 96], F32)
        Tvec = T()
        cw_t = sb("cw_t", [128, 16, 5], F32)
        Tcw = T()
        rep_t = sb("rep_t", [128, 2100], F32)
        Trep = T()
        cb_t = sb("cb_t", [128, 4], F32)
        Tcb = T()
        identb = sb("identb", [128, 128], BF16)
        onesb = sb("onesb", [128, 128], BF16)
        Tib = T()
        P.dma("sp", cst_t[:], cst, writes=[Tcst], sem="c0")
        P.dma("sp", vec_t[:], vecs, writes=[Tvec], sem="c0")
        P.dma("sp", cw_t[:], convw, writes=[Tcw], sem="c0")
        P.dma("sp", rep_t[:], rep, writes=[Trep], sem="c0")
        P.x("pool", "memset", writes=[Tcb], args=(cb_t[:, 0:1], EPS,))
        P.x("pool", "memset", reads=[Tcb], writes=[Tcb], args=(cb_t[:, 1:2], LNC,))
        P.x("dve", "tensor_copy", reads=[Tcst], writes=[Tib], out=identb[:], in_=cst_t[:, 4, :])
        P.x("dve", "tensor_copy", reads=[Tcst, Tib], writes=[Tib], out=onesb[:], in_=cst_t[:, 3, :])
        U_ap, L_ap, Mo_ap, ones_ap, ident_ap = [cst_t[:, i, :] for i in range(5)]
        g_pre_mix = vec_t[:, 0:16]
        g_post_mix = vec_t[:, 16:32]
        g_pre_ffn = vec_t[:, 32:48]
        g_post_ffn = vec_t[:, 48:64]
        b_merge = vec_t[:, 64:96]
        ghead_rep = rep_t[:, 0:2048]
        bgate_rep = rep_t[:, 2048:2080]
        bgo_rep = rep_t[:, 2080:2096]
        flags = rep_t[:, 2096:2100]
        eps_ap = cb_t[:, 0:1]
        lnc_ap = cb_t[:, 1:2]

        def cast_wb(key):
            w = WB[key]
            src = w["src"][:, w["c0"]:w["c0"] + w["n"]].rearrange("(kc p) n -> p kc n", p=128)
            dst = w["scr"].rearrange("p (kc n) -> p kc n", n=w["n"])
            P.dma("pool", dst, src, writes=[w["t"]], sem="cast")

        P.dma("pool", dft_s, dftc.rearrange("p a b -> p (a b)"), writes=[Tdft_s], sem="cast")
        order0 = (["gate", "go"] + ["in%d" % i for i in (0, 1, 2, 3, 4, 5, 6, 7, 12, 13)])
        for key in order0:
            cast_wb(key)
        cast_rest = (["in%d" % i for i in (8, 9, 10, 11)] + ["GTW"] + ["in%d" % i for i in range(14, 22)] + ["ao%d" % i for i in range(4)]
                     + ["bo%d" % i for i in range(4)] + ["wo%d" % i for i in range(4)] + ["fi%d" % i for i in range(22)]
                     + ["fo%d" % i for i in range(16)])
        cast_per_tile = -(-len(cast_rest) // max(1, (2 * NT - 1)))

        def cast_some(n):
            for _ in range(n):
                if not cast_rest:
                    return
                key = cast_rest.pop(0)
                if key == "GTW":
                    P.dma("pool", gtw_s, gtw, writes=[Tgtw_s], sem="cast")
                else:
                    cast_wb(key)

        class WStream:
            def __init__(self, nslots, stack, tag):
                self.n = nslots
                self.slots = [sb("wr%s%d" % (tag, i), [128, 8192], BF16, stack) for i in range(nslots)]
                self.T = [T() for _ in range(nslots)]
                self.tag = tag
                self.order = []
                self.issued = 0
                self.taken = 0

            def set_order(self, order):
                self.order = list(order)
                self.issued = 0
                self.taken = 0

            def _issue(self):
                key = self.order[self.issued]
                s = self.issued % self.n
                w = WB[key]
                P.dma("sp", self.slots[s][:, 0:w["KC"] * w["n"]], w["scr"], reads=[w["t"]], writes=[self.T[s]],
                      sem="wr%s%d" % (self.tag, s))
                self.issued += 1

            def get(self, key):
                assert self.order[self.taken] == key, (self.order[self.taken], key)
                while self.issued < len(self.order) and self.issued < self.taken + self.n:
                    self._issue()
                s = self.taken % self.n
                self.taken += 1
                w = WB[key]
                ap = self.slots[s][:, 0:w["KC"] * w["n"]].rearrange("p (kc n) -> p kc n", n=w["n"])
                return ap, self.T[s]

            def prefetch(self):
                while self.issued < len(self.order) and self.issued < self.taken + self.n:
                    self._issue()

        class Stats:
            def __init__(self, sq, Tsq):
                self.sq, self.Tsq = sq, Tsq
                self.n = 0
                self.pend = []

            def _flush(self):
                for f in self.pend:
                    f()
                self.pend = []

            def add(self, src, Tsrc, nchunk, col0, ntok, first, last):
                s2 = self.n % 2
                self.n += 1
                sqv = self.sq[s2][:, 0:nchunk * ntok].rearrange("p (n t) -> p n t", n=nchunk)
                P.x("act", "activation", reads=Tsrc, writes=[self.Tsq[s2]], out=sqv, in_=src, func=AF.Square)
                self._flush()

                def mm(s2=s2, sqv=sqv):
                    for i in range(nchunk):
                        P.x("pe", "matmul", reads=[self.Tsq[s2], Tib], writes=[Tps[7]], out=psb[7][:, col0:col0 + ntok], lhsT=onesb[:], rhs=sqv[:, i, :],
                            start=(first and i == 0), stop=(last and i == nchunk - 1))
                self.pend.append(mm)

            def finish(self, rstd, Trstd, ntok=512):
                self._flush()
                P.x("act", "activation", reads=[Tps[7], Tcb], writes=[Trstd], out=rstd[:, 0:ntok], in_=psb[7][:, 0:ntok], func=AF.Sqrt, scale=1.0 / D, bias=eps_ap)
                P.x("dve", "reciprocal", reads=[Trstd], writes=[Trstd], out=rstd[:, 0:ntok], in_=rstd[:, 0:ntok])

        def load_xT(xsrc, tok0, x_tm, Tx_tm, xT, TxT, nblk=4, stats=None):
            for blk in range(nblk):
                b2 = blk % len(x_tm)
                P.dma("sp", x_tm[b2][:], xsrc[tok0 + blk * 128: tok0 + (blk + 1) * 128, :], writes=[Tx_tm[b2]],
                      sem="xtm%d" % b2)
                for cg in range(4):
                    ps, tp = ps_next()
                    for i in range(4):
                        c = cg * 4 + i
                        P.x("pe", "transpose", reads=[Tx_tm[b2], Tcst], writes=[tp],
                            args=(ps[:, i * 128:(i + 1) * 128], x_tm[b2][:, c * 128:(c + 1) * 128], ident_ap,))
                    dst = xT[:, cg * 4:cg * 4 + 4, blk * 128:(blk + 1) * 128]
                    P.x("act", "copy", reads=[tp], writes=[TxT[cg * 4 + i] for i in range(4)], out=dst, in_=ps[:].rearrange("p (i t) -> p i t", i=4))
                    if stats is not None:
                        stats.add(dst, [TxT[cg * 4 + i] for i in range(4)], 4, blk * 128, 128, cg == 0, cg == 3)

        def fm_apply(src, Tsrc, gcol, dst, Tdst, rstd, Trstd, ntok=512):
            for c in range(16):
                P.x("dve", "scalar_tensor_tensor", reads=[Tsrc[c], Trstd, Tvec], writes=[Tdst[c]], out=dst[:, c, 0:ntok], in0=src[:, c, 0:ntok],
                    scalar=gcol[:, c:c + 1], in1=rstd[:, 0:ntok], op0=ALU.mult, op1=ALU.mult)

        def mm_ii(ps, tp, wap, wT, col0, act, Tact, KC, ntok=512, m=128):
            for k in range(KC):
                def mm(e, k=k):
                    return e.matmul(ps[0:m, 0:ntok], lhsT=wap[:, k, col0:col0 + m], rhs=act[:, k, 0:ntok],
                                    start=(k == 0), stop=(k == KC - 1))
                P.op("pe", mm, reads=[wT, Tact[k]], writes=[tp])

        def mm_i(ps, tp, wap, wT, c0, n, act, Tact, blk, KC):
            for k in range(KC):
                def mm(e, k=k):
                    return e.matmul(ps[:, 0:n], lhsT=act[:, k, blk * 128:(blk + 1) * 128], rhs=wap[:, k, c0:c0 + n],
                                    start=(k == 0), stop=(k == KC - 1))
                P.op("pe", mm, reads=[wT, Tact[k]], writes=[tp])

        st12 = contextlib.ExitStack()
        st12.__enter__()
        gs = sb("gs", [128, NB, 64], F32, st12)
        Tgs = [T() for _ in range(NB)]
        gso = sb("gso", [128, NB, 16], F32, st12)
        Tgso = [T() for _ in range(NB)]
        TC = [T() for _ in range(NH)]
        TCb = [T() for _ in range(NH)]
        Tn = T()
        Tnb = T()
        TCo = T()
        halo_raw = sb("halo_raw", [128, 16, 8], F32, st12)
        Thalo = T()
        pfx = sb("pfx", [128, 8], F32, st12)
        Tpfx = T()

        st1 = contextlib.ExitStack()
        st1.__enter__()
        x_tm = [sb("x_tm%d" % i, [128, D], F32, st1) for i in range(2)]
        Tx_tm = [T() for _ in range(2)]
        xT = sb("xT", [128, 16, 512], F32, st1)
        TxT = [T() for _ in range(16)]
        hT = sb("hT", [128, 16, 512], BF16, st1)
        ThT = [T() for _ in range(16)]
        sq = [sb("sq%d" % i, [128, 512], BF16, st1) for i in range(2)]
        Tsq = [T() for _ in range(2)]
        rstd = sb("rstd", [128, 512], F32, st1)
        Trstd = T()
        wg_t = sb("wg_t", [128, 16, 32], BF16, st1)
        wgo_t = sb("wgo_t", [128, 16, 16], BF16, st1)
        Twg = T()
        dft_t = sb("dft_t", [128, 3, 256], BF16, st1)
        Tdft = T()
        raw = [sb("raw%d" % i, [128, 516], F32, st1) for i in range(2)]
        Traw = [T() for _ in range(2)]
        acc = [sb("acc%d" % i, [128, 512], F32, st1) for i in range(2)]
        Tacc = [T() for _ in range(2)]
        qko = [sb("qko%d" % i, [128, 512], BF16, st1) for i in range(2)]
        Tqko = [T() for _ in range(2)]
        carry = sb("carry", [128, 16, 4], F32, st1)
        Tcarry = [T() for _ in range(16)]
        vst = [sb("vst%d" % i, [128, 4, 512], BF16, st1) for i in range(2)]
        Tvst = [T() for _ in range(2)]
        sgst = sb("sgst", [128, 4, 512], F32, st1)
        Tsgst = T()
        uT = sb("uT", [128, 8, 512], BF16, st1)
        TuT = [T() for _ in range(8)]
        Vst = sb("Vst", [128, 4, 16 * 128], BF16, st1)
        TVst = [T() for _ in range(4)]
        g32 = sb("g32", [128, 32], F32, st1)
        sp16 = sb("sp16", [128, 16], F32, st1)
        t16 = sb("t16", [128, 16], F32, st1)
        t16b = sb("t16b", [128, 16], F32, st1)
        Tg32, Tsp16, Tt16, Tt16b = T(), T(), T(), T()
        c32 = sb("c32", [128, 32], F32, st1)
        Tc32 = T()
        ws1 = WStream(3, st1, "a")

        P.dma("sp", wg_t[:], WB["gate"]["scr"].rearrange("p (k n) -> p k n", n=32), reads=[WB["gate"]["t"]], writes=[Twg], sem="c1")
        P.dma("sp", wgo_t[:], WB["go"]["scr"].rearrange("p (k n) -> p k n", n=16), reads=[WB["go"]["t"]], writes=[Twg], sem="c1")
        P.dma("sp", dft_t[:], dft_s.rearrange("p (a b) -> p a b", a=3), reads=[Tdft_s], writes=[Tdft], sem="c1")
        P.x("pool", "memset", writes=[Tpfx], args=(pfx[:], 0.0,))
        for i in range(2):
            P.x("pool", "memset", writes=[Traw[i]], args=(raw[i][:], 0.0,))

        def conv_emit(ridx, ch, ncol, dst_dram, dcol0, Tdst):
            r = raw[ridx]
            a = acc[ridx]
            o = qko[ridx]
            P.x("dve", "tensor_scalar", reads=[Traw[ridx], Tcw], writes=[Tacc[ridx]], out=a[:, 0:ncol], in0=r[:, 0:ncol], scalar1=cw_t[:, ch, 0:1], scalar2=None,
                                                 op0=ALU.mult)
            for j in range(1, 5):
                P.x("dve", "scalar_tensor_tensor", reads=[Traw[ridx], Tacc[ridx], Tcw], writes=[Tacc[ridx]], out=a[:, 0:ncol], in0=r[:, j:j + ncol], scalar=cw_t[:, ch, j:j + 1],
                                                                 in1=a[:, 0:ncol], op0=ALU.mult, op1=ALU.add)
            P.x("act", "activation", reads=[Tacc[ridx]], writes=[Tqko[ridx]], out=o[:, 0:ncol], in_=a[:, 0:ncol], func=AF.Silu)
            P.dma("pool", dst_dram[:, dcol0:dcol0 + ncol], o[:, 0:ncol], reads=[Tqko[ridx]], writes=[Tdst], sem="qko%d" % ridx)

        ws1.set_order(["in0", "in1", "in2", "in3"])
        stats1 = Stats(sq, Tsq)
        load_xT(x_halo, 0, x_tm, Tx_tm, xT, TxT, nblk=1, stats=stats1)
        stats1.finish(rstd, Trstd, ntok=128)
        fm_apply(xT, TxT, g_pre_mix, hT, ThT, rstd, Trstd, ntok=128)
        for wb in range(4):
            wap, wT = ws1.get("in%d" % wb)
            for hh in range(4):
                ch = wb * 4 + hh
                ps, tp = ps_next()
                mm_ii(ps, tp, wap, wT, hh * 128, hT, ThT, 16, ntok=128)
                P.x("act", "copy", reads=[tp], writes=[Thalo], out=halo_raw[:, ch, :], in_=ps[:, 0:8])

        def phase1_segment(own):
            xsrc = x_own if own else x_oth
            wkeys = (["in%d" % i for i in range(14)] if own else ["in2", "in3", "in4", "in5", "in6", "in7", "in12", "in13"])
            ws1.set_order(wkeys * NT)
            ks = k_s if own else ko_s
            Tks = Tk_s if own else Tko_s
            vs = v_s if own else vo_s
            Tvs = Tv_s if own else Tvo_s
            lo = 0 if own else 4
            for ch in range(16):
                if (not own) and ch < 8:
                    continue
                P.x("pool", "memset", reads=[], writes=[Tcarry[ch]], args=(carry[:, ch, 0:2], 0.0,))
                P.x("pool", "tensor_copy", reads=[Thalo, Tcarry[ch]], writes=[Tcarry[ch]], out=carry[:, ch, 2:4], in_=halo_raw[:, ch, lo:lo + 2])
            rr = [0]
            for it in range(NT):
                tok0 = it * 512
                load_xT(xsrc, tok0, x_tm, Tx_tm, xT, TxT, stats=stats1)
                stats1.finish(rstd, Trstd)
                fm_apply(xT, TxT, g_pre_mix, hT, ThT, rstd, Trstd)
                def gatesA(blk):
                    c = it * 4 + blk
                    ps, tp = ps_next()
                    if own:
                        for k in range(16):
                            P.x("pe", "matmul", reads=[ThT[k], Twg], writes=[tp], out=ps[:, 0:32], lhsT=hT[:, k, blk * 128:(blk + 1) * 128],
                                                                             rhs=wg_t[:, k, :], start=(k == 0), stop=(k == 15))
                        P.x("dve", "tensor_tensor", reads=[tp, Trep], writes=[Tg32], out=g32[:], in0=ps[:, 0:32], in1=bgate_rep, op=ALU.add)
                        gv = g32[:].rearrange("p (d j h) -> p d j h", d=2, j=2)
                        P.x("act", "activation", reads=[Tg32], writes=[Tsp16], out=sp16[:].rearrange("p (d h) -> p d h", d=2), in_=gv[:, :, 1, :],
                                                                func=AF.Exp, scale=-1.0)
                        P.x("act", "activation", reads=[Tsp16], writes=[Tsp16], out=sp16[:], in_=sp16[:], func=AF.Ln, bias=1.0)
                    else:
                        for k in range(16):
                            P.x("pe", "matmul", reads=[ThT[k], Twg], writes=[tp], out=ps[:, 0:16], lhsT=hT[:, k, blk * 128:(blk + 1) * 128],
                                                                             rhs=wgo_t[:, k, :], start=(k == 0), stop=(k == 15))
                        P.x("dve", "tensor_tensor", reads=[tp, Trep], writes=[Tg32], out=g32[:, 0:16], in0=ps[:, 0:16], in1=bgo_rep, op=ALU.add)
                        P.x("act", "activation", reads=[Tg32], writes=[Tsp16], out=sp16[:, 0:8], in_=g32[:, 8:16], func=AF.Exp, scale=-1.0)
                        P.x("act", "activation", reads=[Tsp16], writes=[Tsp16], out=sp16[:, 0:8], in_=sp16[:, 0:8], func=AF.Ln, bias=1.0)

                def gatesB(blk):
                    c = it * 4 + blk
                    if own:
                        gv = g32[:].rearrange("p (d j h) -> p d j h", d=2, j=2)
                        ps2, tp2 = ps_next()
                        P.x("pe", "matmul", reads=[Tsp16, Tcst], writes=[tp2], out=ps2[:, 0:8], lhsT=U_ap, rhs=sp16[:, 0:8], start=True, stop=True)
                        P.x("pe", "matmul", reads=[Tsp16, Tcst], writes=[tp2], out=ps2[:, 8:16], lhsT=L_ap, rhs=sp16[:, 8:16], start=True, stop=True)
                        P.x("pe", "matmul", reads=[Tsp16, Tcst], writes=[tp2], out=ps2[:, 16:32], lhsT=ones_ap, rhs=sp16[:, 0:16], start=True, stop=True)
                        P.x("dve", "tensor_copy", reads=[tp2], writes=[Tc32], out=c32[:], in_=ps2[:, 0:32])
                        P.x("dve", "tensor_tensor", reads=[Tg32, Tc32], writes=[Tt16], out=t16[:].rearrange("p (d h) -> p d h", d=2), in0=gv[:, :, 0, :],
                                                                            in1=c32[:, 0:16].rearrange("p (d h) -> p d h", d=2), op=ALU.add)
                        P.x("act", "activation", reads=[Tt16, Tcb], writes=[Tgs[c]], out=gs[:, c, 0:16], in_=t16[:], func=AF.Exp, bias=lnc_ap)
                        P.x("dve", "tensor_tensor", reads=[Tt16, Tc32], writes=[Tt16b], out=t16b[:], in0=t16[:], in1=c32[:, 16:32], op=ALU.subtract)
                        P.x("act", "activation", reads=[Tt16b, Tcb], writes=[Tgs[c]], out=gs[:, c, 16:32], in_=t16b[:], func=AF.Exp, bias=lnc_ap)
                        P.x("act", "activation", reads=[Tc32], writes=[Tgs[c]], out=gs[:, c, 32:48], in_=c32[:, 0:16], func=AF.Exp)
                        P.x("act", "activation", reads=[Tc32], writes=[Tgs[c]], out=gs[:, c, 48:64], in_=c32[:, 16:32], func=AF.Exp, scale=-1.0)
                    else:
                        ps2, tp2 = ps_next()
                        P.x("pe", "matmul", reads=[Tsp16, Tcst], writes=[tp2], out=ps2[:, 0:8], lhsT=Mo_ap, rhs=sp16[:, 0:8], start=True, stop=True)
                        P.x("pe", "matmul", reads=[Tsp16, Tcst], writes=[tp2], out=ps2[:, 8:16], lhsT=ones_ap, rhs=sp16[:, 0:8], start=True, stop=True)
                        P.x("dve", "tensor_copy", reads=[tp2], writes=[Tc32], out=c32[:, 0:16], in_=ps2[:, 0:16])
                        P.x("dve", "tensor_tensor", reads=[Tg32, Tc32], writes=[Tt16], out=t16[:, 0:8], in0=g32[:, 0:8], in1=c32[:, 0:8], op=ALU.subtract)
                        P.x("act", "activation", reads=[Tt16, Tcb], writes=[Tt16], out=t16[:, 0:8], in_=t16[:, 0:8], func=AF.Exp, bias=lnc_ap)
                        P.x("act", "activation", reads=[Tpfx, Trep], writes=[Tt16b], out=t16b[:, 0:8], in_=pfx[:], func=AF.Exp, scale=flags[:, 3:4])
                        P.x("dve", "tensor_tensor", reads=[Tt16, Tt16b], writes=[Tgso[c]], out=gso[:, c, 0:8], in0=t16[:, 0:8], in1=t16b[:, 0:8], op=ALU.mult)
                        P.x("act", "activation", reads=[Tc32, Trep], writes=[Tgso[c]], out=gso[:, c, 8:16], in_=c32[:, 8:16], func=AF.Exp, scale=flags[:, 2:3])
                        P.x("dve", "tensor_tensor", reads=[Tpfx, Tc32], writes=[Tpfx], out=pfx[:], in0=pfx[:], in1=c32[:, 8:16], op=ALU.add)
                for wb in ((0, 1, 2, 3) if own else (2, 3)):
                    wap, wT = ws1.get("in%d" % wb)
                    for hh in range(4):
                        ch = wb * 4 + hh
                        ps, tp = ps_next()
                        mm_ii(ps, tp, wap, wT, hh * 128, hT, ThT, 16)
                        ri = rr[0] % 2
                        rr[0] += 1
                        r = raw[ri]
                        P.x("dve", "tensor_copy", reads=[Tcarry[ch]], writes=[Traw[ri]], out=r[:, 0:4], in_=carry[:, ch, :])
                        P.x("act", "copy", reads=[tp, Traw[ri]], writes=[Traw[ri]], out=r[:, 4:516], in_=ps[:])
                        P.x("dve", "tensor_copy", reads=[Traw[ri]], writes=[Tcarry[ch]], out=carry[:, ch, :], in_=r[:, 512:516])
                        dst = (q_s if ch < 8 else ks)[ch % 8]
                        conv_emit(ri, ch, 512, dst, tok0, Tq_s if ch < 8 else Tks)
                for cb in range(4):
                    wap, wT = ws1.get("in%d" % (4 + cb))
                    v2 = cb % 2
                    for blk in range(4):
                        ps, tp = ps_next()
                        mm_i(ps, tp, wap, wT, 0, 512, hT, ThT, blk, 16)
                        P.x("act", "copy", reads=[tp], writes=[Tvst[v2]], out=vst[v2][:, blk, :], in_=ps[:])
                    P.dma("pool", vs[tok0:tok0 + 512, cb * 512:(cb + 1) * 512].rearrange("(b p) f -> p b f", p=128), vst[v2][:],
                          reads=[Tvst[v2]], writes=[Tvs], sem="vst%d" % v2)
                    if cb > 0:
                        gatesB(cb - 1)
                    gatesA(cb)
                if own:
                    for cb in range(4):
                        wap, wT = ws1.get("in%d" % (8 + cb))
                        for blk in range(4):
                            ps, tp = ps_next()
                            mm_i(ps, tp, wap, wT, 0, 512, hT, ThT, blk, 16)
                            P.x("act", "activation", reads=[tp], writes=[Tsgst], out=sgst[:, blk, :], in_=ps[:], func=AF.Sigmoid)
                            P.x("dve", "tensor_tensor", reads=[Tsgst, Trep], writes=[Tsgst], out=sgst[:, blk, :], in0=sgst[:, blk, :],
                                in1=ghead_rep[:, cb * 512:(cb + 1) * 512], op=ALU.mult)
                        P.dma("pool", sgo_s[tok0:tok0 + 512, cb * 512:(cb + 1) * 512].rearrange("(b p) f -> p b f", p=128), sgst[:],
                              reads=[Tsgst], writes=[Tsgo_s], sem="sgst")
                pendV = [None]
                for ub in range(2):
                    wap, wT = ws1.get("in%d" % (12 + ub))
                    for gg in range(4):
                        g = ub * 4 + gg
                        ps, tp = ps_next()
                        mm_ii(ps, tp, wap, wT, gg * 128, hT, ThT, 16)
                        P.x("act", "copy", reads=[tp], writes=[TuT[g]], out=uT[:, g, :], in_=ps[:])
                        if g == 0:
                            gatesB(3)
                        if pendV[0] is not None:
                            pendV[0]()

                        def vdft(g=g):
                            for blk in range(4):
                                ps2, tp2 = ps_next()
                                P.x("pe", "matmul", reads=[TuT[g], Tdft], writes=[tp2], out=ps2[:, 0:256], lhsT=uT[:, g, blk * 128:(blk + 1) * 128],
                                    rhs=dft_t[:, 0, :], start=True, stop=True)
                                P.x("dve", "tensor_copy", reads=[tp2], writes=[TVst[blk]],
                                    out=Vst[:, blk, g * 256:(g + 1) * 256].rearrange("p (h r c) -> p h r c", h=2, r=2),
                                    in_=ps2[:, 0:256].rearrange("p (r h c) -> p h r c", r=2, h=2))
                        pendV[0] = vdft
                pendV[0]()
                tv0 = tok0 + (0 if own else SEG)
                for blk in range(4):
                    P.dma("pool", V_s[:, tv0 + blk * 128: tv0 + (blk + 1) * 128, :].rearrange("g p x -> p g x"),
                          Vst[:, blk, :].rearrange("p (g x) -> p g x", x=128), reads=[TVst[blk]], writes=[TV_s], sem="Vst")
                cast_some(cast_per_tile)
            for ch in range(16):
                if (not own) and ch < 8:
                    continue
                ri = rr[0] % 2
                rr[0] += 1
                r = raw[ri]
                P.x("dve", "tensor_copy", reads=[Tcarry[ch]], writes=[Traw[ri]], out=r[:, 0:4], in_=carry[:, ch, :])
                P.x("dve", "tensor_copy", reads=[Thalo, Traw[ri]], writes=[Traw[ri]], out=r[:, 4:6], in_=halo_raw[:, ch, lo + 2:lo + 4])
                dst = (q_s if ch < 8 else ks)[ch % 8]
                conv_emit(ri, ch, 2, dst, SEG, Tq_s if ch < 8 else Tks)

        phase1_segment(False)
        phase1_segment(True)
        cast_some(len(cast_rest))
        P.barrier()
        st1.__exit__(None, None, None)

        st2 = contextlib.ExitStack()
        st2.__enter__()
        Cst = sb("Cst", [128, NH, 256], F32, st2)
        nst = sb("nst", [128, NH], F32, st2)
        Cbf = sb("Cbf", [128, NH, 256], BF16, st2)
        nbf = sb("nbf", [128, NH], BF16, st2)
        Coth = sb("Coth", [128, NH, 256], F32, st2)
        noth = sb("noth", [128, NH], F32, st2)
        qsc = [sb("qsc%d" % i, [128, NH, 512], BF16, st2) for i in range(2)]
        ksc = [sb("ksc%d" % i, [128, NH, 512], BF16, st2) for i in range(2)]
        vsc = [sb("vsc%d" % i, [128, 4, W_V], BF16, st2) for i in range(2)]
        Tqsc = [T() for _ in range(2)]
        Tksc = [T() for _ in range(2)]
        Tvsc = [T() for _ in range(2)]
        Pm = sb("Pm", [128, NH, 128], BF16, st2)
        TPm = [T() for _ in range(NH)]
        kt = sb("kt", [128, NH, 128], BF16, st2)
        Tkt = [T() for _ in range(NH)]
        hout = [sb("hout%d" % i, [128, W_V], F32, st2) for i in range(3)]
        Thout = [[T(), T()] for _ in range(3)]
        hfl = [sb("hfl%d" % i, [128, W_V], F32, st2) for i in range(3)]
        Thfl = [T() for _ in range(3)]
        sgl = [sb("sgl%d" % i, [128, W_V], F32, st2) for i in range(3)]
        Tsgl = [T() for _ in range(3)]
        hsq = sb("hsq", [128, 256], F32, st2)
        Thsq = T()
        hab = [sb("hab%d" % i, [128, W_V], BF16, st2) for i in range(2)]
        Thab = [T() for _ in range(2)]
        rr8 = sb("rr8", [128, 8], F32, st2)
        Trr8 = T()
        ss8 = sb("ss8", [128, 8], F32, st2)
        Tss8 = T()
        tmpn = sb("tmpn", [128, 8], F32, st2)
        Ttmpn = T()

        def sweep(kind):
            own = kind != "oth"
            order = list(range(NB)) if kind != "bwd" else list(range(NB - 1, -1, -1))
            ksrc, Tksrc = (k_s, Tk_s) if own else (ko_s, Tko_s)
            vsrc, Tvsrc = (v_s, Tv_s) if own else (vo_s, Tvo_s)
            mask = U_ap if kind == "fwd" else L_ap
            if kind == "oth":
                for h in range(NH):
                    P.x("pool", "memset", writes=[TC[h]], args=(Cst[:, h, :], 0.0,))
                    P.x("pool", "memset", writes=[TCb[h]], args=(Cbf[:, h, :], 0.0,))
                P.x("pool", "memset", writes=[Tn], args=(nst[:], 0.0,))
                P.x("pool", "memset", writes=[Tnb], args=(nbf[:], 0.0,))
            else:
                fcol = flags[:, 0:1] if kind == "fwd" else flags[:, 1:2]
                for h in range(NH):
                    P.x("dve", "tensor_scalar", reads=[TCo, Trep], writes=[TC[h]], out=Cst[:, h, :], in0=Coth[:, h, :], scalar1=fcol, scalar2=None, op0=ALU.mult)
                    P.x("act", "copy", reads=[TC[h]], writes=[TCb[h]], out=Cbf[:, h, :], in_=Cst[:, h, :])
                P.x("dve", "tensor_scalar", reads=[TCo, Trep], writes=[Tn], out=nst[:], in0=noth[:], scalar1=fcol, scalar2=None, op0=ALU.mult)
                P.x("act", "copy", reads=[Tn], writes=[Tnb], out=nbf[:], in_=nst[:])
            if own:
                a_i, ap_i, ei_i, eb_i = (0, 16, 32, 48) if kind == "fwd" else (8, 24, 40, 56)
            else:
                ap_i, eb_i = 0, 8

            def load_sc(sci):
                sc = order[sci * 4] // 4
                b = sci % 2
                t0 = sc * 512
                P.dma("sp", ksc[b][:], ksrc[:, :, t0 + 2:t0 + 514].rearrange("h p t -> p h t"), reads=[Tksrc], writes=[Tksc[b]], sem="ksc%d" % b)
                P.dma("sp", vsc[b][:], vsrc[t0:t0 + 512, :].rearrange("(b p) f -> p b f", p=128), reads=[Tvsrc], writes=[Tvsc[b]], sem="vsc%d" % b)
                if own:
                    P.dma("sp", qsc[b][:], q_s[:, :, t0 + 2:t0 + 514].rearrange("h p t -> p h t"), reads=[Tq_s], writes=[Tqsc[b]], sem="qsc%d" % b)

            def prefetch_epi(ci):
                c = order[ci]
                trow = slice(c * 128, (c + 1) * 128)
                P.dma("sp", hfl[ci % 3][:], hf_s[trow, :], reads=[Thf_s], writes=[Thfl[ci % 3]], sem="hfl%d" % (ci % 3))
                P.dma("sp", sgl[ci % 3][:], sgo_s[trow, :], reads=[Tsgo_s], writes=[Tsgl[ci % 3]], sem="sgl%d" % (ci % 3))

            def epilogue(ci):
                c = order[ci]
                ho = hout[ci % 3]
                Tho = Thout[ci % 3]
                trow = slice(c * 128, (c + 1) * 128)
                if kind == "fwd":
                    P.dma("pool", hf_s[trow, :], ho[:], reads=Tho, writes=[Thf_s], sem="hout%d" % (ci % 3))
                elif kind == "bwd":
                    hf_, Thf_ = hfl[ci % 3], Thfl[ci % 3]
                    sg_, Tsg_ = sgl[ci % 3], Tsgl[ci % 3]
                    P.x("pool", "tensor_tensor", reads=Tho + [Thf_], writes=Tho, out=ho[:], in0=ho[:], in1=hf_[:], op=ALU.add)
                    for h in range(NH):
                        P.x("act", "activation", reads=Tho + [Tss8], writes=[Thsq, Tss8], out=hsq[:], in_=ho[:, h * 256:(h + 1) * 256], func=AF.Square, accum_out=ss8[:, h:h + 1])
                    P.x("act", "activation", reads=[Tss8, Tcb], writes=[Tss8], out=ss8[:], in_=ss8[:], func=AF.Sqrt, scale=1.0 / DV, bias=eps_ap)
                    P.x("dve", "reciprocal", reads=[Tss8], writes=[Tss8], out=ss8[:], in_=ss8[:])
                    hb = hab[ci % 2]
                    Thb = Thab[ci % 2]
                    for h in range(NH):
                        P.x("dve", "scalar_tensor_tensor", reads=Tho + [Tss8, Tsg_], writes=[Thb], out=hb[:, h * 256:(h + 1) * 256], in0=ho[:, h * 256:(h + 1) * 256],
                            scalar=ss8[:, h:h + 1], in1=sg_[:, h * 256:(h + 1) * 256], op0=ALU.mult, op1=ALU.mult)
                    P.dma("pool", ha_s[trow, :], hb[:], reads=[Thb], writes=[Tha_s], sem="hab%d" % (ci % 2))

            nsc_tot = NB // 4
            pending = [None]
            load_sc(0)
            for ci, c in enumerate(order):
                sci = ci // 4
                if ci % 4 == 0 and sci + 1 < nsc_tot:
                    load_sc(sci + 1)
                if kind == "bwd":
                    prefetch_epi(ci)
                b = sci % 2
                cc = c % 4
                tsl = slice(cc * 128, (cc + 1) * 128)
                gsrc, Tg = (gs, Tgs[c]) if own else (gso, Tgso[c])
                ho = hout[ci % 3]
                Tho = Thout[ci % 3]
                psS = [ps_next() for _ in range(2)] if own else None
                if own:
                    for h in range(NH):
                        pS, tS = psS[h // 4]
                        hh = h % 4
                        P.x("pe", "matmul", reads=[Tksc[b], Tqsc[b]], writes=[tS], out=pS[:, hh * 128:(hh + 1) * 128], lhsT=ksc[b][:, h, tsl],
                            rhs=qsc[b][:, h, tsl], start=True, stop=True)
                psK, tK = ps_next()
                psKb = psK[:].bitcast(BF16)
                for h in range(NH):
                    P.x("pe", "transpose", reads=[Tksc[b], Tib], writes=[tK], args=(psKb[:, h * 128:(h + 1) * 128], ksc[b][:, h, tsl], identb[:],))
                if own:
                    for h in range(NH):
                        pS, tS = psS[h // 4]
                        hh = h % 4
                        P.x("dve", "scalar_tensor_tensor", reads=[tS, Tg, Tcst], writes=[TPm[h]], out=Pm[:, h, :], in0=pS[:, hh * 128:(hh + 1) * 128],
                            scalar=gsrc[:, c, a_i + h:a_i + h + 1], in1=mask, op0=ALU.mult, op1=ALU.mult)
                for h in range(NH):
                    P.x("act", "activation", reads=[tK, Tg], writes=[Tkt[h]], out=kt[:, h, :], in_=psKb[:, h * 128:(h + 1) * 128], func=AF.Copy,
                        scale=gsrc[:, c, ap_i + h:ap_i + h + 1])
                if own and ci > 1:
                    epilogue(ci - 2)
                psC = [ps_next() for _ in range(4)]
                psn, tn_ = ps_next()
                for h in range(NH):
                    pC, tC = psC[h // 2]
                    cs = slice((h % 2) * 256, (h % 2) * 256 + 256)
                    P.x("pe", "matmul", reads=[Tkt[h], Tvsc[b]], writes=[tC], out=pC[:, cs], lhsT=kt[:, h, :], rhs=vsc[b][:, cc, h * 256:(h + 1) * 256],
                        start=True, stop=True)
                    P.x("pe", "matmul", reads=[Tkt[h], Tib], writes=[tn_], out=psn[:, 8 + h:9 + h], lhsT=kt[:, h, :], rhs=onesb[:, 0:1], start=True, stop=True)
                for h in range(NH):
                    pC, tC = psC[h // 2]
                    cs = slice((h % 2) * 256, (h % 2) * 256 + 256)
                    P.x("dve", "scalar_tensor_tensor", reads=[TC[h], Tg, tC], writes=[TC[h]], out=Cst[:, h, :], in0=Cst[:, h, :],
                        scalar=gsrc[:, c, eb_i + h:eb_i + h + 1], in1=pC[:, cs], op0=ALU.mult, op1=ALU.add)
                P.x("dve", "tensor_tensor", reads=[Tn, Tg], writes=[Ttmpn], out=tmpn[:], in0=nst[:], in1=gsrc[:, c, eb_i:eb_i + 8], op=ALU.mult)
                P.x("dve", "tensor_tensor", reads=[Ttmpn, tn_, Tn], writes=[Tn], out=nst[:], in0=tmpn[:], in1=psn[:, 8:16], op=ALU.add)
                if own:
                    psR = [ps_next() for _ in range(4)]
                    psr, tr_ = ps_next()
                    for h in range(NH):
                        pR, tR = psR[h // 2]
                        cs = slice((h % 2) * 256, (h % 2) * 256 + 256)
                        P.x("pe", "matmul", reads=[TPm[h], Tvsc[b]], writes=[tR], out=pR[:, cs], lhsT=Pm[:, h, :], rhs=vsc[b][:, cc, h * 256:(h + 1) * 256],
                            start=True, stop=False)
                        P.x("pe", "matmul", reads=[Tqsc[b], TCb[h]], writes=[tR], out=pR[:, cs], lhsT=qsc[b][:, h, tsl], rhs=Cbf[:, h, :], start=False, stop=True)
                        P.x("pe", "matmul", reads=[TPm[h], Tib], writes=[tr_], out=psr[:, h:h + 1], lhsT=Pm[:, h, :], rhs=onesb[:, 0:1], start=True, stop=False)
                        P.x("pe", "matmul", reads=[Tqsc[b], Tnb], writes=[tr_], out=psr[:, h:h + 1], lhsT=qsc[b][:, h, tsl], rhs=nbf[:, h:h + 1], start=False, stop=True)
                if own:
                    cast_eng = (["pool", "dve", "pool", "dve", "pool", "dve", "pool", "pool"] if kind == "fwd"
                                else ["pool", "dve", "act", "dve", "pool", "dve", "act", "pool"])
                    for h in range(NH):
                        if cast_eng[h] == "act":
                            P.x("act", "copy", reads=[TC[h]], writes=[TCb[h]], out=Cbf[:, h, :], in_=Cst[:, h, :])
                        else:
                            P.x(cast_eng[h], "tensor_copy", reads=[TC[h]], writes=[TCb[h]], out=Cbf[:, h, :], in_=Cst[:, h, :])
                    P.x("dve", "tensor_copy", reads=[Tn, Tnb], writes=[Tnb], out=nbf[:], in_=nst[:])
                def r_evac(c=c, ho=ho, Tho=Tho, gsrc=gsrc, Tg=Tg, psR=(psR if own else None), psr=(psr if own else None), tr_=(tr_ if own else None)):
                    if own:
                        P.x("act", "activation", reads=[tr_], writes=[Trr8], out=rr8[:], in_=psr[:, 0:8], func=AF.Abs)
                        P.x("dve", "tensor_tensor", reads=[Trr8, Tg], writes=[Trr8], out=rr8[:], in0=rr8[:], in1=gsrc[:, c, ei_i:ei_i + 8], op=ALU.max)
                        P.x("dve", "reciprocal", reads=[Trr8], writes=[Trr8], out=rr8[:], in_=rr8[:])
                        for h in range(NH):
                            pR, tR = psR[h // 2]
                            cs = slice((h % 2) * 256, (h % 2) * 256 + 256)
                            if h < 4 or kind == "fwd":
                                P.x("act", "activation", reads=[tR, Trr8], writes=[Tho[0]], out=ho[:, h * 256:(h + 1) * 256], in_=pR[:, cs], func=AF.Copy, scale=rr8[:, h:h + 1])
                            else:
                                P.x("dve", "tensor_scalar", reads=[tR, Trr8], writes=[Tho[1]], out=ho[:, h * 256:(h + 1) * 256], in0=pR[:, cs], scalar1=rr8[:, h:h + 1],
                                    scalar2=None, op0=ALU.mult)
                r_evac()
            if own:
                epilogue(NB - 2)
                epilogue(NB - 1)
            if kind == "oth":
                for h in range(NH):
                    P.x("dve", "tensor_copy", reads=[TC[h], TCo], writes=[TCo], out=Coth[:, h, :], in_=Cst[:, h, :])
                P.x("dve", "tensor_copy", reads=[Tn, TCo], writes=[TCo], out=noth[:], in_=nst[:])

        ps_mod[0] = 8
        ps_rr[0] = 0
        sweep("oth")
        sweep("fwd")
        P.barrier()
        sweep("bwd")
        P.barrier()
        st2.__exit__(None, None, None)
        st12.__exit__(None, None, None)

        stf = contextlib.ExitStack()
        stf.__enter__()
        KB = 2 * NB
        Vt = [sb("Vt%d" % i, [KB, 128, 128], BF16, stf) for i in range(2)]
        TVt = [T() for _ in range(2)]
        At = sb("At", [128, 128, 2, 64], BF16, stf)
        TAt = T()
        Gt = sb("Gt", [128, 128, 2, NB], BF16, stf)
        TGt = T()
        F1 = sb("F1", [128, 3, 256], BF16, stf)
        TF1 = T()
        Yst = [sb("Yst%d" % i, [NB, 128, 64], BF16, stf) for i in range(2)]
        TYst = [T() for _ in range(2)]
        P.dma("sp", Gt[:], gtw_s.rearrange("p (s r j) -> p s r j", s=128, r=2), reads=[Tgtw_s], writes=[TGt], sem="f0")
        P.dma("sp", F1[:], dft_s.rearrange("p (a b) -> p a b", a=3), reads=[Tdft_s], writes=[TF1], sem="f1")
        Yv = Y_s.rearrange("(j s) c -> j s c", s=128)
        for gh in range(16):
            b = gh % 2
            P.dma("sp", Vt[b][:], V_s[gh].rearrange("(b a) x -> b a x", a=128), reads=[TV_s], writes=[TVt[b]], sem="Vt%d" % b)
            for cp in range(0, 64, 2):
                ps, tp = ps_next()
                for i in range(2):
                    c_ = cp + i
                    P.x("pe", "matmul", reads=[TVt[b], TF1], writes=[tp], out=ps[:, i * 256:(i + 1) * 256], lhsT=Vt[b][:, :, c_], rhs=F1[0:KB, 1, :],
                                                                  start=True, stop=False)
                    P.x("pe", "matmul", reads=[TVt[b], TF1], writes=[tp], out=ps[:, i * 256:(i + 1) * 256], lhsT=Vt[b][:, :, 64 + c_], rhs=F1[0:KB, 2, :],
                                                                  start=False, stop=True)
                eng = "act" if (cp // 2) % 2 == 0 else "dve"
                if eng == "act":
                    P.x("act", "copy", reads=[tp], writes=[TAt], out=At[:, :, :, cp:cp + 2].rearrange("p s r c -> p c r s"),
                                                             in_=ps[:].rearrange("p (c r s) -> p c r s", c=2, r=2))
                else:
                    P.x("dve", "tensor_copy", reads=[tp], writes=[TAt], out=At[:, :, :, cp:cp + 2].rearrange("p s r c -> p c r s"),
                                                                    in_=ps[:].rearrange("p (c r s) -> p c r s", c=2, r=2))
            yb = Yst[gh % 2]
            Tyb = TYst[gh % 2]
            for s0 in range(0, 128, 8):
                ps, tp = ps_next()
                for i in range(8):
                    s = s0 + i
                    P.x("pe", "matmul", reads=[TGt, TAt], writes=[tp], out=ps[0:NB, i * 64:(i + 1) * 64], lhsT=Gt[:, s, 0, :], rhs=At[:, s, 0, :],
                                                                start=True, stop=False)
                    P.x("pe", "matmul", reads=[TGt, TAt], writes=[tp], out=ps[0:NB, i * 64:(i + 1) * 64], lhsT=Gt[:, s, 1, :], rhs=At[:, s, 1, :],
                                                                start=False, stop=True)
                if (s0 // 8) % 2 == 0:
                    P.x("act", "copy", reads=[tp], writes=[Tyb], out=yb[:, s0:s0 + 8, :], in_=ps[0:NB, :].rearrange("p (s c) -> p s c", s=8))
                else:
                    P.x("dve", "tensor_copy", reads=[tp], writes=[Tyb], out=yb[:, s0:s0 + 8, :], in_=ps[0:NB, :].rearrange("p (s c) -> p s c", s=8))
            P.dma("pool", Yv[:, :, gh * 64:(gh + 1) * 64], yb[:], reads=[Tyb], writes=[TY_s], sem="Yst%d" % (gh % 2))
        P.barrier()
        stf.__exit__(None, None, None)

        ps_mod[0] = 7
        ps_rr[0] = 0
        st3 = contextlib.ExitStack()
        st3.__enter__()
        x_tm = [sb("x3_tm%d" % i, [128, D], F32, st3) for i in range(1)]
        Tx_tm = [T() for _ in range(1)]
        xT = sb("x3T", [128, 16, 512], F32, st3)
        TxT = [T() for _ in range(16)]
        hT = sb("h3T", [128, 16, 512], BF16, st3)
        ThT = [T() for _ in range(16)]
        mixT = sb("mixT", [128, 16, 512], F32, st3)
        TmixT = [T() for _ in range(16)]
        FF = sb("FF", [128, 44, 512], BF16, st3)
        TFF = [T() for _ in range(44)]
        sq = [sb("sq3%d" % i, [128, 512], BF16, st3) for i in range(2)]
        Tsq = [T() for _ in range(2)]
        rstd = sb("rstd3", [128, 512], F32, st3)
        Trstd = T()
        gA = sb("gA", [128, 4, 512], F32, st3)
        gB = sb("gB", [128, 4, 512], F32, st3)
        TgA = [T() for _ in range(4)]
        TgB = [T() for _ in range(4)]
        ws3 = WStream(2, st3, "b")
        order3 = []
        for jj in range(4):
            order3 += ["in%d" % (14 + jj), "in%d" % (18 + jj), "ao%d" % jj, "bo%d" % jj]
        order3 += ["wo%d" % i for i in range(4)]
        for jj in range(11):
            order3 += ["fi%d" % jj, "fi%d" % (11 + jj)]
        order3 += ["fo%d" % i for i in range(16)]
        ws3.set_order(order3 * NT)
        haT = FF[:, 0:16, :]
        ThaT = TFF[0:16]
        fbT = FF[:, 16:24, :]
        TfbT = TFF[16:24]
        mixinT = FF[:, 24:40, :]
        TmixinT = TFF[24:40]

        stats3 = Stats(sq, Tsq)

        def post_norm(src, Tsrc, gcol, next_stats):
            stats3.finish(rstd, Trstd)
            for c in range(16):
                P.x("dve", "scalar_tensor_tensor", reads=[Tsrc[c], Trstd, Tvec], writes=[Tsrc[c]], out=src[:, c, :], in0=src[:, c, :],
                    scalar=gcol[:, c:c + 1], in1=rstd[:], op0=ALU.mult, op1=ALU.mult)
                P.x("pool" if c % 2 == 0 else "dve", "tensor_tensor", reads=[TxT[c], Tsrc[c]], writes=[TxT[c]], out=xT[:, c, :], in0=xT[:, c, :],
                    in1=src[:, c, :], op=ALU.add)
                if next_stats:
                    stats3.add(xT[:, c:c + 1, :], [TxT[c]], 1, 0, 512, c == 0, c == 15)

        for it in range(NT):
            tok0 = it * 512
            ws3.prefetch()
            load_xT(x_own, tok0, x_tm, Tx_tm, xT, TxT, stats=stats3)
            for k in range(16):
                P.dma("sp", haT[:, k, :], ha_s[tok0: tok0 + 512, k * 128:(k + 1) * 128],
                      reads=[Tha_s], writes=[ThaT[k]], sem="haT%d" % (k % 4), transpose=True)
            for k in range(8):
                P.dma("sp", fbT[:, k, :], Y_s[tok0: tok0 + 512, k * 128:(k + 1) * 128],
                      reads=[TY_s], writes=[TfbT[k]], sem="fbT%d" % (k % 2), transpose=True)
            stats3.finish(rstd, Trstd)
            fm_apply(xT, TxT, g_pre_mix, hT, ThT, rstd, Trstd)
            for jj in range(4):
                wA, TwA = ws3.get("in%d" % (14 + jj))
                for j4 in range(4):
                    j = jj * 4 + j4
                    ps, tp = ps_next()
                    mm_ii(ps, tp, wA, TwA, j4 * 128, hT, ThT, 16)
                    P.x("act", "activation", reads=[tp, Tvec], writes=[TgA[j4]], out=gA[:, j4, :], in_=ps[:], func=AF.Sigmoid, bias=b_merge[:, j:j + 1])
                wB, TwB = ws3.get("in%d" % (18 + jj))
                for j4 in range(4):
                    j = jj * 4 + j4
                    ps, tp = ps_next()
                    mm_ii(ps, tp, wB, TwB, j4 * 128, hT, ThT, 16)
                    P.x("act", "activation", reads=[tp, Tvec], writes=[TgB[j4]], out=gB[:, j4, :], in_=ps[:], func=AF.Sigmoid, bias=b_merge[:, 16 + j:17 + j])
                wao, Twao = ws3.get("ao%d" % jj)
                for j4 in range(4):
                    ps, tp = ps_next()
                    mm_ii(ps, tp, wao, Twao, j4 * 128, haT, ThaT, 16)
                    P.x("dve", "tensor_tensor", reads=[tp, TgA[j4]], writes=[TgA[j4]], out=gA[:, j4, :], in0=gA[:, j4, :], in1=ps[:], op=ALU.mult)
                wbo, Twbo = ws3.get("bo%d" % jj)
                for j4 in range(4):
                    j = jj * 4 + j4
                    ps, tp = ps_next()
                    mm_ii(ps, tp, wbo, Twbo, j4 * 128, fbT, TfbT, 8)
                    P.x("dve", "tensor_tensor", reads=[tp, TgB[j4]], writes=[TgB[j4]], out=gB[:, j4, :], in0=gB[:, j4, :], in1=ps[:], op=ALU.mult)
                    P.x("pool", "tensor_tensor", reads=[TgA[j4], TgB[j4]], writes=[TmixinT[j]], out=mixinT[:, j, :], in0=gA[:, j4, :], in1=gB[:, j4, :], op=ALU.add)
            if it == 0 and "dbg_ff" in DEBUG_OUT:
                dbg_h = dscr("dbg_h", [128, 16 * 512], BF16)
                P.dma("pool", dbg_h, hT[:].rearrange("p a b -> p (a b)"), reads=ThT, sem="dbg")
                dbg_g = dscr("dbg_g", [128, 8 * 512], F32)
                P.dma("pool", dbg_g[:, 0:2048], gA[:].rearrange("p a b -> p (a b)"), reads=TgA, sem="dbg")
                P.dma("pool", dbg_g[:, 2048:4096], gB[:].rearrange("p a b -> p (a b)"), reads=TgB, sem="dbg")
                dbg_ff = dscr("dbg_ff", [128, 40 * 512], BF16)
                P.dma("pool", dbg_ff, FF[:, 0:40, :].rearrange("p a b -> p (a b)"), reads=TFF[0:40], sem="dbg")
            for jj in range(4):
                wo, Two = ws3.get("wo%d" % jj)
                for j4 in range(4):
                    j = jj * 4 + j4
                    ps, tp = ps_next()
                    mm_ii(ps, tp, wo, Two, j4 * 128, mixinT, TmixinT, 16)
                    P.x("act", "copy", reads=[tp], writes=[TmixT[j]], out=mixT[:, j, :], in_=ps[:])
                    stats3.add(mixT[:, j:j + 1, :], [TmixT[j]], 1, 0, 512, j == 0, j == 15)
            post_norm(mixT, TmixT, g_post_mix, True)
            stats3.finish(rstd, Trstd)
            fm_apply(xT, TxT, g_pre_ffn, hT, ThT, rstd, Trstd)
            for jj in range(11):
                wg_, Twg_ = ws3.get("fi%d" % jj)
                for j4 in range(4):
                    ps, tp = ps_next()
                    mm_ii(ps, tp, wg_, Twg_, j4 * 128, hT, ThT, 16)
                    P.x("act", "activation", reads=[tp], writes=[TgA[j4]], out=gA[:, j4, :], in_=ps[:], func=AF.Silu)
                wu_, Twu_ = ws3.get("fi%d" % (11 + jj))
                for j4 in range(4):
                    j = jj * 4 + j4
                    ps, tp = ps_next()
                    mm_ii(ps, tp, wu_, Twu_, j4 * 128, hT, ThT, 16)
                    P.x("dve", "tensor_tensor", reads=[tp, TgA[j4]], writes=[TFF[j]], out=FF[:, j, :], in0=gA[:, j4, :], in1=ps[:], op=ALU.mult)
            for j in range(16):
                wf, Twf = ws3.get("fo%d" % j)
                ps, tp = ps_next()
                mm_ii(ps, tp, wf, Twf, 0, FF, TFF, 44)
                P.x("act", "copy", reads=[tp], writes=[TmixT[j]], out=mixT[:, j, :], in_=ps[:])
                stats3.add(mixT[:, j:j + 1, :], [TmixT[j]], 1, 0, 512, j == 0, j == 15)
            post_norm(mixT, TmixT, g_post_ffn, False)
            for blk in range(4):
                b2 = 0
                for cg in range(4):
                    ps, tp = ps_next()
                    for i in range(4):
                        c = cg * 4 + i
                        P.x("pe", "transpose", reads=[TxT[c], Tcst], writes=[tp], args=(ps[:, i * 128:(i + 1) * 128], xT[:, c, blk * 128:(blk + 1) * 128], ident_ap,))
                    P.x("act", "copy", reads=[tp], writes=[Tx_tm[b2]], out=x_tm[b2][:, cg * 512:(cg + 1) * 512], in_=ps[:])
                P.dma("pool", y_out[tok0 + blk * 128: tok0 + (blk + 1) * 128, :], x_tm[b2][:], reads=[Tx_tm[b2]], sem="yo%d" % b2)
        P.finish()
        st3.__exit__(None, None, None)
        P.emit()
    return nc


def _fm(v):
    return np.ascontiguousarray(np.asarray(v, np.float32).reshape(-1, 128).T)


def core_inputs(NB, role, x_own, x_oth, halo8, W, seq_T):
    SEG = NB * 128
    hf = role["hf"]
    prompt = role["prompt"]
    x_halo = np.zeros((128, D), np.float32)
    x_halo[0:8] = halo8
    w_in = W["w_in"]
    bg = W["b_gates"]
    if hf == 1:
        w_go = np.concatenate([w_in[:, OFF_G + 0:OFF_G + 8], w_in[:, OFF_G + 8:OFF_G + 16]], axis=1)
        b_go = np.concatenate([bg[0:8], bg[8:16]])
    else:
        w_go = np.concatenate([w_in[:, OFF_G + 16:OFF_G + 24], w_in[:, OFF_G + 24:OFF_G + 32]], axis=1)
        b_go = np.concatenate([bg[16:24], bg[24:32]])
    vecs = np.concatenate([_fm(W["g_pre_mix"]), _fm(W["g_post_mix"]), _fm(W["g_pre_ffn"]), _fm(W["g_post_ffn"]),
                           _fm(W["b_merge"])], axis=1)
    convw = np.ascontiguousarray(W["conv_w"].reshape(5, 16, 128).transpose(2, 1, 0))
    flagP = 1.0 if (prompt and hf == 1) else 0.0
    flagS = 1.0 if (prompt and hf == 0) else 0.0
    rep = np.concatenate([np.tile(W["g_head"][None, :], (128, 1)), np.tile(bg[None, :], (128, 1)),
                          np.tile(b_go[None, :], (128, 1)),
                          np.tile(np.array([[flagP, flagS, -flagP, -flagS]], np.float32), (128, 1))], axis=1).astype(np.float32)
    idx = np.arange(128)
    U = (idx[:, None] <= idx[None, :]).astype(np.float32)
    L = (idx[:, None] >= idx[None, :]).astype(np.float32)
    Mo = (idx[:, None] > idx[None, :]).astype(np.float32) if hf == 1 else (idx[:, None] < idx[None, :]).astype(np.float32)
    cst = np.stack([U, L, Mo, np.ones((128, 128), np.float32), np.eye(128, dtype=np.float32)], axis=1)
    c = np.arange(128, dtype=np.float64)
    ang = 2 * np.pi * np.outer(c, c) / 128.0
    chd = np.concatenate([np.cos(ang), -np.sin(ang)], axis=1)
    N = 2 * SEG
    s = np.arange(128)
    j = np.arange(NB)
    tok = s[:, None] + 128 * j[None, :]
    kp = (tok + SEG * hf) if prompt else 2 * tok
    KB = 2 * NB
    kap = kp[:, 0] % KB
    bpart = np.arange(KB)
    b_true = np.where(bpart < NB, bpart + NB * hf, (bpart - NB) + NB * (1 - hf)).astype(np.float64)
    phi = 2 * np.pi * np.outer(b_true, kap) / float(KB)
    Fr, Fi = np.cos(phi), -np.sin(phi)
    F1a = np.zeros((128, 256))
    F1b = np.zeros((128, 256))
    F1a[:KB] = np.concatenate([Fr, Fi], axis=1)
    F1b[:KB] = np.concatenate([-Fi, Fr], axis=1)
    dftc = np.stack([chd, F1a, F1b], axis=1).astype(np.float32)
    scale = 1.0 / np.sqrt(float(seq_T) * 128.0)
    a = np.arange(128, dtype=np.float64)
    th = 2 * np.pi * (a[:, None, None] * (kp[None, :, :] % N)) / N
    Gr = scale * np.cos(th)
    nGi = scale * np.sin(th)
    gtw = np.stack([Gr, nGi], axis=2).reshape(128, -1).astype(np.float32)
    return {
        "x_own": np.ascontiguousarray(x_own, np.float32), "x_oth": np.ascontiguousarray(x_oth, np.float32), "x_halo": x_halo,
        "w_in": W["w_in"], "w_a_out": W["w_a_out"], "w_b_out": W["w_b_out"], "w_out": W["w_out"],
        "w_ffn_in": W["w_ffn_in"], "w_ffn_out": W["w_ffn_out"], "w_go": np.ascontiguousarray(w_go, np.float32),
        "vecs": np.ascontiguousarray(vecs, np.float32), "convw": convw.astype(np.float32), "rep": rep, "cst": cst,
        "dftc": dftc, "gtw": gtw,
    }


def make_in_maps(NB, seqs, W):
    SEG = NB * 128
    maps, plan = [], []
    z2 = np.zeros((2, D), np.float32)
    for si, xs in enumerate(seqs):
        Tn = xs.shape[0]
        if Tn == SEG:
            halo8 = np.zeros((8, D), np.float32)
            maps.append(core_inputs(NB, dict(hf=0, prompt=False), xs, np.zeros((SEG, D), np.float32), halo8, W, Tn))
            plan.append((si, 0))
        else:
            assert Tn == 2 * SEG
            a, b = xs[:SEG], xs[SEG:]
            h0 = np.concatenate([z2, b[0:2], a[-2:], z2], axis=0)
            h1 = np.concatenate([a[-2:], z2, z2, b[0:2]], axis=0)
            maps.append(core_inputs(NB, dict(hf=0, prompt=True), a, b, h0, W, Tn))
            plan.append((si, 0))
            maps.append(core_inputs(NB, dict(hf=1, prompt=True), b, a, h1, W, Tn))
            plan.append((si, 1))
    return maps, plan


_W_KEYS = ["g_pre_mix", "w_in", "conv_w", "b_gates", "g_head", "w_a_out", "w_b_out", "b_merge", "w_out", "g_post_mix",
           "g_pre_ffn", "w_ffn_in", "w_ffn_out", "g_post_ffn"]


def run_seqs(NB, seqs, weights):
    W = {k: np.ascontiguousarray(np.asarray(weights[k], np.float32)[0]) for k in _W_KEYS}
    maps, plan = make_in_maps(NB, seqs, W)
    nc = build(NB)
    res = run_bass_kernel_spmd(nc, maps, core_ids=list(range(len(maps))))
    SEG = NB * 128
    outs = [np.zeros_like(np.asarray(s, np.float32)) for s in seqs]
    for (si, hf), r in zip(plan, res.results):
        outs[si][hf * SEG:(hf + 1) * SEG] = r["y_out"]
    if DEBUG_OUT:
        return outs, res.results
    return outs


def kernel(**inputs):
    xp = np.asarray(inputs["x_prompt"], np.float32)
    xs = np.asarray(inputs["x_sample"], np.float32)
    seqs = [xp[0], xp[1], xs[0], xs[1], xs[2], xs[3]]
    outs = run_seqs(64, seqs, inputs)
    y_prompt = np.stack(outs[0:2], axis=0)
    y_sample = np.stack(outs[2:6], axis=0)
    return (y_prompt, y_sample)
```

```python
import contextlib
import numpy as np
import concourse.bass as bass
import concourse.mybir as mybir
from concourse.bass_utils import run_bass_kernel_spmd

F32 = mybir.dt.float32
BF16 = mybir.dt.bfloat16
AF = mybir.ActivationFunctionType
ALU = mybir.AluOpType

D = 2048
NH = 8
DK = 128
DV = 256
W_QK = 1024
W_V = 2048
OFF_Q = 0
OFF_K = 1024
OFF_V = 2048
OFF_O = 4096
OFF_G = 6144
OFF_B = 6176
OFF_M = 7200
W_IN = 11296
D_FF = 5632
EPS = 1e-6
LNC = float(np.log(128.0 ** -0.5))

ENGS = ("pe", "act", "dve", "pool", "sp")


class T:
    __slots__ = ("name", "w", "r", "untracked")

    def __init__(self, name="", untracked=False):
        self.name = name
        self.w = None
        self.r = []
        self.untracked = untracked


class Op:
    __slots__ = ("eng", "emit", "waits", "signal", "sidx", "dma")

    def __init__(self, eng, emit):
        self.eng = eng
        self.emit = emit
        self.waits = []
        self.signal = False
        self.sidx = 0
        self.dma = None


class Prog:
    def __init__(self, nc):
        self.nc = nc
        self.ops = {e: [] for e in ENGS}
        self.dma_cnt = {}
        self.waited = {e: {} for e in ENGS}

    def _need(self, op, ev):
        if ev is None:
            return
        e = op.eng
        if ev[0] == "e":
            _, eng2, idx = ev
            if eng2 == "pe" and e == "pe":
                return
            key = ("e", eng2)
            if self.waited[e].get(key, -1) >= idx:
                return
            self.waited[e][key] = idx
            self.ops[eng2][idx].signal = True
            op.waits.append(ev)
        else:
            _, sk, val = ev
            key = ("d", sk)
            if self.waited[e].get(key, -1) >= val:
                return
            self.waited[e][key] = val
            op.waits.append(ev)

    def _track(self, op, ev, reads, writes):
        reads = [t for t in reads if not t.untracked]
        writes = [t for t in writes if not t.untracked]
        for t in reads:
            self._need(op, t.w)
        for t in writes:
            self._need(op, t.w)
            for r in t.r:
                self._need(op, r)
        for t in reads:
            t.r.append(ev)
            if len(t.r) > 64:
                t.r = t.r[-48:]
        for t in writes:
            t.w = ev
            t.r = []

    def op(self, eng, emit, reads=(), writes=()):
        o = Op(eng, emit)
        ev = ("e", eng, len(self.ops[eng]))
        self._track(o, ev, reads, writes)
        self.ops[eng].append(o)
        return ev

    def x(self, eng, name, reads=(), writes=(), args=(), **kw):
        return self.op(eng, lambda e, name=name, args=args, kw=kw: getattr(e, name)(*args, **kw), reads, writes)

    def dma(self, q, out, in_, reads=(), writes=(), sem="dma", **kw):
        o = Op(q, lambda e: e.dma_start(out=out, in_=in_, **kw))
        n = self.dma_cnt.get(sem, 0) + 1
        self.dma_cnt[sem] = n
        ev = ("d", sem, 16 * n)
        o.dma = (sem, 16 * n)
        if n > 1:
            self._need(o, ("d", sem, 16 * (n - 1)))
        self._track(o, ev, reads, writes)
        self.ops[q].append(o)
        return ev

    def barrier(self):
        evs = []
        for e in ENGS:
            for i in range(len(self.ops[e]) - 1, -1, -1):
                o = self.ops[e][i]
                if o.dma is None and o.emit is not None:
                    evs.append(("e", e, i))
                    break
        for sk, n in self.dma_cnt.items():
            evs.append(("d", sk, 16 * n))
        for e in ENGS:
            o = Op(e, None)
            for ev in evs:
                if ev[0] == "e" and ev[1] == e:
                    continue
                if ev[0] == "e" and ev[1] == "pe" and e == "pe":
                    continue
                self._need(o, ev)
            if o.waits:
                self.ops[e].append(o)

    def finish(self):
        o = Op("sp", None)
        for sk, n in self.dma_cnt.items():
            self._need(o, ("d", sk, 16 * n))
        self.ops["sp"].append(o)

    def emit(self):
        nc = self.nc
        for e in ENGS:
            c = 0
            for o in self.ops[e]:
                if o.signal:
                    c += 1
                o.sidx = c
        with contextlib.ExitStack() as st:
            esem = {e: st.enter_context(nc.semaphore("S_" + e)) for e in ENGS}
            dsem = {sk: st.enter_context(nc.semaphore("D_%s" % (sk,))) for sk in self.dma_cnt}
            block = st.enter_context(nc.Block())

            def run(ename):
                def body(eng):
                    for o in self.ops[ename]:
                        for ev in o.waits:
                            if ev[0] == "e":
                                eng.wait_ge(esem[ev[1]], self.ops[ev[1]][ev[2]].sidx)
                            else:
                                eng.wait_ge(dsem[ev[1]], ev[2])
                        if o.emit is None:
                            continue
                        ins = o.emit(eng)
                        if o.dma is not None:
                            ins.then_inc(dsem[o.dma[0]], 16)
                        elif o.signal:
                            ins.then_inc(esem[ename], 1)
                return body

            block.tensor(run("pe"))
            block.scalar(run("act"))
            block.vector(run("dve"))
            block.gpsimd(run("pool"))
            block.sync(run("sp"))


DEBUG_OUT = ()


def build(NB):
    SEG = NB * 128
    NT = NB // 4
    nc = bass.Bass("TRN2", target_bir_lowering=False)
    P = Prog(nc)

    def din(name, shape, dt=F32):
        return nc.dram_tensor(name, list(shape), dt, kind="ExternalInput").ap()

    def dscr(name, shape, dt):
        return nc.dram_tensor(name, list(shape), dt, kind=("ExternalOutput" if name in DEBUG_OUT else "Internal")).ap()

    x_own = din("x_own", [SEG, D])
    x_oth = din("x_oth", [SEG, D])
    x_halo = din("x_halo", [128, D])
    w_in = din("w_in", [D, W_IN])
    w_a_out = din("w_a_out", [W_V, D])
    w_b_out = din("w_b_out", [1024, D])
    w_out = din("w_out", [D, D])
    w_ffn_in = din("w_ffn_in", [D, 2 * D_FF])
    w_ffn_out = din("w_ffn_out", [D_FF, D])
    w_go = din("w_go", [D, 16])
    vecs = din("vecs", [128, 16 * 4 + 32])
    convw = din("convw", [128, 16, 5])
    rep = din("rep", [128, 2048 + 32 + 16 + 4])
    cst = din("cst", [128, 5, 128])
    dftc = din("dftc", [128, 3, 256])
    gtw = din("gtw", [128, 128 * 2 * NB])
    y_out = nc.dram_tensor("y_out", [SEG, D], F32, kind="ExternalOutput").ap()

    WB = {}

    def add_wb(key, src, c0, ncols):
        K = src.shape[0]
        KC = K // 128
        scr = dscr("wb_%s" % key, [128, KC * ncols], BF16)
        WB[key] = dict(src=src, c0=c0, n=ncols, KC=KC, scr=scr, t=T("wb_" + key))

    for i in range(22):
        add_wb("in%d" % i, w_in, 512 * i if i < 12 else 0, 512)
    for i in range(2):
        WB["in%d" % (12 + i)]["c0"] = OFF_B + 512 * i
    for i in range(8):
        WB["in%d" % (14 + i)]["c0"] = OFF_M + 512 * i
    add_wb("gate", w_in, OFF_G, 32)
    add_wb("go", w_go, 0, 16)
    for i in range(4):
        add_wb("ao%d" % i, w_a_out, 512 * i, 512)
        add_wb("bo%d" % i, w_b_out, 512 * i, 512)
        add_wb("wo%d" % i, w_out, 512 * i, 512)
    for i in range(22):
        add_wb("fi%d" % i, w_ffn_in, 512 * i, 512)
    for i in range(16):
        add_wb("fo%d" % i, w_ffn_out, 128 * i, 128)

    q_s = dscr("q_s", [NH, 128, SEG + 512], BF16)
    k_s = dscr("k_s", [NH, 128, SEG + 512], BF16)
    ko_s = dscr("ko_s", [NH, 128, SEG + 512], BF16)
    v_s = dscr("v_s", [SEG, W_V], BF16)
    vo_s = dscr("vo_s", [SEG, W_V], BF16)
    sgo_s = dscr("sgo_s", [SEG, W_V], F32)
    V_s = dscr("V_s", [16, 2 * SEG, 128], BF16)
    hf_s = dscr("hf_s", [SEG, W_V], F32)
    ha_s = dscr("ha_s", [SEG, W_V], BF16)
    Y_s = dscr("Y_s", [SEG, 1024], BF16)
    gtw_s = dscr("gtw_s", [128, 128 * 2 * NB], BF16)
    dft_s = dscr("dft_s", [128, 3 * 256], BF16)
    Tq_s, Tk_s, Tko_s, Tv_s, Tvo_s, Tsgo_s, TV_s, Thf_s, Tha_s, TY_s = [T(untracked=True) for _ in range(10)]
    Tgtw_s, Tdft_s = T(), T()

    st = contextlib.ExitStack()
    with st:
        def sb(name, shape, dt, stack=st):
            return stack.enter_context(nc.sbuf_tensor(name, list(shape), dt))

        psb = [st.enter_context(nc.psum_tensor("ps%d" % i, [128, 512], F32)) for i in range(8)]
        Tps = [T("ps%d" % i) for i in range(8)]
        ps_rr = [0]

        ps_live = [False] * 8
        ps_mod = [7]

        def ps_next():
            i = ps_rr[0] % ps_mod[0]
            ps_rr[0] = (i + 1) % ps_mod[0]
            assert (not ps_live[i]) or len(Tps[i].r) > 0, "PSUM bank %d re-allocated before its consumer was recorded" % i
            ps_live[i] = True
            return psb[i], Tps[i]

        cst_t = sb("cst_t", [128, 5, 128], F32)
        Tcst = T()
        vec_t = sb("vec_t", [128, 96], F32)
        Tvec = T()
        cw_t = sb("cw_t", [128, 16, 5], F32)
        Tcw = T()
        rep_t = sb("rep_t", [128, 2100], F32)
        Trep = T()
        cb_t = sb("cb_t", [128, 4], F32)
        Tcb = T()
        identb = sb("identb", [128, 128], BF16)
        onesb = sb("onesb", [128, 128], BF16)
        Tib = T()
        P.dma("sp", cst_t[:], cst, writes=[Tcst], sem="c0")
        P.dma("sp", vec_t[:], vecs, writes=[Tvec], sem="c0")
        P.dma("sp", cw_t[:], convw, writes=[Tcw], sem="c0")
        P.dma("sp", rep_t[:], rep, writes=[Trep], sem="c0")
        P.x("pool", "memset", writes=[Tcb], args=(cb_t[:, 0:1], EPS,))
        P.x("pool", "memset", reads=[Tcb], writes=[Tcb], args=(cb_t[:, 1:2], LNC,))
        P.x("dve", "tensor_copy", reads=[Tcst], writes=[Tib], out=identb[:], in_=cst_t[:, 4, :])
        P.x("dve", "tensor_copy", reads=[Tcst, Tib], writes=[Tib], out=onesb[:], in_=cst_t[:, 3, :])
        U_ap, L_ap, Mo_ap, ones_ap, ident_ap = [cst_t[:, i, :] for i in range(5)]
        g_pre_mix = vec_t[:, 0:16]
        g_post_mix = vec_t[:, 16:32]
        g_pre_ffn = vec_t[:, 32:48]
        g_post_ffn = vec_t[:, 48:64]
        b_merge = vec_t[:, 64:96]
        ghead_rep = rep_t[:, 0:2048]
        bgate_rep = rep_t[:, 2048:2080]
        bgo_rep = rep_t[:, 2080:2096]
        flags = rep_t[:, 2096:2100]
        eps_ap = cb_t[:, 0:1]
        lnc_ap = cb_t[:, 1:2]

        def cast_wb(key):
            w = WB[key]
            src = w["src"][:, w["c0"]:w["c0"] + w["n"]].rearrange("(kc p) n -> p kc n", p=128)
            dst = w["scr"].rearrange("p (kc n) -> p kc n", n=w["n"])
            P.dma("pool", dst, src, writes=[w["t"]], sem="cast")

        P.dma("pool", dft_s, dftc.rearrange("p a b -> p (a b)"), writes=[Tdft_s], sem="cast")
        order0 = (["gate", "go"] + ["in%d" % i for i in (0, 1, 2, 3, 4, 5, 6, 7, 12, 13)])
        for key in order0:
            cast_wb(key)
        cast_rest = (["in%d" % i for i in (8, 9, 10, 11)] + ["GTW"] + ["in%d" % i for i in range(14, 22)] + ["ao%d" % i for i in range(4)]
                     + ["bo%d" % i for i in range(4)] + ["wo%d" % i for i in range(4)] + ["fi%d" % i for i in range(22)]
                     + ["fo%d" % i for i in range(16)])
        cast_per_tile = -(-len(cast_rest) // max(1, (2 * NT - 1)))

        def cast_some(n):
            for _ in range(n):
                if not cast_rest:
                    return
                key = cast_rest.pop(0)
                if key == "GTW":
                    P.dma("pool", gtw_s, gtw, writes=[Tgtw_s], sem="cast")
                else:
                    cast_wb(key)

        class WStream:
            def __init__(self, nslots, stack, tag):
                self.n = nslots
                self.slots = [sb("wr%s%d" % (tag, i), [128, 8192], BF16, stack) for i in range(nslots)]
                self.T = [T() for _ in range(nslots)]
                self.tag = tag
                self.order = []
                self.issued = 0
                self.taken = 0

            def set_order(self, order):
                self.order = list(order)
                self.issued = 0
                self.taken = 0

            def _issue(self):
                key = self.order[self.issued]
                s = self.issued % self.n
                w = WB[key]
                P.dma("sp", self.slots[s][:, 0:w["KC"] * w["n"]], w["scr"], reads=[w["t"]], writes=[self.T[s]],
                      sem="wr%s%d" % (self.tag, s))
                self.issued += 1

            def get(self, key):
                assert self.order[self.taken] == key, (self.order[self.taken], key)
                while self.issued < len(self.order) and self.issued < self.taken + self.n:
                    self._issue()
                s = self.taken % self.n
                self.taken += 1
                w = WB[key]
                ap = self.slots[s][:, 0:w["KC"] * w["n"]].rearrange("p (kc n) -> p kc n", n=w["n"])
                return ap, self.T[s]

            def prefetch(self):
                while self.issued < len(self.order) and self.issued < self.taken + self.n:
                    self._issue()

        class Stats:
            def __init__(self, sq, Tsq):
                self.sq, self.Tsq = sq, Tsq
                self.n = 0
                self.pend = []

            def _flush(self):
                for f in self.pend:
                    f()
                self.pend = []

            def add(self, src, Tsrc, nchunk, col0, ntok, first, last):
                s2 = self.n % 2
                self.n += 1
                sqv = self.sq[s2][:, 0:nchunk * ntok].rearrange("p (n t) -> p n t", n=nchunk)
                P.x("act", "activation", reads=Tsrc, writes=[self.Tsq[s2]], out=sqv, in_=src, func=AF.Square)
                self._flush()

                def mm(s2=s2, sqv=sqv):
                    for i in range(nchunk):
                        P.x("pe", "matmul", reads=[self.Tsq[s2], Tib], writes=[Tps[7]], out=psb[7][:, col0:col0 + ntok], lhsT=onesb[:], rhs=sqv[:, i, :],
                            start=(first and i == 0), stop=(last and i == nchunk - 1))
                self.pend.append(mm)

            def finish(self, rstd, Trstd, ntok=512):
                self._flush()
                P.x("act", "activation", reads=[Tps[7], Tcb], writes=[Trstd], out=rstd[:, 0:ntok], in_=psb[7][:, 0:ntok], func=AF.Sqrt, scale=1.0 / D, bias=eps_ap)
                P.x("dve", "reciprocal", reads=[Trstd], writes=[Trstd], out=rstd[:, 0:ntok], in_=rstd[:, 0:ntok])

        def load_xT(xsrc, tok0, x_tm, Tx_tm, xT, TxT, nblk=4, stats=None):
            for blk in range(nblk):
                b2 = blk % len(x_tm)
                P.dma("sp", x_tm[b2][:], xsrc[tok0 + blk * 128: tok0 + (blk + 1) * 128, :], writes=[Tx_tm[b2]],
                      sem="xtm%d" % b2)
                for cg in range(4):
                    ps, tp = ps_next()
                    for i in range(4):
                        c = cg * 4 + i
                        P.x("pe", "transpose", reads=[Tx_tm[b2], Tcst], writes=[tp],
                            args=(ps[:, i * 128:(i + 1) * 128], x_tm[b2][:, c * 128:(c + 1) * 128], ident_ap,))
                    dst = xT[:, cg * 4:cg * 4 + 4, blk * 128:(blk + 1) * 128]
                    P.x("act", "copy", reads=[tp], writes=[TxT[cg * 4 + i] for i in range(4)], out=dst, in_=ps[:].rearrange("p (i t) -> p i t", i=4))
                    if stats is not None:
                        stats.add(dst, [TxT[cg * 4 + i] for i in range(4)], 4, blk * 128, 128, cg == 0, cg == 3)

        def fm_apply(src, Tsrc, gcol, dst, Tdst, rstd, Trstd, ntok=512):
            for c in range(16):
                P.x("dve", "scalar_tensor_tensor", reads=[Tsrc[c], Trstd, Tvec], writes=[Tdst[c]], out=dst[:, c, 0:ntok], in0=src[:, c, 0:ntok],
                    scalar=gcol[:, c:c + 1], in1=rstd[:, 0:ntok], op0=ALU.mult, op1=ALU.mult)

        def mm_ii(ps, tp, wap, wT, col0, act, Tact, KC, ntok=512, m=128):
            for k in range(KC):
                def mm(e, k=k):
                    return e.matmul(ps[0:m, 0:ntok], lhsT=wap[:, k, col0:col0 + m], rhs=act[:, k, 0:ntok],
                                    start=(k == 0), stop=(k == KC - 1))
                P.op("pe", mm, reads=[wT, Tact[k]], writes=[tp])

        def mm_i(ps, tp, wap, wT, c0, n, act, Tact, blk, KC):
            for k in range(KC):
                def mm(e, k=k):
                    return e.matmul(ps[:, 0:n], lhsT=act[:, k, blk * 128:(blk + 1) * 128], rhs=wap[:, k, c0:c0 + n],
                                    start=(k == 0), stop=(k == KC - 1))
                P.op("pe", mm, reads=[wT, Tact[k]], writes=[tp])

        st12 = contextlib.ExitStack()
        st12.__enter__()
        gs = sb("gs", [128, NB, 64], F32, st12)
        Tgs = [T() for _ in range(NB)]
        gso = sb("gso", [128, NB, 16], F32, st12)
        Tgso = [T() for _ in range(NB)]
        TC = [T() for _ in range(NH)]
        TCb = [T() for _ in range(NH)]
        Tn = T()
        Tnb = T()
        TCo = T()
        halo_raw = sb("halo_raw", [128, 16, 8], F32, st12)
        Thalo = T()
        pfx = sb("pfx", [128, 8], F32, st12)
        Tpfx = T()

        st1 = contextlib.ExitStack()
        st1.__enter__()
        x_tm = [sb("x_tm%d" % i, [128, D], F32, st1) for i in range(2)]
        Tx_tm = [T() for _ in range(2)]
        xT = sb("xT", [128, 16, 512], F32, st1)
        TxT = [T() for _ in range(16)]
        hT = sb("hT", [128, 16, 512], BF16, st1)
        ThT = [T() for _ in range(16)]
        sq = [sb("sq%d" % i, [128, 512], BF16, st1) for i in range(2)]
        Tsq = [T() for _ in range(2)]
        rstd = sb("rstd", [128, 512], F32, st1)
        Trstd = T()
        wg_t = sb("wg_t", [128, 16, 32], BF16, st1)
        wgo_t = sb("wgo_t", [128, 16, 16], BF16, st1)
        Twg = T()
        dft_t = sb("dft_t", [128, 3, 256], BF16, st1)
        Tdft = T()
        raw = [sb("raw%d" % i, [128, 516], F32, st1) for i in range(2)]
        Traw = [T() for _ in range(2)]
        acc = [sb("acc%d" % i, [128, 512], F32, st1) for i in range(2)]
        Tacc = [T() for _ in range(2)]
        qko = [sb("qko%d" % i, [128, 512], BF16, st1) for i in range(2)]
        Tqko = [T() for _ in range(2)]
        carry = sb("carry", [128, 16, 4], F32, st1)
        Tcarry = [T() for _ in range(16)]
        vst = [sb("vst%d" % i, [128, 4, 512], BF16, st1) for i in range(2)]
        Tvst = [T() for _ in range(2)]
        sgst = sb("sgst", [128, 4, 512], F32, st1)
        Tsgst = T()
        uT = sb("uT", [128, 8, 512], BF16, st1)
        TuT = [T() for _ in range(8)]
        Vst = sb("Vst", [128, 4, 16 * 128], BF16, st1)
        TVst = [T() for _ in range(4)]
        g32 = sb("g32", [128, 32], F32, st1)
        sp16 = sb("sp16", [128, 16], F32, st1)
        t16 = sb("t16", [128, 16], F32, st1)
        t16b = sb("t16b", [128, 16], F32, st1)
        Tg32, Tsp16, Tt16, Tt16b = T(), T(), T(), T()
        c32 = sb("c32", [128, 32], F32, st1)
        Tc32 = T()
        ws1 = WStream(3, st1, "a")

        P.dma("sp", wg_t[:], WB["gate"]["scr"].rearrange("p (k n) -> p k n", n=32), reads=[WB["gate"]["t"]], writes=[Twg], sem="c1")
        P.dma("sp", wgo_t[:], WB["go"]["scr"].rearrange("p (k n) -> p k n", n=16), reads=[WB["go"]["t"]], writes=[Twg], sem="c1")
        P.dma("sp", dft_t[:], dft_s.rearrange("p (a b) -> p a b", a=3), reads=[Tdft_s], writes=[Tdft], sem="c1")
        P.x("pool", "memset", writes=[Tpfx], args=(pfx[:], 0.0,))
        for i in range(2):
            P.x("pool", "memset", writes=[Traw[i]], args=(raw[i][:], 0.0,))

        def conv_emit(ridx, ch, ncol, dst_dram, dcol0, Tdst):
            r = raw[ridx]
            a = acc[ridx]
            o = qko[ridx]
            P.x("dve", "tensor_scalar", reads=[Traw[ridx], Tcw], writes=[Tacc[ridx]], out=a[:, 0:ncol], in0=r[:, 0:ncol], scalar1=cw_t[:, ch, 0:1], scalar2=None,
                                                 op0=ALU.mult)
            for j in range(1, 5):
                P.x("dve", "scalar_tensor_tensor", reads=[Traw[ridx], Tacc[ridx], Tcw], writes=[Tacc[ridx]], out=a[:, 0:ncol], in0=r[:, j:j + ncol], scalar=cw_t[:, ch, j:j + 1],
                                                                 in1=a[:, 0:ncol], op0=ALU.mult, op1=ALU.add)
            P.x("act", "activation", reads=[Tacc[ridx]], writes=[Tqko[ridx]], out=o[:, 0:ncol], in_=a[:, 0:ncol], func=AF.Silu)
            P.dma("pool", dst_dram[:, dcol0:dcol0 + ncol], o[:, 0:ncol], reads=[Tqko[ridx]], writes=[Tdst], sem="qko%d" % ridx)

        ws1.set_order(["in0", "in1", "in2", "in3"])
        stats1 = Stats(sq, Tsq)
        load_xT(x_halo, 0, x_tm, Tx_tm, xT, TxT, nblk=1, stats=stats1)
        stats1.finish(rstd, Trstd, ntok=128)
        fm_apply(xT, TxT, g_pre_mix, hT, ThT, rstd, Trstd, ntok=128)
        for wb in range(4):
            wap, wT = ws1.get("in%d" % wb)
            for hh in range(4):
                ch = wb * 4 + hh
                ps, tp = ps_next()
                mm_ii(ps, tp, wap, wT, hh * 128, hT, ThT, 16, ntok=128)
                P.x("act", "copy", reads=[tp], writes=[Thalo], out=halo_raw[:, ch, :], in_=ps[:, 0:8])

        def phase1_segment(own):
            xsrc = x_own if own else x_oth
            wkeys = (["in%d" % i for i in range(14)] if own else ["in2", "in3", "in4", "in5", "in6", "in7", "in12", "in13"])
            ws1.set_order(wkeys * NT)
            ks = k_s if own else ko_s
            Tks = Tk_s if own else Tko_s
            vs = v_s if own else vo_s
            Tvs = Tv_s if own else Tvo_s
            lo = 0 if own else 4
            for ch in range(16):
                if (not own) and ch < 8:
                    continue
                P.x("pool", "memset", reads=[], writes=[Tcarry[ch]], args=(carry[:, ch, 0:2], 0.0,))
                P.x("pool", "tensor_copy", reads=[Thalo, Tcarry[ch]], writes=[Tcarry[ch]], out=carry[:, ch, 2:4], in_=halo_raw[:, ch, lo:lo + 2])
            rr = [0]
            for it in range(NT):
                tok0 = it * 512
                load_xT(xsrc, tok0, x_tm, Tx_tm, xT, TxT, stats=stats1)
                stats1.finish(rstd, Trstd)
                fm_apply(xT, TxT, g_pre_mix, hT, ThT, rstd, Trstd)
                for wb in ((0, 1, 2, 3) if own else (2, 3)):
                    wap, wT = ws1.get("in%d" % wb)
                    for hh in range(4):
                        ch = wb * 4 + hh
                        ps, tp = ps_next()
                        mm_ii(ps, tp, wap, wT, hh * 128, hT, ThT, 16)
                        ri = rr[0] % 2
                        rr[0] += 1
                        r = raw[ri]
                        P.x("dve", "tensor_copy", reads=[Tcarry[ch]], writes=[Traw[ri]], out=r[:, 0:4], in_=carry[:, ch, :])
                        P.x("act", "copy", reads=[tp, Traw[ri]], writes=[Traw[ri]], out=r[:, 4:516], in_=ps[:])
                        P.x("dve", "tensor_copy", reads=[Traw[ri]], writes=[Tcarry[ch]], out=carry[:, ch, :], in_=r[:, 512:516])
                        dst = (q_s if ch < 8 else ks)[ch % 8]
                        conv_emit(ri, ch, 512, dst, tok0, Tq_s if ch < 8 else Tks)
                for cb in range(4):
                    wap, wT = ws1.get("in%d" % (4 + cb))
                    v2 = cb % 2
                    for blk in range(4):
                        ps, tp = ps_next()
                        mm_i(ps, tp, wap, wT, 0, 512, hT, ThT, blk, 16)
                        P.x("act", "copy", reads=[tp], writes=[Tvst[v2]], out=vst[v2][:, blk, :], in_=ps[:])
                    P.dma("pool", vs[tok0:tok0 + 512, cb * 512:(cb + 1) * 512].rearrange("(b p) f -> p b f", p=128), vst[v2][:],
                          reads=[Tvst[v2]], writes=[Tvs], sem="vst%d" % v2)
                if own:
                    for cb in range(4):
                        wap, wT = ws1.get("in%d" % (8 + cb))
                        for blk in range(4):
                            ps, tp = ps_next()
                            mm_i(ps, tp, wap, wT, 0, 512, hT, ThT, blk, 16)
                            P.x("act", "activation", reads=[tp], writes=[Tsgst], out=sgst[:, blk, :], in_=ps[:], func=AF.Sigmoid)
                            P.x("dve", "tensor_tensor", reads=[Tsgst, Trep], writes=[Tsgst], out=sgst[:, blk, :], in0=sgst[:, blk, :],
                                in1=ghead_rep[:, cb * 512:(cb + 1) * 512], op=ALU.mult)
                        P.dma("pool", sgo_s[tok0:tok0 + 512, cb * 512:(cb + 1) * 512].rearrange("(b p) f -> p b f", p=128), sgst[:],
                              reads=[Tsgst], writes=[Tsgo_s], sem="sgst")
                for ub in range(2):
                    wap, wT = ws1.get("in%d" % (12 + ub))
                    for gg in range(4):
                        g = ub * 4 + gg
                        ps, tp = ps_next()
                        mm_ii(ps, tp, wap, wT, gg * 128, hT, ThT, 16)
                        P.x("act", "copy", reads=[tp], writes=[TuT[g]], out=uT[:, g, :], in_=ps[:])
                        for blk in range(4):
                            ps2, tp2 = ps_next()
                            P.x("pe", "matmul", reads=[TuT[g], Tdft], writes=[tp2], out=ps2[:, 0:256], lhsT=uT[:, g, blk * 128:(blk + 1) * 128],
                                                                               rhs=dft_t[:, 0, :], start=True, stop=True)
                            P.x("dve", "tensor_copy", reads=[tp2], writes=[TVst[blk]],
                                out=Vst[:, blk, g * 256:(g + 1) * 256].rearrange("p (h r c) -> p h r c", h=2, r=2),
                                in_=ps2[:, 0:256].rearrange("p (r h c) -> p h r c", r=2, h=2))
                tv0 = tok0 + (0 if own else SEG)
                for blk in range(4):
                    P.dma("pool", V_s[:, tv0 + blk * 128: tv0 + (blk + 1) * 128, :].rearrange("g p x -> p g x"),
                          Vst[:, blk, :].rearrange("p (g x) -> p g x", x=128), reads=[TVst[blk]], writes=[TV_s], sem="Vst")
                for blk in range(4):
                    c = it * 4 + blk
                    ps, tp = ps_next()
                    if own:
                        for k in range(16):
                            P.x("pe", "matmul", reads=[ThT[k], Twg], writes=[tp], out=ps[:, 0:32], lhsT=hT[:, k, blk * 128:(blk + 1) * 128],
                                                                             rhs=wg_t[:, k, :], start=(k == 0), stop=(k == 15))
                        P.x("dve", "tensor_tensor", reads=[tp, Trep], writes=[Tg32], out=g32[:], in0=ps[:, 0:32], in1=bgate_rep, op=ALU.add)
                        gv = g32[:].rearrange("p (d j h) -> p d j h", d=2, j=2)
                        P.x("act", "activation", reads=[Tg32], writes=[Tsp16], out=sp16[:].rearrange("p (d h) -> p d h", d=2), in_=gv[:, :, 1, :],
                                                                func=AF.Exp, scale=-1.0)
                        P.x("act", "activation", reads=[Tsp16], writes=[Tsp16], out=sp16[:], in_=sp16[:], func=AF.Ln, bias=1.0)
                        ps2, tp2 = ps_next()
                        P.x("pe", "matmul", reads=[Tsp16, Tcst], writes=[tp2], out=ps2[:, 0:8], lhsT=U_ap, rhs=sp16[:, 0:8], start=True, stop=True)
                        P.x("pe", "matmul", reads=[Tsp16, Tcst], writes=[tp2], out=ps2[:, 8:16], lhsT=L_ap, rhs=sp16[:, 8:16], start=True, stop=True)
                        P.x("pe", "matmul", reads=[Tsp16, Tcst], writes=[tp2], out=ps2[:, 16:32], lhsT=ones_ap, rhs=sp16[:, 0:16], start=True, stop=True)
                        P.x("dve", "tensor_copy", reads=[tp2], writes=[Tc32], out=c32[:], in_=ps2[:, 0:32])
                        P.x("dve", "tensor_tensor", reads=[Tg32, Tc32], writes=[Tt16], out=t16[:].rearrange("p (d h) -> p d h", d=2), in0=gv[:, :, 0, :],
                                                                            in1=c32[:, 0:16].rearrange("p (d h) -> p d h", d=2), op=ALU.add)
                        P.x("act", "activation", reads=[Tt16, Tcb], writes=[Tgs[c]], out=gs[:, c, 0:16], in_=t16[:], func=AF.Exp, bias=lnc_ap)
                        P.x("dve", "tensor_tensor", reads=[Tt16, Tc32], writes=[Tt16b], out=t16b[:], in0=t16[:], in1=c32[:, 16:32], op=ALU.subtract)
                        P.x("act", "activation", reads=[Tt16b, Tcb], writes=[Tgs[c]], out=gs[:, c, 16:32], in_=t16b[:], func=AF.Exp, bias=lnc_ap)
                        P.x("act", "activation", reads=[Tc32], writes=[Tgs[c]], out=gs[:, c, 32:48], in_=c32[:, 0:16], func=AF.Exp)
                        P.x("act", "activation", reads=[Tc32], writes=[Tgs[c]], out=gs[:, c, 48:64], in_=c32[:, 16:32], func=AF.Exp, scale=-1.0)
                    else:
                        for k in range(16):
                            P.x("pe", "matmul", reads=[ThT[k], Twg], writes=[tp], out=ps[:, 0:16], lhsT=hT[:, k, blk * 128:(blk + 1) * 128],
                                                                             rhs=wgo_t[:, k, :], start=(k == 0), stop=(k == 15))
                        P.x("dve", "tensor_tensor", reads=[tp, Trep], writes=[Tg32], out=g32[:, 0:16], in0=ps[:, 0:16], in1=bgo_rep, op=ALU.add)
                        P.x("act", "activation", reads=[Tg32], writes=[Tsp16], out=sp16[:, 0:8], in_=g32[:, 8:16], func=AF.Exp, scale=-1.0)
                        P.x("act", "activation", reads=[Tsp16], writes=[Tsp16], out=sp16[:, 0:8], in_=sp16[:, 0:8], func=AF.Ln, bias=1.0)
                        ps2, tp2 = ps_next()
                        P.x("pe", "matmul", reads=[Tsp16, Tcst], writes=[tp2], out=ps2[:, 0:8], lhsT=Mo_ap, rhs=sp16[:, 0:8], start=True, stop=True)
                        P.x("pe", "matmul", reads=[Tsp16, Tcst], writes=[tp2], out=ps2[:, 8:16], lhsT=ones_ap, rhs=sp16[:, 0:8], start=True, stop=True)
                        P.x("dve", "tensor_copy", reads=[tp2], writes=[Tc32], out=c32[:, 0:16], in_=ps2[:, 0:16])
                        P.x("dve", "tensor_tensor", reads=[Tg32, Tc32], writes=[Tt16], out=t16[:, 0:8], in0=g32[:, 0:8], in1=c32[:, 0:8], op=ALU.subtract)
                        P.x("act", "activation", reads=[Tt16, Tcb], writes=[Tt16], out=t16[:, 0:8], in_=t16[:, 0:8], func=AF.Exp, bias=lnc_ap)
                        P.x("act", "activation", reads=[Tpfx, Trep], writes=[Tt16b], out=t16b[:, 0:8], in_=pfx[:], func=AF.Exp, scale=flags[:, 3:4])
                        P.x("dve", "tensor_tensor", reads=[Tt16, Tt16b], writes=[Tgso[c]], out=gso[:, c, 0:8], in0=t16[:, 0:8], in1=t16b[:, 0:8], op=ALU.mult)
                        P.x("act", "activation", reads=[Tc32, Trep], writes=[Tgso[c]], out=gso[:, c, 8:16], in_=c32[:, 8:16], func=AF.Exp, scale=flags[:, 2:3])
                        P.x("dve", "tensor_tensor", reads=[Tpfx, Tc32], writes=[Tpfx], out=pfx[:], in0=pfx[:], in1=c32[:, 8:16], op=ALU.add)
                cast_some(cast_per_tile)
            for ch in range(16):
                if (not own) and ch < 8:
                    continue
                ri = rr[0] % 2
                rr[0] += 1
                r = raw[ri]
                P.x("dve", "tensor_copy", reads=[Tcarry[ch]], writes=[Traw[ri]], out=r[:, 0:4], in_=carry[:, ch, :])
                P.x("dve", "tensor_copy", reads=[Thalo, Traw[ri]], writes=[Traw[ri]], out=r[:, 4:6], in_=halo_raw[:, ch, lo + 2:lo + 4])
                dst = (q_s if ch < 8 else ks)[ch % 8]
                conv_emit(ri, ch, 2, dst, SEG, Tq_s if ch < 8 else Tks)

        phase1_segment(False)
        phase1_segment(True)
        cast_some(len(cast_rest))
        P.barrier()
        st1.__exit__(None, None, None)

        st2 = contextlib.ExitStack()
        st2.__enter__()
        Cst = sb("Cst", [128, NH, 256], F32, st2)
        nst = sb("nst", [128, NH], F32, st2)
        Cbf = sb("Cbf", [128, NH, 256], BF16, st2)
        nbf = sb("nbf", [128, NH], BF16, st2)
        Coth = sb("Coth", [128, NH, 256], F32, st2)
        noth = sb("noth", [128, NH], F32, st2)
        qsc = [sb("qsc%d" % i, [128, NH, 512], BF16, st2) for i in range(2)]
        ksc = [sb("ksc%d" % i, [128, NH, 512], BF16, st2) for i in range(2)]
        vsc = [sb("vsc%d" % i, [128, 4, W_V], BF16, st2) for i in range(2)]
        Tqsc = [T() for _ in range(2)]
        Tksc = [T() for _ in range(2)]
        Tvsc = [T() for _ in range(2)]
        Pm = sb("Pm", [128, NH, 128], BF16, st2)
        TPm = [T() for _ in range(NH)]
        kt = sb("kt", [128, NH, 128], BF16, st2)
        Tkt = [T() for _ in range(NH)]
        hout = [sb("hout%d" % i, [128, W_V], F32, st2) for i in range(3)]
        Thout = [[T(), T()] for _ in range(3)]
        hfl = [sb("hfl%d" % i, [128, W_V], F32, st2) for i in range(3)]
        Thfl = [T() for _ in range(3)]
        sgl = [sb("sgl%d" % i, [128, W_V], F32, st2) for i in range(3)]
        Tsgl = [T() for _ in range(3)]
        hsq = sb("hsq", [128, 256], F32, st2)
        Thsq = T()
        hab = [sb("hab%d" % i, [128, W_V], BF16, st2) for i in range(2)]
        Thab = [T() for _ in range(2)]
        rr8 = sb("rr8", [128, 8], F32, st2)
        Trr8 = T()
        ss8 = sb("ss8", [128, 8], F32, st2)
        Tss8 = T()
        tmpn = sb("tmpn", [128, 8], F32, st2)
        Ttmpn = T()

        def sweep(kind):
            own = kind != "oth"
            order = list(range(NB)) if kind != "bwd" else list(range(NB - 1, -1, -1))
            ksrc, Tksrc = (k_s, Tk_s) if own else (ko_s, Tko_s)
            vsrc, Tvsrc = (v_s, Tv_s) if own else (vo_s, Tvo_s)
            mask = U_ap if kind == "fwd" else L_ap
            if kind == "oth":
                for h in range(NH):
                    P.x("pool", "memset", writes=[TC[h]], args=(Cst[:, h, :], 0.0,))
                    P.x("pool", "memset", writes=[TCb[h]], args=(Cbf[:, h, :], 0.0,))
                P.x("pool", "memset", writes=[Tn], args=(nst[:], 0.0,))
                P.x("pool", "memset", writes=[Tnb], args=(nbf[:], 0.0,))
            else:
                fcol = flags[:, 0:1] if kind == "fwd" else flags[:, 1:2]
                for h in range(NH):
                    P.x("dve", "tensor_scalar", reads=[TCo, Trep], writes=[TC[h]], out=Cst[:, h, :], in0=Coth[:, h, :], scalar1=fcol, scalar2=None, op0=ALU.mult)
                    P.x("act", "copy", reads=[TC[h]], writes=[TCb[h]], out=Cbf[:, h, :], in_=Cst[:, h, :])
                P.x("dve", "tensor_scalar", reads=[TCo, Trep], writes=[Tn], out=nst[:], in0=noth[:], scalar1=fcol, scalar2=None, op0=ALU.mult)
                P.x("act", "copy", reads=[Tn], writes=[Tnb], out=nbf[:], in_=nst[:])
            if own:
                a_i, ap_i, ei_i, eb_i = (0, 16, 32, 48) if kind == "fwd" else (8, 24, 40, 56)
            else:
                ap_i, eb_i = 0, 8

            def load_sc(sci):
                sc = order[sci * 4] // 4
                b = sci % 2
                t0 = sc * 512
                P.dma("sp", ksc[b][:], ksrc[:, :, t0 + 2:t0 + 514].rearrange("h p t -> p h t"), reads=[Tksrc], writes=[Tksc[b]], sem="ksc%d" % b)
                P.dma("sp", vsc[b][:], vsrc[t0:t0 + 512, :].rearrange("(b p) f -> p b f", p=128), reads=[Tvsrc], writes=[Tvsc[b]], sem="vsc%d" % b)
                if own:
                    P.dma("sp", qsc[b][:], q_s[:, :, t0 + 2:t0 + 514].rearrange("h p t -> p h t"), reads=[Tq_s], writes=[Tqsc[b]], sem="qsc%d" % b)

            def prefetch_epi(ci):
                c = order[ci]
                trow = slice(c * 128, (c + 1) * 128)
                P.dma("sp", hfl[ci % 3][:], hf_s[trow, :], reads=[Thf_s], writes=[Thfl[ci % 3]], sem="hfl%d" % (ci % 3))
                P.dma("sp", sgl[ci % 3][:], sgo_s[trow, :], reads=[Tsgo_s], writes=[Tsgl[ci % 3]], sem="sgl%d" % (ci % 3))

            def epilogue(ci):
                c = order[ci]
                ho = hout[ci % 3]
                Tho = Thout[ci % 3]
                trow = slice(c * 128, (c + 1) * 128)
                if kind == "fwd":
                    P.dma("pool", hf_s[trow, :], ho[:], reads=Tho, writes=[Thf_s], sem="hout%d" % (ci % 3))
                elif kind == "bwd":
                    hf_, Thf_ = hfl[ci % 3], Thfl[ci % 3]
                    sg_, Tsg_ = sgl[ci % 3], Tsgl[ci % 3]
                    P.x("pool", "tensor_tensor", reads=Tho + [Thf_], writes=Tho, out=ho[:], in0=ho[:], in1=hf_[:], op=ALU.add)
                    for h in range(NH):
                        P.x("act", "activation", reads=Tho + [Tss8], writes=[Thsq, Tss8], out=hsq[:], in_=ho[:, h * 256:(h + 1) * 256], func=AF.Square, accum_out=ss8[:, h:h + 1])
                    P.x("act", "activation", reads=[Tss8, Tcb], writes=[Tss8], out=ss8[:], in_=ss8[:], func=AF.Sqrt, scale=1.0 / DV, bias=eps_ap)
                    P.x("dve", "reciprocal", reads=[Tss8], writes=[Tss8], out=ss8[:], in_=ss8[:])
                    hb = hab[ci % 2]
                    Thb = Thab[ci % 2]
                    for h in range(NH):
                        P.x("dve", "scalar_tensor_tensor", reads=Tho + [Tss8, Tsg_], writes=[Thb], out=hb[:, h * 256:(h + 1) * 256], in0=ho[:, h * 256:(h + 1) * 256],
                            scalar=ss8[:, h:h + 1], in1=sg_[:, h * 256:(h + 1) * 256], op0=ALU.mult, op1=ALU.mult)
                    P.dma("pool", ha_s[trow, :], hb[:], reads=[Thb], writes=[Tha_s], sem="hab%d" % (ci % 2))

            nsc_tot = NB // 4
            pending = [None]
            load_sc(0)
            for ci, c in enumerate(order):
                sci = ci // 4
                if ci % 4 == 0 and sci + 1 < nsc_tot:
                    load_sc(sci + 1)
                if kind == "bwd":
                    prefetch_epi(ci)
                b = sci % 2
                cc = c % 4
                tsl = slice(cc * 128, (cc + 1) * 128)
                gsrc, Tg = (gs, Tgs[c]) if own else (gso, Tgso[c])
                ho = hout[ci % 3]
                Tho = Thout[ci % 3]
                psS = [ps_next() for _ in range(2)] if own else None
                if own:
                    for h in range(NH):
                        pS, tS = psS[h // 4]
                        hh = h % 4
                        P.x("pe", "matmul", reads=[Tksc[b], Tqsc[b]], writes=[tS], out=pS[:, hh * 128:(hh + 1) * 128], lhsT=ksc[b][:, h, tsl],
                            rhs=qsc[b][:, h, tsl], start=True, stop=True)
                psK, tK = ps_next()
                psKb = psK[:].bitcast(BF16)
                for h in range(NH):
                    P.x("pe", "transpose", reads=[Tksc[b], Tib], writes=[tK], args=(psKb[:, h * 128:(h + 1) * 128], ksc[b][:, h, tsl], identb[:],))
                if own:
                    for h in range(NH):
                        pS, tS = psS[h // 4]
                        hh = h % 4
                        P.x("dve", "scalar_tensor_tensor", reads=[tS, Tg, Tcst], writes=[TPm[h]], out=Pm[:, h, :], in0=pS[:, hh * 128:(hh + 1) * 128],
                            scalar=gsrc[:, c, a_i + h:a_i + h + 1], in1=mask, op0=ALU.mult, op1=ALU.mult)
                for h in range(NH):
                    P.x("act", "activation", reads=[tK, Tg], writes=[Tkt[h]], out=kt[:, h, :], in_=psKb[:, h * 128:(h + 1) * 128], func=AF.Copy,
                        scale=gsrc[:, c, ap_i + h:ap_i + h + 1])
                psC = [ps_next() for _ in range(4)]
                psn, tn_ = ps_next()
                for h in range(NH):
                    pC, tC = psC[h // 2]
                    cs = slice((h % 2) * 256, (h % 2) * 256 + 256)
                    P.x("pe", "matmul", reads=[Tkt[h], Tvsc[b]], writes=[tC], out=pC[:, cs], lhsT=kt[:, h, :], rhs=vsc[b][:, cc, h * 256:(h + 1) * 256],
                        start=True, stop=True)
                    P.x("pe", "matmul", reads=[Tkt[h], Tib], writes=[tn_], out=psn[:, 8 + h:9 + h], lhsT=kt[:, h, :], rhs=onesb[:, 0:1], start=True, stop=True)
                for h in range(NH):
                    pC, tC = psC[h // 2]
                    cs = slice((h % 2) * 256, (h % 2) * 256 + 256)
                    P.x("dve", "scalar_tensor_tensor", reads=[TC[h], Tg, tC], writes=[TC[h]], out=Cst[:, h, :], in0=Cst[:, h, :],
                        scalar=gsrc[:, c, eb_i + h:eb_i + h + 1], in1=pC[:, cs], op0=ALU.mult, op1=ALU.add)
                P.x("dve", "tensor_tensor", reads=[Tn, Tg], writes=[Ttmpn], out=tmpn[:], in0=nst[:], in1=gsrc[:, c, eb_i:eb_i + 8], op=ALU.mult)
                P.x("dve", "tensor_tensor", reads=[Ttmpn, tn_, Tn], writes=[Tn], out=nst[:], in0=tmpn[:], in1=psn[:, 8:16], op=ALU.add)
                if own:
                    psR = [ps_next() for _ in range(4)]
                    psr, tr_ = ps_next()
                    for h in range(NH):
                        pR, tR = psR[h // 2]
                        cs = slice((h % 2) * 256, (h % 2) * 256 + 256)
                        P.x("pe", "matmul", reads=[TPm[h], Tvsc[b]], writes=[tR], out=pR[:, cs], lhsT=Pm[:, h, :], rhs=vsc[b][:, cc, h * 256:(h + 1) * 256],
                            start=True, stop=False)
                        P.x("pe", "matmul", reads=[Tqsc[b], TCb[h]], writes=[tR], out=pR[:, cs], lhsT=qsc[b][:, h, tsl], rhs=Cbf[:, h, :], start=False, stop=True)
                        P.x("pe", "matmul", reads=[TPm[h], Tib], writes=[tr_], out=psr[:, h:h + 1], lhsT=Pm[:, h, :], rhs=onesb[:, 0:1], start=True, stop=False)
                        P.x("pe", "matmul", reads=[Tqsc[b], Tnb], writes=[tr_], out=psr[:, h:h + 1], lhsT=qsc[b][:, h, tsl], rhs=nbf[:, h:h + 1], start=False, stop=True)
                if own:
                    cast_eng = (["dve", "dve", "pool", "dve", "dve", "dve", "pool", "dve"] if kind == "fwd"
                                else ["pool", "dve", "act", "dve", "pool", "dve", "act", "pool"])
                    for h in range(NH):
                        if cast_eng[h] == "act":
                            P.x("act", "copy", reads=[TC[h]], writes=[TCb[h]], out=Cbf[:, h, :], in_=Cst[:, h, :])
                        else:
                            P.x(cast_eng[h], "tensor_copy", reads=[TC[h]], writes=[TCb[h]], out=Cbf[:, h, :], in_=Cst[:, h, :])
                    P.x("dve", "tensor_copy", reads=[Tn, Tnb], writes=[Tnb], out=nbf[:], in_=nst[:])
                def r_evac(c=c, ho=ho, Tho=Tho, gsrc=gsrc, Tg=Tg, psR=(psR if own else None), psr=(psr if own else None), tr_=(tr_ if own else None)):
                    if own:
                        P.x("act", "activation", reads=[tr_], writes=[Trr8], out=rr8[:], in_=psr[:, 0:8], func=AF.Abs)
                        P.x("dve", "tensor_tensor", reads=[Trr8, Tg], writes=[Trr8], out=rr8[:], in0=rr8[:], in1=gsrc[:, c, ei_i:ei_i + 8], op=ALU.max)
                        P.x("dve", "reciprocal", reads=[Trr8], writes=[Trr8], out=rr8[:], in_=rr8[:])
                        for h in range(NH):
                            pR, tR = psR[h // 2]
                            cs = slice((h % 2) * 256, (h % 2) * 256 + 256)
                            if h < 4 or kind == "fwd":
                                P.x("act", "activation", reads=[tR, Trr8], writes=[Tho[0]], out=ho[:, h * 256:(h + 1) * 256], in_=pR[:, cs], func=AF.Copy, scale=rr8[:, h:h + 1])
                            else:
                                P.x("dve", "tensor_scalar", reads=[tR, Trr8], writes=[Tho[1]], out=ho[:, h * 256:(h + 1) * 256], in0=pR[:, cs], scalar1=rr8[:, h:h + 1],
                                    scalar2=None, op0=ALU.mult)
                r_evac()
                if own and ci > 1:
                    epilogue(ci - 2)
            if own:
                epilogue(NB - 2)
                epilogue(NB - 1)
            if kind == "oth":
                for h in range(NH):
                    P.x("dve", "tensor_copy", reads=[TC[h], TCo], writes=[TCo], out=Coth[:, h, :], in_=Cst[:, h, :])
                P.x("dve", "tensor_copy", reads=[Tn, TCo], writes=[TCo], out=noth[:], in_=nst[:])

        ps_mod[0] = 8
        ps_rr[0] = 0
        sweep("oth")
        sweep("fwd")
        P.barrier()
        sweep("bwd")
        P.barrier()
        st2.__exit__(None, None, None)
        st12.__exit__(None, None, None)

        stf = contextlib.ExitStack()
        stf.__enter__()
        KB = 2 * NB
        Vt = [sb("Vt%d" % i, [KB, 128, 128], BF16, stf) for i in range(2)]
        TVt = [T() for _ in range(2)]
        At = sb("At", [128, 128, 2, 64], BF16, stf)
        TAt = T()
        Gt = sb("Gt", [128, 128, 2, NB], BF16, stf)
        TGt = T()
        F1 = sb("F1", [128, 3, 256], BF16, stf)
        TF1 = T()
        Yst = [sb("Yst%d" % i, [NB, 128, 64], BF16, stf) for i in range(2)]
        TYst = [T() for _ in range(2)]
        P.dma("sp", Gt[:], gtw_s.rearrange("p (s r j) -> p s r j", s=128, r=2), reads=[Tgtw_s], writes=[TGt], sem="f0")
        P.dma("sp", F1[:], dft_s.rearrange("p (a b) -> p a b", a=3), reads=[Tdft_s], writes=[TF1], sem="f1")
        Yv = Y_s.rearrange("(j s) c -> j s c", s=128)
        for gh in range(16):
            b = gh % 2
            P.dma("sp", Vt[b][:], V_s[gh].rearrange("(b a) x -> b a x", a=128), reads=[TV_s], writes=[TVt[b]], sem="Vt%d" % b)
            for cp in range(0, 64, 2):
                ps, tp = ps_next()
                for i in range(2):
                    c_ = cp + i
                    P.x("pe", "matmul", reads=[TVt[b], TF1], writes=[tp], out=ps[:, i * 256:(i + 1) * 256], lhsT=Vt[b][:, :, c_], rhs=F1[0:KB, 1, :],
                                                                  start=True, stop=False)
                    P.x("pe", "matmul", reads=[TVt[b], TF1], writes=[tp], out=ps[:, i * 256:(i + 1) * 256], lhsT=Vt[b][:, :, 64 + c_], rhs=F1[0:KB, 2, :],
                                                                  start=False, stop=True)
                eng = "act" if (cp // 2) % 2 == 0 else "dve"
                if eng == "act":
                    P.x("act", "copy", reads=[tp], writes=[TAt], out=At[:, :, :, cp:cp + 2].rearrange("p s r c -> p c r s"),
                                                             in_=ps[:].rearrange("p (c r s) -> p c r s", c=2, r=2))
                else:
                    P.x("dve", "tensor_copy", reads=[tp], writes=[TAt], out=At[:, :, :, cp:cp + 2].rearrange("p s r c -> p c r s"),
                                                                    in_=ps[:].rearrange("p (c r s) -> p c r s", c=2, r=2))
            yb = Yst[gh % 2]
            Tyb = TYst[gh % 2]
            for s0 in range(0, 128, 8):
                ps, tp = ps_next()
                for i in range(8):
                    s = s0 + i
                    P.x("pe", "matmul", reads=[TGt, TAt], writes=[tp], out=ps[0:NB, i * 64:(i + 1) * 64], lhsT=Gt[:, s, 0, :], rhs=At[:, s, 0, :],
                                                                start=True, stop=False)
                    P.x("pe", "matmul", reads=[TGt, TAt], writes=[tp], out=ps[0:NB, i * 64:(i + 1) * 64], lhsT=Gt[:, s, 1, :], rhs=At[:, s, 1, :],
                                                                start=False, stop=True)
                if (s0 // 8) % 2 == 0:
                    P.x("act", "copy", reads=[tp], writes=[Tyb], out=yb[:, s0:s0 + 8, :], in_=ps[0:NB, :].rearrange("p (s c) -> p s c", s=8))
                else:
                    P.x("dve", "tensor_copy", reads=[tp], writes=[Tyb], out=yb[:, s0:s0 + 8, :], in_=ps[0:NB, :].rearrange("p (s c) -> p s c", s=8))
            P.dma("pool", Yv[:, :, gh * 64:(gh + 1) * 64], yb[:], reads=[Tyb], writes=[TY_s], sem="Yst%d" % (gh % 2))
        P.barrier()
        stf.__exit__(None, None, None)

        ps_mod[0] = 7
        ps_rr[0] = 0
        st3 = contextlib.ExitStack()
        st3.__enter__()
        x_tm = [sb("x3_tm%d" % i, [128, D], F32, st3) for i in range(1)]
        Tx_tm = [T() for _ in range(1)]
        xT = sb("x3T", [128, 16, 512], F32, st3)
        TxT = [T() for _ in range(16)]
        hT = sb("h3T", [128, 16, 512], BF16, st3)
        ThT = [T() for _ in range(16)]
        mixT = sb("mixT", [128, 16, 512], F32, st3)
        TmixT = [T() for _ in range(16)]
        FF = sb("FF", [128, 44, 512], BF16, st3)
        TFF = [T() for _ in range(44)]
        sq = [sb("sq3%d" % i, [128, 512], BF16, st3) for i in range(2)]
        Tsq = [T() for _ in range(2)]
        rstd = sb("rstd3", [128, 512], F32, st3)
        Trstd = T()
        gA = sb("gA", [128, 4, 512], F32, st3)
        gB = sb("gB", [128, 4, 512], F32, st3)
        TgA = [T() for _ in range(4)]
        TgB = [T() for _ in range(4)]
        ws3 = WStream(2, st3, "b")
        order3 = []
        for jj in range(4):
            order3 += ["in%d" % (14 + jj), "in%d" % (18 + jj), "ao%d" % jj, "bo%d" % jj]
        order3 += ["wo%d" % i for i in range(4)]
        for jj in range(11):
            order3 += ["fi%d" % jj, "fi%d" % (11 + jj)]
        order3 += ["fo%d" % i for i in range(16)]
        ws3.set_order(order3 * NT)
        haT = FF[:, 0:16, :]
        ThaT = TFF[0:16]
        fbT = FF[:, 16:24, :]
        TfbT = TFF[16:24]
        mixinT = FF[:, 24:40, :]
        TmixinT = TFF[24:40]

        stats3 = Stats(sq, Tsq)

        def post_norm(src, Tsrc, gcol, next_stats):
            stats3.finish(rstd, Trstd)
            for c in range(16):
                P.x("dve", "scalar_tensor_tensor", reads=[Tsrc[c], Trstd, Tvec], writes=[Tsrc[c]], out=src[:, c, :], in0=src[:, c, :],
                    scalar=gcol[:, c:c + 1], in1=rstd[:], op0=ALU.mult, op1=ALU.mult)
                P.x("pool" if c % 2 == 0 else "dve", "tensor_tensor", reads=[TxT[c], Tsrc[c]], writes=[TxT[c]], out=xT[:, c, :], in0=xT[:, c, :],
                    in1=src[:, c, :], op=ALU.add)
                if next_stats:
                    stats3.add(xT[:, c:c + 1, :], [TxT[c]], 1, 0, 512, c == 0, c == 15)

        for it in range(NT):
            tok0 = it * 512
            ws3.prefetch()
            load_xT(x_own, tok0, x_tm, Tx_tm, xT, TxT, stats=stats3)
            for k in range(16):
                P.dma("sp", haT[:, k, :], ha_s[tok0: tok0 + 512, k * 128:(k + 1) * 128],
                      reads=[Tha_s], writes=[ThaT[k]], sem="haT%d" % (k % 4), transpose=True)
            for k in range(8):
                P.dma("sp", fbT[:, k, :], Y_s[tok0: tok0 + 512, k * 128:(k + 1) * 128],
                      reads=[TY_s], writes=[TfbT[k]], sem="fbT%d" % (k % 2), transpose=True)
            stats3.finish(rstd, Trstd)
            fm_apply(xT, TxT, g_pre_mix, hT, ThT, rstd, Trstd)
            for jj in range(4):
                wA, TwA = ws3.get("in%d" % (14 + jj))
                for j4 in range(4):
                    j = jj * 4 + j4
                    ps, tp = ps_next()
                    mm_ii(ps, tp, wA, TwA, j4 * 128, hT, ThT, 16)
                    P.x("act", "activation", reads=[tp, Tvec], writes=[TgA[j4]], out=gA[:, j4, :], in_=ps[:], func=AF.Sigmoid, bias=b_merge[:, j:j + 1])
                wB, TwB = ws3.get("in%d" % (18 + jj))
                for j4 in range(4):
                    j = jj * 4 + j4
                    ps, tp = ps_next()
                    mm_ii(ps, tp, wB, TwB, j4 * 128, hT, ThT, 16)
                    P.x("act", "activation", reads=[tp, Tvec], writes=[TgB[j4]], out=gB[:, j4, :], in_=ps[:], func=AF.Sigmoid, bias=b_merge[:, 16 + j:17 + j])
                wao, Twao = ws3.get("ao%d" % jj)
                for j4 in range(4):
                    ps, tp = ps_next()
                    mm_ii(ps, tp, wao, Twao, j4 * 128, haT, ThaT, 16)
                    P.x("dve", "tensor_tensor", reads=[tp, TgA[j4]], writes=[TgA[j4]], out=gA[:, j4, :], in0=gA[:, j4, :], in1=ps[:], op=ALU.mult)
                wbo, Twbo = ws3.get("bo%d" % jj)
                for j4 in range(4):
                    j = jj * 4 + j4
                    ps, tp = ps_next()
                    mm_ii(ps, tp, wbo, Twbo, j4 * 128, fbT, TfbT, 8)
                    P.x("dve", "tensor_tensor", reads=[tp, TgB[j4]], writes=[TgB[j4]], out=gB[:, j4, :], in0=gB[:, j4, :], in1=ps[:], op=ALU.mult)
                    P.x("pool", "tensor_tensor", reads=[TgA[j4], TgB[j4]], writes=[TmixinT[j]], out=mixinT[:, j, :], in0=gA[:, j4, :], in1=gB[:, j4, :], op=ALU.add)
            if it == 0 and "dbg_ff" in DEBUG_OUT:
                dbg_h = dscr("dbg_h", [128, 16 * 512], BF16)
                P.dma("pool", dbg_h, hT[:].rearrange("p a b -> p (a b)"), reads=ThT, sem="dbg")
                dbg_g = dscr("dbg_g", [128, 8 * 512], F32)
                P.dma("pool", dbg_g[:, 0:2048], gA[:].rearrange("p a b -> p (a b)"), reads=TgA, sem="dbg")
                P.dma("pool", dbg_g[:, 2048:4096], gB[:].rearrange("p a b -> p (a b)"), reads=TgB, sem="dbg")
                dbg_ff = dscr("dbg_ff", [128, 40 * 512], BF16)
                P.dma("pool", dbg_ff, FF[:, 0:40, :].rearrange("p a b -> p (a b)"), reads=TFF[0:40], sem="dbg")
            for jj in range(4):
                wo, Two = ws3.get("wo%d" % jj)
                for j4 in range(4):
                    j = jj * 4 + j4
                    ps, tp = ps_next()
                    mm_ii(ps, tp, wo, Two, j4 * 128, mixinT, TmixinT, 16)
                    P.x("act", "copy", reads=[tp], writes=[TmixT[j]], out=mixT[:, j, :], in_=ps[:])
                    stats3.add(mixT[:, j:j + 1, :], [TmixT[j]], 1, 0, 512, j == 0, j == 15)
            post_norm(mixT, TmixT, g_post_mix, True)
            stats3.finish(rstd, Trstd)
            fm_apply(xT, TxT, g_pre_ffn, hT, ThT, rstd, Trstd)
            for jj in range(11):
                wg_, Twg_ = ws3.get("fi%d" % jj)
                for j4 in range(4):
                    ps, tp = ps_next()
                    mm_ii(ps, tp, wg_, Twg_, j4 * 128, hT, ThT, 16)
                    P.x("act", "activation", reads=[tp], writes=[TgA[j4]], out=gA[:, j4, :], in_=ps[:], func=AF.Silu)
                wu_, Twu_ = ws3.get("fi%d" % (11 + jj))
                for j4 in range(4):
                    j = jj * 4 + j4
                    ps, tp = ps_next()
                    mm_ii(ps, tp, wu_, Twu_, j4 * 128, hT, ThT, 16)
                    P.x("dve", "tensor_tensor", reads=[tp, TgA[j4]], writes=[TFF[j]], out=FF[:, j, :], in0=gA[:, j4, :], in1=ps[:], op=ALU.mult)
            for j in range(16):
                wf, Twf = ws3.get("fo%d" % j)
                ps, tp = ps_next()
                mm_ii(ps, tp, wf, Twf, 0, FF, TFF, 44)
                P.x("act", "copy", reads=[tp], writes=[TmixT[j]], out=mixT[:, j, :], in_=ps[:])
                stats3.add(mixT[:, j:j + 1, :], [TmixT[j]], 1, 0, 512, j == 0, j == 15)
            post_norm(mixT, TmixT, g_post_ffn, False)
            for blk in range(4):
                b2 = 0
                for cg in range(4):
                    ps, tp = ps_next()
                    for i in range(4):
                        c = cg * 4 + i
                        P.x("pe", "transpose", reads=[TxT[c], Tcst], writes=[tp], args=(ps[:, i * 128:(i + 1) * 128], xT[:, c, blk * 128:(blk + 1) * 128], ident_ap,))
                    P.x("act", "copy", reads=[tp], writes=[Tx_tm[b2]], out=x_tm[b2][:, cg * 512:(cg + 1) * 512], in_=ps[:])
                P.dma("pool", y_out[tok0 + blk * 128: tok0 + (blk + 1) * 128, :], x_tm[b2][:], reads=[Tx_tm[b2]], sem="yo%d" % b2)
        P.finish()
        st3.__exit__(None, None, None)
        P.emit()
    return nc


def _fm(v):
    return np.ascontiguousarray(np.asarray(v, np.float32).reshape(-1, 128).T)


def core_inputs(NB, role, x_own, x_oth, halo8, W, seq_T):
    SEG = NB * 128
    hf = role["hf"]
    prompt = role["prompt"]
    x_halo = np.zeros((128, D), np.float32)
    x_halo[0:8] = halo8
    w_in = W["w_in"]
    bg = W["b_gates"]
    if hf == 1:
        w_go = np.concatenate([w_in[:, OFF_G + 0:OFF_G + 8], w_in[:, OFF_G + 8:OFF_G + 16]], axis=1)
        b_go = np.concatenate([bg[0:8], bg[8:16]])
    else:
        w_go = np.concatenate([w_in[:, OFF_G + 16:OFF_G + 24], w_in[:, OFF_G + 24:OFF_G + 32]], axis=1)
        b_go = np.concatenate([bg[16:24], bg[24:32]])
    vecs = np.concatenate([_fm(W["g_pre_mix"]), _fm(W["g_post_mix"]), _fm(W["g_pre_ffn"]), _fm(W["g_post_ffn"]),
                           _fm(W["b_merge"])], axis=1)
    convw = np.ascontiguousarray(W["conv_w"].reshape(5, 16, 128).transpose(2, 1, 0))
    flagP = 1.0 if (prompt and hf == 1) else 0.0
    flagS = 1.0 if (prompt and hf == 0) else 0.0
    rep = np.concatenate([np.tile(W["g_head"][None, :], (128, 1)), np.tile(bg[None, :], (128, 1)),
                          np.tile(b_go[None, :], (128, 1)),
                          np.tile(np.array([[flagP, flagS, -flagP, -flagS]], np.float32), (128, 1))], axis=1).astype(np.float32)
    idx = np.arange(128)
    U = (idx[:, None] <= idx[None, :]).astype(np.float32)
    L = (idx[:, None] >= idx[None, :]).astype(np.float32)
    Mo = (idx[:, None] > idx[None, :]).astype(np.float32) if hf == 1 else (idx[:, None] < idx[None, :]).astype(np.float32)
    cst = np.stack([U, L, Mo, np.ones((128, 128), np.float32), np.eye(128, dtype=np.float32)], axis=1)
    c = np.arange(128, dtype=np.float64)
    ang = 2 * np.pi * np.outer(c, c) / 128.0
    chd = np.concatenate([np.cos(ang), -np.sin(ang)], axis=1)
    N = 2 * SEG
    s = np.arange(128)
    j = np.arange(NB)
    tok = s[:, None] + 128 * j[None, :]
    kp = (tok + SEG * hf) if prompt else 2 * tok
    KB = 2 * NB
    kap = kp[:, 0] % KB
    bpart = np.arange(KB)
    b_true = np.where(bpart < NB, bpart + NB * hf, (bpart - NB) + NB * (1 - hf)).astype(np.float64)
    phi = 2 * np.pi * np.outer(b_true, kap) / float(KB)
    Fr, Fi = np.cos(phi), -np.sin(phi)
    F1a = np.zeros((128, 256))
    F1b = np.zeros((128, 256))
    F1a[:KB] = np.concatenate([Fr, Fi], axis=1)
    F1b[:KB] = np.concatenate([-Fi, Fr], axis=1)
    dftc = np.stack([chd, F1a, F1b], axis=1).astype(np.float32)
    scale = 1.0 / np.sqrt(float(seq_T) * 128.0)
    a = np.arange(128, dtype=np.float64)
    th = 2 * np.pi * (a[:, None, None] * (kp[None, :, :] % N)) / N
    Gr = scale * np.cos(th)
    nGi = scale * np.sin(th)
    gtw = np.stack([Gr, nGi], axis=2).reshape(128, -1).astype(np.float32)
    return {
        "x_own": np.ascontiguousarray(x_own, np.float32), "x_oth": np.ascontiguousarray(x_oth, np.float32), "x_halo": x_halo,
        "w_in": W["w_in"], "w_a_out": W["w_a_out"], "w_b_out": W["w_b_out"], "w_out": W["w_out"],
        "w_ffn_in": W["w_ffn_in"], "w_ffn_out": W["w_ffn_out"], "w_go": np.ascontiguousarray(w_go, np.float32),
        "vecs": np.ascontiguousarray(vecs, np.float32), "convw": convw.astype(np.float32), "rep": rep, "cst": cst,
        "dftc": dftc, "gtw": gtw,
    }


def make_in_maps(NB, seqs, W):
    SEG = NB * 128
    maps, plan = [], []
    z2 = np.zeros((2, D), np.float32)
    for si, xs in enumerate(seqs):
        Tn = xs.shape[0]
        if Tn == SEG:
            halo8 = np.zeros((8, D), np.float32)
            maps.append(core_inputs(NB, dict(hf=0, prompt=False), xs, np.zeros((SEG, D), np.float32), halo8, W, Tn))
            plan.append((si, 0))
        else:
            assert Tn == 2 * SEG
            a, b = xs[:SEG], xs[SEG:]
            h0 = np.concatenate([z2, b[0:2], a[-2:], z2], axis=0)
            h1 = np.concatenate([a[-2:], z2, z2, b[0:2]], axis=0)
            maps.append(core_inputs(NB, dict(hf=0, prompt=True), a, b, h0, W, Tn))
            plan.append((si, 0))
            maps.append(core_inputs(NB, dict(hf=1, prompt=True), b, a, h1, W, Tn))
            plan.append((si, 1))
    return maps, plan


_W_KEYS = ["g_pre_mix", "w_in", "conv_w", "b_gates", "g_head", "w_a_out", "w_b_out", "b_merge", "w_out", "g_post_mix",
           "g_pre_ffn", "w_ffn_in", "w_ffn_out", "g_post_ffn"]


def run_seqs(NB, seqs, weights):
    W = {k: np.ascontiguousarray(np.asarray(weights[k], np.float32)[0]) for k in _W_KEYS}
    maps, plan = make_in_maps(NB, seqs, W)
    nc = build(NB)
    res = run_bass_kernel_spmd(nc, maps, core_ids=list(range(len(maps))))
    SEG = NB * 128
    outs = [np.zeros_like(np.asarray(s, np.float32)) for s in seqs]
    for (si, hf), r in zip(plan, res.results):
        outs[si][hf * SEG:(hf + 1) * SEG] = r["y_out"]
    if DEBUG_OUT:
        return outs, res.results
    return outs


def kernel(**inputs):
    xp = np.asarray(inputs["x_prompt"], np.float32)
    xs = np.asarray(inputs["x_sample"], np.float32)
    seqs = [xp[0], xp[1], xs[0], xs[1], xs[2], xs[3]]
    outs = run_seqs(64, seqs, inputs)
    y_prompt = np.stack(outs[0:2], axis=0)
    y_sample = np.stack(outs[2:6], axis=0)
    return (y_prompt, y_sample)
```

```python
import contextlib
import numpy as np
import concourse.bass as bass
import concourse.mybir as mybir
from concourse.bass_utils import run_bass_kernel_spmd

F32 = mybir.dt.float32
BF16 = mybir.dt.bfloat16
AF = mybir.ActivationFunctionType
ALU = mybir.AluOpType

D = 2048
NH = 8
DK = 128
DV = 256
W_QK = 1024
W_V = 2048
OFF_Q = 0
OFF_K = 1024
OFF_V = 2048
OFF_O = 4096
OFF_G = 6144
OFF_B = 6176
OFF_M = 7200
W_IN = 11296
D_FF = 5632
EPS = 1e-6
LNC = float(np.log(128.0 ** -0.5))

ENGS = ("pe", "act", "dve", "pool", "sp")


class T:
    __slots__ = ("name", "w", "r", "untracked")

    def __init__(self, name="", untracked=False):
        self.name = name
        self.w = None
        self.r = []
        self.untracked = untracked


class Op:
    __slots__ = ("eng", "emit", "waits", "signal", "sidx", "dma")

    def __init__(self, eng, emit):
        self.eng = eng
        self.emit = emit
        self.waits = []
        self.signal = False
        self.sidx = 0
        self.dma = None


class Prog:
    def __init__(self, nc):
        self.nc = nc
        self.ops = {e: [] for e in ENGS}
        self.dma_cnt = {}
        self.waited = {e: {} for e in ENGS}

    def _need(self, op, ev):
        if ev is None:
            return
        e = op.eng
        if ev[0] == "e":
            _, eng2, idx = ev
            if eng2 == "pe" and e == "pe":
                return
            key = ("e", eng2)
            if self.waited[e].get(key, -1) >= idx:
                return
            self.waited[e][key] = idx
            self.ops[eng2][idx].signal = True
            op.waits.append(ev)
        else:
            _, sk, val = ev
            key = ("d", sk)
            if self.waited[e].get(key, -1) >= val:
                return
            self.waited[e][key] = val
            op.waits.append(ev)

    def _track(self, op, ev, reads, writes):
        reads = [t for t in reads if not t.untracked]
        writes = [t for t in writes if not t.untracked]
        for t in reads:
            self._need(op, t.w)
        for t in writes:
            self._need(op, t.w)
            for r in t.r:
                self._need(op, r)
        for t in reads:
            t.r.append(ev)
            if len(t.r) > 64:
                t.r = t.r[-48:]
        for t in writes:
            t.w = ev
            t.r = []

    def op(self, eng, emit, reads=(), writes=()):
        o = Op(eng, emit)
        ev = ("e", eng, len(self.ops[eng]))
        self._track(o, ev, reads, writes)
        self.ops[eng].append(o)
        return ev

    def x(self, eng, name, reads=(), writes=(), args=(), **kw):
        return self.op(eng, lambda e, name=name, args=args, kw=kw: getattr(e, name)(*args, **kw), reads, writes)

    def dma(self, q, out, in_, reads=(), writes=(), sem="dma", **kw):
        o = Op(q, lambda e: e.dma_start(out=out, in_=in_, **kw))
        n = self.dma_cnt.get(sem, 0) + 1
        self.dma_cnt[sem] = n
        ev = ("d", sem, 16 * n)
        o.dma = (sem, 16 * n)
        if n > 1:
            self._need(o, ("d", sem, 16 * (n - 1)))
        self._track(o, ev, reads, writes)
        self.ops[q].append(o)
        return ev

    def barrier(self):
        evs = []
        for e in ENGS:
            for i in range(len(self.ops[e]) - 1, -1, -1):
                o = self.ops[e][i]
                if o.dma is None and o.emit is not None:
                    evs.append(("e", e, i))
                    break
        for sk, n in self.dma_cnt.items():
            evs.append(("d", sk, 16 * n))
        for e in ENGS:
            o = Op(e, None)
            for ev in evs:
                if ev[0] == "e" and ev[1] == e:
                    continue
                if ev[0] == "e" and ev[1] == "pe" and e == "pe":
                    continue
                self._need(o, ev)
            if o.waits:
                self.ops[e].append(o)

    def finish(self):
        o = Op("sp", None)
        for sk, n in self.dma_cnt.items():
            self._need(o, ("d", sk, 16 * n))
        self.ops["sp"].append(o)

    def emit(self):
        nc = self.nc
        for e in ENGS:
            c = 0
            for o in self.ops[e]:
                if o.signal:
                    c += 1
                o.sidx = c
        with contextlib.ExitStack() as st:
            esem = {e: st.enter_context(nc.semaphore("S_" + e)) for e in ENGS}
            dsem = {sk: st.enter_context(nc.semaphore("D_%s" % (sk,))) for sk in self.dma_cnt}
            block = st.enter_context(nc.Block())

            def run(ename):
                def body(eng):
                    for o in self.ops[ename]:
                        for ev in o.waits:
                            if ev[0] == "e":
                                eng.wait_ge(esem[ev[1]], self.ops[ev[1]][ev[2]].sidx)
                            else:
                                eng.wait_ge(dsem[ev[1]], ev[2])
                        if o.emit is None:
                            continue
                        ins = o.emit(eng)
                        if o.dma is not None:
                            ins.then_inc(dsem[o.dma[0]], 16)
                        elif o.signal:
                            ins.then_inc(esem[ename], 1)
                return body

            block.tensor(run("pe"))
            block.scalar(run("act"))
            block.vector(run("dve"))
            block.gpsimd(run("pool"))
            block.sync(run("sp"))


DEBUG_OUT = ()


def build(NB):
    SEG = NB * 128
    NT = NB // 4
    nc = bass.Bass("TRN2", target_bir_lowering=False)
    P = Prog(nc)

    def din(name, shape, dt=F32):
        return nc.dram_tensor(name, list(shape), dt, kind="ExternalInput").ap()

    def dscr(name, shape, dt):
        return nc.dram_tensor(name, list(shape), dt, kind=("ExternalOutput" if name in DEBUG_OUT else "Internal")).ap()

    x_own = din("x_own", [SEG, D])
    x_oth = din("x_oth", [SEG, D])
    x_halo = din("x_halo", [128, D])
    w_in = din("w_in", [D, W_IN])
    w_a_out = din("w_a_out", [W_V, D])
    w_b_out = din("w_b_out", [1024, D])
    w_out = din("w_out", [D, D])
    w_ffn_in = din("w_ffn_in", [D, 2 * D_FF])
    w_ffn_out = din("w_ffn_out", [D_FF, D])
    w_go = din("w_go", [D, 16])
    vecs = din("vecs", [128, 16 * 4 + 32])
    convw = din("convw", [128, 16, 5])
    rep = din("rep", [128, 2048 + 32 + 16 + 4])
    cst = din("cst", [128, 5, 128])
    dftc = din("dftc", [128, 3, 256])
    gtw = din("gtw", [128, 128 * 2 * NB])
    y_out = nc.dram_tensor("y_out", [SEG, D], F32, kind="ExternalOutput").ap()

    WB = {}

    def add_wb(key, src, c0, ncols):
        K = src.shape[0]
        KC = K // 128
        scr = dscr("wb_%s" % key, [128, KC * ncols], BF16)
        WB[key] = dict(src=src, c0=c0, n=ncols, KC=KC, scr=scr, t=T("wb_" + key))

    for i in range(22):
        add_wb("in%d" % i, w_in, 512 * i if i < 12 else 0, 512)
    for i in range(2):
        WB["in%d" % (12 + i)]["c0"] = OFF_B + 512 * i
    for i in range(8):
        WB["in%d" % (14 + i)]["c0"] = OFF_M + 512 * i
    add_wb("gate", w_in, OFF_G, 32)
    add_wb("go", w_go, 0, 16)
    for i in range(4):
        add_wb("ao%d" % i, w_a_out, 512 * i, 512)
        add_wb("bo%d" % i, w_b_out, 512 * i, 512)
        add_wb("wo%d" % i, w_out, 512 * i, 512)
    for i in range(22):
        add_wb("fi%d" % i, w_ffn_in, 512 * i, 512)
    for i in range(16):
        add_wb("fo%d" % i, w_ffn_out, 128 * i, 128)

    q_s = dscr("q_s", [NH, 128, SEG + 512], BF16)
    k_s = dscr("k_s", [NH, 128, SEG + 512], BF16)
    ko_s = dscr("ko_s", [NH, 128, SEG + 512], BF16)
    v_s = dscr("v_s", [SEG, W_V], BF16)
    vo_s = dscr("vo_s", [SEG, W_V], BF16)
    sgo_s = dscr("sgo_s", [SEG, W_V], F32)
    V_s = dscr("V_s", [16, 2 * SEG, 128], BF16)
    hf_s = dscr("hf_s", [SEG, W_V], F32)
    ha_s = dscr("ha_s", [SEG, W_V], BF16)
    Y_s = dscr("Y_s", [SEG, 1024], BF16)
    gtw_s = dscr("gtw_s", [128, 128 * 2 * NB], BF16)
    dft_s = dscr("dft_s", [128, 3 * 256], BF16)
    Tq_s, Tk_s, Tko_s, Tv_s, Tvo_s, Tsgo_s, TV_s, Thf_s, Tha_s, TY_s = [T(untracked=True) for _ in range(10)]
    Tgtw_s, Tdft_s = T(), T()

    st = contextlib.ExitStack()
    with st:
        def sb(name, shape, dt, stack=st):
            return stack.enter_context(nc.sbuf_tensor(name, list(shape), dt))

        psb = [st.enter_context(nc.psum_tensor("ps%d" % i, [128, 512], F32)) for i in range(8)]
        Tps = [T("ps%d" % i) for i in range(8)]
        ps_rr = [0]

        ps_live = [False] * 8
        ps_mod = [7]

        def ps_next():
            i = ps_rr[0] % ps_mod[0]
            ps_rr[0] = (i + 1) % ps_mod[0]
            assert (not ps_live[i]) or len(Tps[i].r) > 0, "PSUM bank %d re-allocated before its consumer was recorded" % i
            ps_live[i] = True
            return psb[i], Tps[i]

        cst_t = sb("cst_t", [128, 5, 128], F32)
        Tcst = T()
        vec_t = sb("vec_t", [128, 96], F32)
        Tvec = T()
        cw_t = sb("cw_t", [128, 16, 5], F32)
        Tcw = T()
        rep_t = sb("rep_t", [128, 2100], F32)
        Trep = T()
        cb_t = sb("cb_t", [128, 4], F32)
        Tcb = T()
        identb = sb("identb", [128, 128], BF16)
        onesb = sb("onesb", [128, 128], BF16)
        Tib = T()
        P.dma("sp", cst_t[:], cst, writes=[Tcst], sem="c0")
        P.dma("sp", vec_t[:], vecs, writes=[Tvec], sem="c0")
        P.dma("sp", cw_t[:], convw, writes=[Tcw], sem="c0")
        P.dma("sp", rep_t[:], rep, writes=[Trep], sem="c0")
        P.x("pool", "memset", writes=[Tcb], args=(cb_t[:, 0:1], EPS,))
        P.x("pool", "memset", reads=[Tcb], writes=[Tcb], args=(cb_t[:, 1:2], LNC,))
        P.x("dve", "tensor_copy", reads=[Tcst], writes=[Tib], out=identb[:], in_=cst_t[:, 4, :])
        P.x("dve", "tensor_copy", reads=[Tcst, Tib], writes=[Tib], out=onesb[:], in_=cst_t[:, 3, :])
        U_ap, L_ap, Mo_ap, ones_ap, ident_ap = [cst_t[:, i, :] for i in range(5)]
        g_pre_mix = vec_t[:, 0:16]
        g_post_mix = vec_t[:, 16:32]
        g_pre_ffn = vec_t[:, 32:48]
        g_post_ffn = vec_t[:, 48:64]
        b_merge = vec_t[:, 64:96]
        ghead_rep = rep_t[:, 0:2048]
        bgate_rep = rep_t[:, 2048:2080]
        bgo_rep = rep_t[:, 2080:2096]
        flags = rep_t[:, 2096:2100]
        eps_ap = cb_t[:, 0:1]
        lnc_ap = cb_t[:, 1:2]

        def cast_wb(key):
            w = WB[key]
            src = w["src"][:, w["c0"]:w["c0"] + w["n"]].rearrange("(kc p) n -> p kc n", p=128)
            dst = w["scr"].rearrange("p (kc n) -> p kc n", n=w["n"])
            P.dma("pool", dst, src, writes=[w["t"]], sem="cast")

        P.dma("pool", dft_s, dftc.rearrange("p a b -> p (a b)"), writes=[Tdft_s], sem="cast")
        order0 = (["gate", "go"] + ["in%d" % i for i in (0, 1, 2, 3, 4, 5, 6, 7, 12, 13)])
        for key in order0:
            cast_wb(key)
        cast_rest = (["in%d" % i for i in (8, 9, 10, 11)] + ["GTW"] + ["in%d" % i for i in range(14, 22)] + ["ao%d" % i for i in range(4)]
                     + ["bo%d" % i for i in range(4)] + ["wo%d" % i for i in range(4)] + ["fi%d" % i for i in range(22)]
                     + ["fo%d" % i for i in range(16)])
        cast_per_tile = -(-len(cast_rest) // max(1, (2 * NT - 1)))

        def cast_some(n):
            for _ in range(n):
                if not cast_rest:
                    return
                key = cast_rest.pop(0)
                if key == "GTW":
                    P.dma("pool", gtw_s, gtw, writes=[Tgtw_s], sem="cast")
                else:
                    cast_wb(key)

        class WStream:
            def __init__(self, nslots, stack, tag):
                self.n = nslots
                self.slots = [sb("wr%s%d" % (tag, i), [128, 8192], BF16, stack) for i in range(nslots)]
                self.T = [T() for _ in range(nslots)]
                self.tag = tag
                self.order = []
                self.issued = 0
                self.taken = 0

            def set_order(self, order):
                self.order = list(order)
                self.issued = 0
                self.taken = 0

            def _issue(self):
                key = self.order[self.issued]
                s = self.issued % self.n
                w = WB[key]
                P.dma("sp", self.slots[s][:, 0:w["KC"] * w["n"]], w["scr"], reads=[w["t"]], writes=[self.T[s]],
                      sem="wr%s%d" % (self.tag, s))
                self.issued += 1

            def get(self, key):
                assert self.order[self.taken] == key, (self.order[self.taken], key)
                while self.issued < len(self.order) and self.issued < self.taken + self.n:
                    self._issue()
                s = self.taken % self.n
                self.taken += 1
                w = WB[key]
                ap = self.slots[s][:, 0:w["KC"] * w["n"]].rearrange("p (kc n) -> p kc n", n=w["n"])
                return ap, self.T[s]

            def prefetch(self):
                while self.issued < len(self.order) and self.issued < self.taken + self.n:
                    self._issue()

        class Stats:
            def __init__(self, sq, Tsq):
                self.sq, self.Tsq = sq, Tsq
                self.n = 0
                self.pend = []

            def _flush(self):
                for f in self.pend:
                    f()
                self.pend = []

            def add(self, src, Tsrc, nchunk, col0, ntok, first, last):
                s2 = self.n % 2
                self.n += 1
                sqv = self.sq[s2][:, 0:nchunk * ntok].rearrange("p (n t) -> p n t", n=nchunk)
                P.x("act", "activation", reads=Tsrc, writes=[self.Tsq[s2]], out=sqv, in_=src, func=AF.Square)
                self._flush()

                def mm(s2=s2, sqv=sqv):
                    for i in range(nchunk):
                        P.x("pe", "matmul", reads=[self.Tsq[s2], Tib], writes=[Tps[7]], out=psb[7][:, col0:col0 + ntok], lhsT=onesb[:], rhs=sqv[:, i, :],
                            start=(first and i == 0), stop=(last and i == nchunk - 1))
                self.pend.append(mm)

            def finish(self, rstd, Trstd, ntok=512):
                self._flush()
                P.x("act", "activation", reads=[Tps[7], Tcb], writes=[Trstd], out=rstd[:, 0:ntok], in_=psb[7][:, 0:ntok], func=AF.Sqrt, scale=1.0 / D, bias=eps_ap)
                P.x("dve", "reciprocal", reads=[Trstd], writes=[Trstd], out=rstd[:, 0:ntok], in_=rstd[:, 0:ntok])

        def load_xT(xsrc, tok0, x_tm, Tx_tm, xT, TxT, nblk=4, stats=None):
            for blk in range(nblk):
                b2 = blk % len(x_tm)
                P.dma("sp", x_tm[b2][:], xsrc[tok0 + blk * 128: tok0 + (blk + 1) * 128, :], writes=[Tx_tm[b2]],
                      sem="xtm%d" % b2)
                for cg in range(4):
                    ps, tp = ps_next()
                    for i in range(4):
                        c = cg * 4 + i
                        P.x("pe", "transpose", reads=[Tx_tm[b2], Tcst], writes=[tp],
                            args=(ps[:, i * 128:(i + 1) * 128], x_tm[b2][:, c * 128:(c + 1) * 128], ident_ap,))
                    dst = xT[:, cg * 4:cg * 4 + 4, blk * 128:(blk + 1) * 128]
                    P.x("act", "copy", reads=[tp], writes=[TxT[cg * 4 + i] for i in range(4)], out=dst, in_=ps[:].rearrange("p (i t) -> p i t", i=4))
                    if stats is not None:
                        stats.add(dst, [TxT[cg * 4 + i] for i in range(4)], 4, blk * 128, 128, cg == 0, cg == 3)

        def fm_apply(src, Tsrc, gcol, dst, Tdst, rstd, Trstd, ntok=512):
            for c in range(16):
                P.x("dve", "scalar_tensor_tensor", reads=[Tsrc[c], Trstd, Tvec], writes=[Tdst[c]], out=dst[:, c, 0:ntok], in0=src[:, c, 0:ntok],
                    scalar=gcol[:, c:c + 1], in1=rstd[:, 0:ntok], op0=ALU.mult, op1=ALU.mult)

        def mm_ii(ps, tp, wap, wT, col0, act, Tact, KC, ntok=512, m=128):
            for k in range(KC):
                def mm(e, k=k):
                    return e.matmul(ps[0:m, 0:ntok], lhsT=wap[:, k, col0:col0 + m], rhs=act[:, k, 0:ntok],
                                    start=(k == 0), stop=(k == KC - 1))
                P.op("pe", mm, reads=[wT, Tact[k]], writes=[tp])

        def mm_i(ps, tp, wap, wT, c0, n, act, Tact, blk, KC):
            for k in range(KC):
                def mm(e, k=k):
                    return e.matmul(ps[:, 0:n], lhsT=act[:, k, blk * 128:(blk + 1) * 128], rhs=wap[:, k, c0:c0 + n],
                                    start=(k == 0), stop=(k == KC - 1))
                P.op("pe", mm, reads=[wT, Tact[k]], writes=[tp])

        st12 = contextlib.ExitStack()
        st12.__enter__()
        gs = sb("gs", [128, NB, 64], F32, st12)
        Tgs = [T() for _ in range(NB)]
        gso = sb("gso", [128, NB, 16], F32, st12)
        Tgso = [T() for _ in range(NB)]
        TC = [T() for _ in range(NH)]
        TCb = [T() for _ in range(NH)]
        Tn = T()
        Tnb = T()
        TCo = T()
        halo_raw = sb("halo_raw", [128, 16, 8], F32, st12)
        Thalo = T()
        pfx = sb("pfx", [128, 8], F32, st12)
        Tpfx = T()

        st1 = contextlib.ExitStack()
        st1.__enter__()
        x_tm = [sb("x_tm%d" % i, [128, D], F32, st1) for i in range(2)]
        Tx_tm = [T() for _ in range(2)]
        xT = sb("xT", [128, 16, 512], F32, st1)
        TxT = [T() for _ in range(16)]
        hT = sb("hT", [128, 16, 512], BF16, st1)
        ThT = [T() for _ in range(16)]
        sq = [sb("sq%d" % i, [128, 512], BF16, st1) for i in range(2)]
        Tsq = [T() for _ in range(2)]
        rstd = sb("rstd", [128, 512], F32, st1)
        Trstd = T()
        wg_t = sb("wg_t", [128, 16, 32], BF16, st1)
        wgo_t = sb("wgo_t", [128, 16, 16], BF16, st1)
        Twg = T()
        dft_t = sb("dft_t", [128, 3, 256], BF16, st1)
        Tdft = T()
        raw = [sb("raw%d" % i, [128, 516], F32, st1) for i in range(2)]
        Traw = [T() for _ in range(2)]
        acc = [sb("acc%d" % i, [128, 512], F32, st1) for i in range(2)]
        Tacc = [T() for _ in range(2)]
        qko = [sb("qko%d" % i, [128, 512], BF16, st1) for i in range(2)]
        Tqko = [T() for _ in range(2)]
        carry = sb("carry", [128, 16, 4], F32, st1)
        Tcarry = [T() for _ in range(16)]
        vst = [sb("vst%d" % i, [128, 4, 512], BF16, st1) for i in range(2)]
        Tvst = [T() for _ in range(2)]
        sgst = sb("sgst", [128, 4, 512], F32, st1)
        Tsgst = T()
        uT = sb("uT", [128, 8, 512], BF16, st1)
        TuT = [T() for _ in range(8)]
        Vst = sb("Vst", [128, 4, 16 * 128], BF16, st1)
        TVst = [T() for _ in range(4)]
        g32 = sb("g32", [128, 32], F32, st1)
        sp16 = sb("sp16", [128, 16], F32, st1)
        t16 = sb("t16", [128, 16], F32, st1)
        t16b = sb("t16b", [128, 16], F32, st1)
        Tg32, Tsp16, Tt16, Tt16b = T(), T(), T(), T()
        c32 = sb("c32", [128, 32], F32, st1)
        Tc32 = T()
        ws1 = WStream(3, st1, "a")

        P.dma("sp", wg_t[:], WB["gate"]["scr"].rearrange("p (k n) -> p k n", n=32), reads=[WB["gate"]["t"]], writes=[Twg], sem="c1")
        P.dma("sp", wgo_t[:], WB["go"]["scr"].rearrange("p (k n) -> p k n", n=16), reads=[WB["go"]["t"]], writes=[Twg], sem="c1")
        P.dma("sp", dft_t[:], dft_s.rearrange("p (a b) -> p a b", a=3), reads=[Tdft_s], writes=[Tdft], sem="c1")
        P.x("pool", "memset", writes=[Tpfx], args=(pfx[:], 0.0,))
        for i in range(2):
            P.x("pool", "memset", writes=[Traw[i]], args=(raw[i][:], 0.0,))

        def conv_emit(ridx, ch, ncol, dst_dram, dcol0, Tdst):
            r = raw[ridx]
            a = acc[ridx]
            o = qko[ridx]
            P.x("dve", "tensor_scalar", reads=[Traw[ridx], Tcw], writes=[Tacc[ridx]], out=a[:, 0:ncol], in0=r[:, 0:ncol], scalar1=cw_t[:, ch, 0:1], scalar2=None,
                                                 op0=ALU.mult)
            for j in range(1, 5):
                P.x("dve", "scalar_tensor_tensor", reads=[Traw[ridx], Tacc[ridx], Tcw], writes=[Tacc[ridx]], out=a[:, 0:ncol], in0=r[:, j:j + ncol], scalar=cw_t[:, ch, j:j + 1],
                                                                 in1=a[:, 0:ncol], op0=ALU.mult, op1=ALU.add)
            P.x("act", "activation", reads=[Tacc[ridx]], writes=[Tqko[ridx]], out=o[:, 0:ncol], in_=a[:, 0:ncol], func=AF.Silu)
            P.dma("pool", dst_dram[:, dcol0:dcol0 + ncol], o[:, 0:ncol], reads=[Tqko[ridx]], writes=[Tdst], sem="qko%d" % ridx)

        ws1.set_order(["in0", "in1", "in2", "in3"])
        stats1 = Stats(sq, Tsq)
        load_xT(x_halo, 0, x_tm, Tx_tm, xT, TxT, nblk=1, stats=stats1)
        stats1.finish(rstd, Trstd, ntok=128)
        fm_apply(xT, TxT, g_pre_mix, hT, ThT, rstd, Trstd, ntok=128)
        for wb in range(4):
            wap, wT = ws1.get("in%d" % wb)
            for hh in range(4):
                ch = wb * 4 + hh
                ps, tp = ps_next()
                mm_ii(ps, tp, wap, wT, hh * 128, hT, ThT, 16, ntok=128)
                P.x("act", "copy", reads=[tp], writes=[Thalo], out=halo_raw[:, ch, :], in_=ps[:, 0:8])

        def phase1_segment(own):
            xsrc = x_own if own else x_oth
            wkeys = (["in%d" % i for i in range(14)] if own else ["in2", "in3", "in4", "in5", "in6", "in7", "in12", "in13"])
            ws1.set_order(wkeys * NT)
            ks = k_s if own else ko_s
            Tks = Tk_s if own else Tko_s
            vs = v_s if own else vo_s
            Tvs = Tv_s if own else Tvo_s
            lo = 0 if own else 4
            for ch in range(16):
                if (not own) and ch < 8:
                    continue
                P.x("pool", "memset", reads=[], writes=[Tcarry[ch]], args=(carry[:, ch, 0:2], 0.0,))
                P.x("pool", "tensor_copy", reads=[Thalo, Tcarry[ch]], writes=[Tcarry[ch]], out=carry[:, ch, 2:4], in_=halo_raw[:, ch, lo:lo + 2])
            rr = [0]
            for it in range(NT):
                tok0 = it * 512
                load_xT(xsrc, tok0, x_tm, Tx_tm, xT, TxT, stats=stats1)
                stats1.finish(rstd, Trstd)
                fm_apply(xT, TxT, g_pre_mix, hT, ThT, rstd, Trstd)
                for wb in ((0, 1, 2, 3) if own else (2, 3)):
                    wap, wT = ws1.get("in%d" % wb)
                    for hh in range(4):
                        ch = wb * 4 + hh
                        ps, tp = ps_next()
                        mm_ii(ps, tp, wap, wT, hh * 128, hT, ThT, 16)
                        ri = rr[0] % 2
                        rr[0] += 1
                        r = raw[ri]
                        P.x("dve", "tensor_copy", reads=[Tcarry[ch]], writes=[Traw[ri]], out=r[:, 0:4], in_=carry[:, ch, :])
                        P.x("act", "copy", reads=[tp, Traw[ri]], writes=[Traw[ri]], out=r[:, 4:516], in_=ps[:])
                        P.x("dve", "tensor_copy", reads=[Traw[ri]], writes=[Tcarry[ch]], out=carry[:, ch, :], in_=r[:, 512:516])
                        dst = (q_s if ch < 8 else ks)[ch % 8]
                        conv_emit(ri, ch, 512, dst, tok0, Tq_s if ch < 8 else Tks)
                for cb in range(4):
                    wap, wT = ws1.get("in%d" % (4 + cb))
                    v2 = cb % 2
                    for blk in range(4):
                        ps, tp = ps_next()
                        mm_i(ps, tp, wap, wT, 0, 512, hT, ThT, blk, 16)
                        P.x("act", "copy", reads=[tp], writes=[Tvst[v2]], out=vst[v2][:, blk, :], in_=ps[:])
                    P.dma("pool", vs[tok0:tok0 + 512, cb * 512:(cb + 1) * 512].rearrange("(b p) f -> p b f", p=128), vst[v2][:],
                          reads=[Tvst[v2]], writes=[Tvs], sem="vst%d" % v2)
                if own:
                    for cb in range(4):
                        wap, wT = ws1.get("in%d" % (8 + cb))
                        for blk in range(4):
                            ps, tp = ps_next()
                            mm_i(ps, tp, wap, wT, 0, 512, hT, ThT, blk, 16)
                            P.x("act", "activation", reads=[tp], writes=[Tsgst], out=sgst[:, blk, :], in_=ps[:], func=AF.Sigmoid)
                            P.x("dve", "tensor_tensor", reads=[Tsgst, Trep], writes=[Tsgst], out=sgst[:, blk, :], in0=sgst[:, blk, :],
                                in1=ghead_rep[:, cb * 512:(cb + 1) * 512], op=ALU.mult)
                        P.dma("pool", sgo_s[tok0:tok0 + 512, cb * 512:(cb + 1) * 512].rearrange("(b p) f -> p b f", p=128), sgst[:],
                              reads=[Tsgst], writes=[Tsgo_s], sem="sgst")
                for ub in range(2):
                    wap, wT = ws1.get("in%d" % (12 + ub))
                    for gg in range(4):
                        g = ub * 4 + gg
                        ps, tp = ps_next()
                        mm_ii(ps, tp, wap, wT, gg * 128, hT, ThT, 16)
                        P.x("act", "copy", reads=[tp], writes=[TuT[g]], out=uT[:, g, :], in_=ps[:])
                        for blk in range(4):
                            ps2, tp2 = ps_next()
                            P.x("pe", "matmul", reads=[TuT[g], Tdft], writes=[tp2], out=ps2[:, 0:256], lhsT=uT[:, g, blk * 128:(blk + 1) * 128],
                                                                               rhs=dft_t[:, 0, :], start=True, stop=True)
                            P.x("dve", "tensor_copy", reads=[tp2], writes=[TVst[blk]],
                                out=Vst[:, blk, g * 256:(g + 1) * 256].rearrange("p (h r c) -> p h r c", h=2, r=2),
                                in_=ps2[:, 0:256].rearrange("p (r h c) -> p h r c", r=2, h=2))
                tv0 = tok0 + (0 if own else SEG)
                for blk in range(4):
                    P.dma("pool", V_s[:, tv0 + blk * 128: tv0 + (blk + 1) * 128, :].rearrange("g p x -> p g x"),
                          Vst[:, blk, :].rearrange("p (g x) -> p g x", x=128), reads=[TVst[blk]], writes=[TV_s], sem="Vst")
                for blk in range(4):
                    c = it * 4 + blk
                    ps, tp = ps_next()
                    if own:
                        for k in range(16):
                            P.x("pe", "matmul", reads=[ThT[k], Twg], writes=[tp], out=ps[:, 0:32], lhsT=hT[:, k, blk * 128:(blk + 1) * 128],
                                                                             rhs=wg_t[:, k, :], start=(k == 0), stop=(k == 15))
                        P.x("dve", "tensor_tensor", reads=[tp, Trep], writes=[Tg32], out=g32[:], in0=ps[:, 0:32], in1=bgate_rep, op=ALU.add)
                        gv = g32[:].rearrange("p (d j h) -> p d j h", d=2, j=2)
                        P.x("act", "activation", reads=[Tg32], writes=[Tsp16], out=sp16[:].rearrange("p (d h) -> p d h", d=2), in_=gv[:, :, 1, :],
                                                                func=AF.Exp, scale=-1.0)
                        P.x("act", "activation", reads=[Tsp16], writes=[Tsp16], out=sp16[:], in_=sp16[:], func=AF.Ln, bias=1.0)
                        ps2, tp2 = ps_next()
                        P.x("pe", "matmul", reads=[Tsp16, Tcst], writes=[tp2], out=ps2[:, 0:8], lhsT=U_ap, rhs=sp16[:, 0:8], start=True, stop=True)
                        P.x("pe", "matmul", reads=[Tsp16, Tcst], writes=[tp2], out=ps2[:, 8:16], lhsT=L_ap, rhs=sp16[:, 8:16], start=True, stop=True)
                        P.x("pe", "matmul", reads=[Tsp16, Tcst], writes=[tp2], out=ps2[:, 16:32], lhsT=ones_ap, rhs=sp16[:, 0:16], start=True, stop=True)
                        P.x("dve", "tensor_copy", reads=[tp2], writes=[Tc32], out=c32[:], in_=ps2[:, 0:32])
                        P.x("dve", "tensor_tensor", reads=[Tg32, Tc32], writes=[Tt16], out=t16[:].rearrange("p (d h) -> p d h", d=2), in0=gv[:, :, 0, :],
                                                                            in1=c32[:, 0:16].rearrange("p (d h) -> p d h", d=2), op=ALU.add)
                        P.x("act", "activation", reads=[Tt16, Tcb], writes=[Tgs[c]], out=gs[:, c, 0:16], in_=t16[:], func=AF.Exp, bias=lnc_ap)
                        P.x("dve", "tensor_tensor", reads=[Tt16, Tc32], writes=[Tt16b], out=t16b[:], in0=t16[:], in1=c32[:, 16:32], op=ALU.subtract)
                        P.x("act", "activation", reads=[Tt16b, Tcb], writes=[Tgs[c]], out=gs[:, c, 16:32], in_=t16b[:], func=AF.Exp, bias=lnc_ap)
                        P.x("act", "activation", reads=[Tc32], writes=[Tgs[c]], out=gs[:, c, 32:48], in_=c32[:, 0:16], func=AF.Exp)
                        P.x("act", "activation", reads=[Tc32], writes=[Tgs[c]], out=gs[:, c, 48:64], in_=c32[:, 16:32], func=AF.Exp, scale=-1.0)
                    else:
                        for k in range(16):
                            P.x("pe", "matmul", reads=[ThT[k], Twg], writes=[tp], out=ps[:, 0:16], lhsT=hT[:, k, blk * 128:(blk + 1) * 128],
                                                                             rhs=wgo_t[:, k, :], start=(k == 0), stop=(k == 15))
                        P.x("dve", "tensor_tensor", reads=[tp, Trep], writes=[Tg32], out=g32[:, 0:16], in0=ps[:, 0:16], in1=bgo_rep, op=ALU.add)
                        P.x("act", "activation", reads=[Tg32], writes=[Tsp16], out=sp16[:, 0:8], in_=g32[:, 8:16], func=AF.Exp, scale=-1.0)
                        P.x("act", "activation", reads=[Tsp16], writes=[Tsp16], out=sp16[:, 0:8], in_=sp16[:, 0:8], func=AF.Ln, bias=1.0)
                        ps2, tp2 = ps_next()
                        P.x("pe", "matmul", reads=[Tsp16, Tcst], writes=[tp2], out=ps2[:, 0:8], lhsT=Mo_ap, rhs=sp16[:, 0:8], start=True, stop=True)
                        P.x("pe", "matmul", reads=[Tsp16, Tcst], writes=[tp2], out=ps2[:, 8:16], lhsT=ones_ap, rhs=sp16[:, 0:8], start=True, stop=True)
                        P.x("dve", "tensor_copy", reads=[tp2], writes=[Tc32], out=c32[:, 0:16], in_=ps2[:, 0:16])
                        P.x("dve", "tensor_tensor", reads=[Tg32, Tc32], writes=[Tt16], out=t16[:, 0:8], in0=g32[:, 0:8], in1=c32[:, 0:8], op=ALU.subtract)
                        P.x("act", "activation", reads=[Tt16, Tcb], writes=[Tt16], out=t16[:, 0:8], in_=t16[:, 0:8], func=AF.Exp, bias=lnc_ap)
                        P.x("act", "activation", reads=[Tpfx, Trep], writes=[Tt16b], out=t16b[:, 0:8], in_=pfx[:], func=AF.Exp, scale=flags[:, 3:4])
                        P.x("dve", "tensor_tensor", reads=[Tt16, Tt16b], writes=[Tgso[c]], out=gso[:, c, 0:8], in0=t16[:, 0:8], in1=t16b[:, 0:8], op=ALU.mult)
                        P.x("act", "activation", reads=[Tc32, Trep], writes=[Tgso[c]], out=gso[:, c, 8:16], in_=c32[:, 8:16], func=AF.Exp, scale=flags[:, 2:3])
                        P.x("dve", "tensor_tensor", reads=[Tpfx, Tc32], writes=[Tpfx], out=pfx[:], in0=pfx[:], in1=c32[:, 8:16], op=ALU.add)
                cast_some(cast_per_tile)
            for ch in range(16):
                if (not own) and ch < 8:
                    continue
                ri = rr[0] % 2
                rr[0] += 1
                r = raw[ri]
                P.x("dve", "tensor_copy", reads=[Tcarry[ch]], writes=[Traw[ri]], out=r[:, 0:4], in_=carry[:, ch, :])
                P.x("dve", "tensor_copy", reads=[Thalo, Traw[ri]], writes=[Traw[ri]], out=r[:, 4:6], in_=halo_raw[:, ch, lo + 2:lo + 4])
                dst = (q_s if ch < 8 else ks)[ch % 8]
                conv_emit(ri, ch, 2, dst, SEG, Tq_s if ch < 8 else Tks)

        phase1_segment(False)
        phase1_segment(True)
        cast_some(len(cast_rest))
        P.barrier()
        st1.__exit__(None, None, None)

        st2 = contextlib.ExitStack()
        st2.__enter__()
        Cst = sb("Cst", [128, NH, 256], F32, st2)
        nst = sb("nst", [128, NH], F32, st2)
        Cbf = sb("Cbf", [128, NH, 256], BF16, st2)
        nbf = sb("nbf", [128, NH], BF16, st2)
        Coth = sb("Coth", [128, NH, 256], F32, st2)
        noth = sb("noth", [128, NH], F32, st2)
        qsc = [sb("qsc%d" % i, [128, NH, 512], BF16, st2) for i in range(2)]
        ksc = [sb("ksc%d" % i, [128, NH, 512], BF16, st2) for i in range(2)]
        vsc = [sb("vsc%d" % i, [128, 4, W_V], BF16, st2) for i in range(2)]
        Tqsc = [T() for _ in range(2)]
        Tksc = [T() for _ in range(2)]
        Tvsc = [T() for _ in range(2)]
        Pm = sb("Pm", [128, NH, 128], BF16, st2)
        TPm = [T() for _ in range(NH)]
        kt = sb("kt", [128, NH, 128], BF16, st2)
        Tkt = [T() for _ in range(NH)]
        hout = [sb("hout%d" % i, [128, W_V], F32, st2) for i in range(3)]
        Thout = [[T(), T()] for _ in range(3)]
        hfl = [sb("hfl%d" % i, [128, W_V], F32, st2) for i in range(3)]
        Thfl = [T() for _ in range(3)]
        sgl = [sb("sgl%d" % i, [128, W_V], F32, st2) for i in range(3)]
        Tsgl = [T() for _ in range(3)]
        hsq = sb("hsq", [128, 256], F32, st2)
        Thsq = T()
        hab = [sb("hab%d" % i, [128, W_V], BF16, st2) for i in range(2)]
        Thab = [T() for _ in range(2)]
        rr8 = sb("rr8", [128, 8], F32, st2)
        Trr8 = T()
        ss8 = sb("ss8", [128, 8], F32, st2)
        Tss8 = T()
        tmpn = sb("tmpn", [128, 8], F32, st2)
        Ttmpn = T()

        def sweep(kind):
            own = kind != "oth"
            order = list(range(NB)) if kind != "bwd" else list(range(NB - 1, -1, -1))
            ksrc, Tksrc = (k_s, Tk_s) if own else (ko_s, Tko_s)
            vsrc, Tvsrc = (v_s, Tv_s) if own else (vo_s, Tvo_s)
            mask = U_ap if kind == "fwd" else L_ap
            if kind == "oth":
                for h in range(NH):
                    P.x("pool", "memset", writes=[TC[h]], args=(Cst[:, h, :], 0.0,))
                    P.x("pool", "memset", writes=[TCb[h]], args=(Cbf[:, h, :], 0.0,))
                P.x("pool", "memset", writes=[Tn], args=(nst[:], 0.0,))
                P.x("pool", "memset", writes=[Tnb], args=(nbf[:], 0.0,))
            else:
                fcol = flags[:, 0:1] if kind == "fwd" else flags[:, 1:2]
                for h in range(NH):
                    P.x("dve", "tensor_scalar", reads=[TCo, Trep], writes=[TC[h]], out=Cst[:, h, :], in0=Coth[:, h, :], scalar1=fcol, scalar2=None, op0=ALU.mult)
                    P.x("act", "copy", reads=[TC[h]], writes=[TCb[h]], out=Cbf[:, h, :], in_=Cst[:, h, :])
                P.x("dve", "tensor_scalar", reads=[TCo, Trep], writes=[Tn], out=nst[:], in0=noth[:], scalar1=fcol, scalar2=None, op0=ALU.mult)
                P.x("act", "copy", reads=[Tn], writes=[Tnb], out=nbf[:], in_=nst[:])
            if own:
                a_i, ap_i, ei_i, eb_i = (0, 16, 32, 48) if kind == "fwd" else (8, 24, 40, 56)
            else:
                ap_i, eb_i = 0, 8

            def load_sc(sci):
                sc = order[sci * 4] // 4
                b = sci % 2
                t0 = sc * 512
                P.dma("sp", ksc[b][:], ksrc[:, :, t0 + 2:t0 + 514].rearrange("h p t -> p h t"), reads=[Tksrc], writes=[Tksc[b]], sem="ksc%d" % b)
                P.dma("sp", vsc[b][:], vsrc[t0:t0 + 512, :].rearrange("(b p) f -> p b f", p=128), reads=[Tvsrc], writes=[Tvsc[b]], sem="vsc%d" % b)
                if own:
                    P.dma("sp", qsc[b][:], q_s[:, :, t0 + 2:t0 + 514].rearrange("h p t -> p h t"), reads=[Tq_s], writes=[Tqsc[b]], sem="qsc%d" % b)

            def prefetch_epi(ci):
                c = order[ci]
                trow = slice(c * 128, (c + 1) * 128)
                P.dma("sp", hfl[ci % 3][:], hf_s[trow, :], reads=[Thf_s], writes=[Thfl[ci % 3]], sem="hfl%d" % (ci % 3))
                P.dma("sp", sgl[ci % 3][:], sgo_s[trow, :], reads=[Tsgo_s], writes=[Tsgl[ci % 3]], sem="sgl%d" % (ci % 3))

            def epilogue(ci):
                c = order[ci]
                ho = hout[ci % 3]
                Tho = Thout[ci % 3]
                trow = slice(c * 128, (c + 1) * 128)
                if kind == "fwd":
                    P.dma("pool", hf_s[trow, :], ho[:], reads=Tho, writes=[Thf_s], sem="hout%d" % (ci % 3))
                elif kind == "bwd":
                    hf_, Thf_ = hfl[ci % 3], Thfl[ci % 3]
                    sg_, Tsg_ = sgl[ci % 3], Tsgl[ci % 3]
                    for h in range(NH):
                        P.x("act", "activation", reads=Tho + [Tss8], writes=[Thsq, Tss8], out=hsq[:], in_=ho[:, h * 256:(h + 1) * 256], func=AF.Square, accum_out=ss8[:, h:h + 1])
                    P.x("act", "activation", reads=[Tss8, Tcb], writes=[Tss8], out=ss8[:], in_=ss8[:], func=AF.Sqrt, scale=1.0 / DV, bias=eps_ap)
                    P.x("dve", "reciprocal", reads=[Tss8], writes=[Tss8], out=ss8[:], in_=ss8[:])
                    hb = hab[ci % 2]
                    Thb = Thab[ci % 2]
                    for h in range(NH):
                        P.x("dve", "scalar_tensor_tensor", reads=Tho + [Tss8, Tsg_], writes=[Thb], out=hb[:, h * 256:(h + 1) * 256], in0=ho[:, h * 256:(h + 1) * 256],
                            scalar=ss8[:, h:h + 1], in1=sg_[:, h * 256:(h + 1) * 256], op0=ALU.mult, op1=ALU.mult)
                    P.dma("pool", ha_s[trow, :], hb[:], reads=[Thb], writes=[Tha_s], sem="hab%d" % (ci % 2))

            nsc_tot = NB // 4
            pending = [None]
            load_sc(0)
            for ci, c in enumerate(order):
                sci = ci // 4
                if ci % 4 == 0 and sci + 1 < nsc_tot:
                    load_sc(sci + 1)
                if kind == "bwd":
                    prefetch_epi(ci)
                b = sci % 2
                cc = c % 4
                tsl = slice(cc * 128, (cc + 1) * 128)
                gsrc, Tg = (gs, Tgs[c]) if own else (gso, Tgso[c])
                ho = hout[ci % 3]
                Tho = Thout[ci % 3]
                psS = [ps_next() for _ in range(2)] if own else None
                if own:
                    for h in range(NH):
                        pS, tS = psS[h // 4]
                        hh = h % 4
                        P.x("pe", "matmul", reads=[Tksc[b], Tqsc[b]], writes=[tS], out=pS[:, hh * 128:(hh + 1) * 128], lhsT=ksc[b][:, h, tsl],
                            rhs=qsc[b][:, h, tsl], start=True, stop=True)
                psK, tK = ps_next()
                psKb = psK[:].bitcast(BF16)
                for h in range(NH):
                    P.x("pe", "transpose", reads=[Tksc[b], Tib], writes=[tK], args=(psKb[:, h * 128:(h + 1) * 128], ksc[b][:, h, tsl], identb[:],))
                if own:
                    for h in range(NH):
                        pS, tS = psS[h // 4]
                        hh = h % 4
                        P.x("dve", "scalar_tensor_tensor", reads=[tS, Tg, Tcst], writes=[TPm[h]], out=Pm[:, h, :], in0=pS[:, hh * 128:(hh + 1) * 128],
                            scalar=gsrc[:, c, a_i + h:a_i + h + 1], in1=mask, op0=ALU.mult, op1=ALU.mult)
                for h in range(NH):
                    P.x("act", "activation", reads=[tK, Tg], writes=[Tkt[h]], out=kt[:, h, :], in_=psKb[:, h * 128:(h + 1) * 128], func=AF.Copy,
                        scale=gsrc[:, c, ap_i + h:ap_i + h + 1])
                psC = [ps_next() for _ in range(4)]
                psn, tn_ = ps_next()
                for h in range(NH):
                    pC, tC = psC[h // 2]
                    cs = slice((h % 2) * 256, (h % 2) * 256 + 256)
                    P.x("pe", "matmul", reads=[Tkt[h], Tvsc[b]], writes=[tC], out=pC[:, cs], lhsT=kt[:, h, :], rhs=vsc[b][:, cc, h * 256:(h + 1) * 256],
                        start=True, stop=True)
                    P.x("pe", "matmul", reads=[Tkt[h], Tib], writes=[tn_], out=psn[:, 8 + h:9 + h], lhsT=kt[:, h, :], rhs=onesb[:, 0:1], start=True, stop=True)
                for h in range(NH):
                    pC, tC = psC[h // 2]
                    cs = slice((h % 2) * 256, (h % 2) * 256 + 256)
                    P.x("dve", "scalar_tensor_tensor", reads=[TC[h], Tg, tC], writes=[TC[h]], out=Cst[:, h, :], in0=Cst[:, h, :],
                        scalar=gsrc[:, c, eb_i + h:eb_i + h + 1], in1=pC[:, cs], op0=ALU.mult, op1=ALU.add)
                P.x("dve", "tensor_tensor", reads=[Tn, Tg], writes=[Ttmpn], out=tmpn[:], in0=nst[:], in1=gsrc[:, c, eb_i:eb_i + 8], op=ALU.mult)
                P.x("dve", "tensor_tensor", reads=[Ttmpn, tn_, Tn], writes=[Tn], out=nst[:], in0=tmpn[:], in1=psn[:, 8:16], op=ALU.add)
                if own:
                    psR = [ps_next() for _ in range(4)]
                    psr, tr_ = ps_next()
                    for h in range(NH):
                        pR, tR = psR[h // 2]
                        cs = slice((h % 2) * 256, (h % 2) * 256 + 256)
                        P.x("pe", "matmul", reads=[TPm[h], Tvsc[b]], writes=[tR], out=pR[:, cs], lhsT=Pm[:, h, :], rhs=vsc[b][:, cc, h * 256:(h + 1) * 256],
                            start=True, stop=False)
                        P.x("pe", "matmul", reads=[Tqsc[b], TCb[h]], writes=[tR], out=pR[:, cs], lhsT=qsc[b][:, h, tsl], rhs=Cbf[:, h, :], start=False, stop=True)
                        P.x("pe", "matmul", reads=[TPm[h], Tib], writes=[tr_], out=psr[:, h:h + 1], lhsT=Pm[:, h, :], rhs=onesb[:, 0:1], start=True, stop=False)
                        P.x("pe", "matmul", reads=[Tqsc[b], Tnb], writes=[tr_], out=psr[:, h:h + 1], lhsT=qsc[b][:, h, tsl], rhs=nbf[:, h:h + 1], start=False, stop=True)
                if own:
                    cast_eng = (["dve", "dve", "pool", "dve", "dve", "dve", "pool", "dve"] if kind == "fwd"
                                else ["pool", "act", "pool", "act", "pool", "act", "pool", "act"])
                    for h in range(NH):
                        if cast_eng[h] == "act":
                            P.x("act", "copy", reads=[TC[h]], writes=[TCb[h]], out=Cbf[:, h, :], in_=Cst[:, h, :])
                        else:
                            P.x(cast_eng[h], "tensor_copy", reads=[TC[h]], writes=[TCb[h]], out=Cbf[:, h, :], in_=Cst[:, h, :])
                    P.x("dve", "tensor_copy", reads=[Tn, Tnb], writes=[Tnb], out=nbf[:], in_=nst[:])
                def r_evac(c=c, ci=ci, ho=ho, Tho=Tho, gsrc=gsrc, Tg=Tg, psR=(psR if own else None), psr=(psr if own else None), tr_=(tr_ if own else None)):
                    if own:
                        P.x("act", "activation", reads=[tr_], writes=[Trr8], out=rr8[:], in_=psr[:, 0:8], func=AF.Abs)
                        P.x("dve", "tensor_tensor", reads=[Trr8, Tg], writes=[Trr8], out=rr8[:], in0=rr8[:], in1=gsrc[:, c, ei_i:ei_i + 8], op=ALU.max)
                        P.x("dve", "reciprocal", reads=[Trr8], writes=[Trr8], out=rr8[:], in_=rr8[:])
                        for h in range(NH):
                            pR, tR = psR[h // 2]
                            cs = slice((h % 2) * 256, (h % 2) * 256 + 256)
                            if kind == "bwd":
                                P.x("dve", "scalar_tensor_tensor", reads=[tR, Trr8, Thfl[ci % 3]], writes=[Tho[h // 4]], out=ho[:, h * 256:(h + 1) * 256],
                                    in0=pR[:, cs], scalar=rr8[:, h:h + 1], in1=hfl[ci % 3][:, h * 256:(h + 1) * 256], op0=ALU.mult, op1=ALU.add)
                            elif h < 4 or kind == "fwd":
                                P.x("act", "activation", reads=[tR, Trr8], writes=[Tho[0]], out=ho[:, h * 256:(h + 1) * 256], in_=pR[:, cs], func=AF.Copy, scale=rr8[:, h:h + 1])
                            else:
                                P.x("dve", "tensor_scalar", reads=[tR, Trr8], writes=[Tho[1]], out=ho[:, h * 256:(h + 1) * 256], in0=pR[:, cs], scalar1=rr8[:, h:h + 1],
                                    scalar2=None, op0=ALU.mult)
                r_evac()
                if own and ci > 1:
                    epilogue(ci - 2)
            if own:
                epilogue(NB - 2)
                epilogue(NB - 1)
            if kind == "oth":
                for h in range(NH):
                    P.x("dve", "tensor_copy", reads=[TC[h], TCo], writes=[TCo], out=Coth[:, h, :], in_=Cst[:, h, :])
                P.x("dve", "tensor_copy", reads=[Tn, TCo], writes=[TCo], out=noth[:], in_=nst[:])

        ps_mod[0] = 8
        ps_rr[0] = 0
        sweep("oth")
        sweep("fwd")
        P.barrier()
        sweep("bwd")
        P.barrier()
        st2.__exit__(None, None, None)
        st12.__exit__(None, None, None)

        stf = contextlib.ExitStack()
        stf.__enter__()
        KB = 2 * NB
        Vt = [sb("Vt%d" % i, [KB, 128, 128], BF16, stf) for i in range(2)]
        TVt = [T() for _ in range(2)]
        At = sb("At", [128, 128, 2, 64], BF16, stf)
        TAt = T()
        Gt = sb("Gt", [128, 128, 2, NB], BF16, stf)
        TGt = T()
        F1 = sb("F1", [128, 3, 256], BF16, stf)
        TF1 = T()
        Yst = [sb("Yst%d" % i, [NB, 128, 64], BF16, stf) for i in range(2)]
        TYst = [T() for _ in range(2)]
        P.dma("sp", Gt[:], gtw_s.rearrange("p (s r j) -> p s r j", s=128, r=2), reads=[Tgtw_s], writes=[TGt], sem="f0")
        P.dma("sp", F1[:], dft_s.rearrange("p (a b) -> p a b", a=3), reads=[Tdft_s], writes=[TF1], sem="f1")
        Yv = Y_s.rearrange("(j s) c -> j s c", s=128)
        for gh in range(16):
            b = gh % 2
            P.dma("sp", Vt[b][:], V_s[gh].rearrange("(b a) x -> b a x", a=128), reads=[TV_s], writes=[TVt[b]], sem="Vt%d" % b)
            for cp in range(0, 64, 2):
                ps, tp = ps_next()
                for i in range(2):
                    c_ = cp + i
                    P.x("pe", "matmul", reads=[TVt[b], TF1], writes=[tp], out=ps[:, i * 256:(i + 1) * 256], lhsT=Vt[b][:, :, c_], rhs=F1[0:KB, 1, :],
                                                                  start=True, stop=False)
                    P.x("pe", "matmul", reads=[TVt[b], TF1], writes=[tp], out=ps[:, i * 256:(i + 1) * 256], lhsT=Vt[b][:, :, 64 + c_], rhs=F1[0:KB, 2, :],
                                                                  start=False, stop=True)
                eng = "act" if (cp // 2) % 2 == 0 else "dve"
                if eng == "act":
                    P.x("act", "copy", reads=[tp], writes=[TAt], out=At[:, :, :, cp:cp + 2].rearrange("p s r c -> p c r s"),
                                                             in_=ps[:].rearrange("p (c r s) -> p c r s", c=2, r=2))
                else:
                    P.x("dve", "tensor_copy", reads=[tp], writes=[TAt], out=At[:, :, :, cp:cp + 2].rearrange("p s r c -> p c r s"),
                                                                    in_=ps[:].rearrange("p (c r s) -> p c r s", c=2, r=2))
            yb = Yst[gh % 2]
            Tyb = TYst[gh % 2]
            for s0 in range(0, 128, 8):
                ps, tp = ps_next()
                for i in range(8):
                    s = s0 + i
                    P.x("pe", "matmul", reads=[TGt, TAt], writes=[tp], out=ps[0:NB, i * 64:(i + 1) * 64], lhsT=Gt[:, s, 0, :], rhs=At[:, s, 0, :],
                                                                start=True, stop=False)
                    P.x("pe", "matmul", reads=[TGt, TAt], writes=[tp], out=ps[0:NB, i * 64:(i + 1) * 64], lhsT=Gt[:, s, 1, :], rhs=At[:, s, 1, :],
                                                                start=False, stop=True)
                if (s0 // 8) % 2 == 0:
                    P.x("act", "copy", reads=[tp], writes=[Tyb], out=yb[:, s0:s0 + 8, :], in_=ps[0:NB, :].rearrange("p (s c) -> p s c", s=8))
                else:
                    P.x("dve", "tensor_copy", reads=[tp], writes=[Tyb], out=yb[:, s0:s0 + 8, :], in_=ps[0:NB, :].rearrange("p (s c) -> p s c", s=8))
            P.dma("pool", Yv[:, :, gh * 64:(gh + 1) * 64], yb[:], reads=[Tyb], writes=[TY_s], sem="Yst%d" % (gh % 2))
        P.barrier()
        stf.__exit__(None, None, None)

        ps_mod[0] = 7
        ps_rr[0] = 0
        st3 = contextlib.ExitStack()
        st3.__enter__()
        x_tm = [sb("x3_tm%d" % i, [128, D], F32, st3) for i in range(1)]
        Tx_tm = [T() for _ in range(1)]
        xT = sb("x3T", [128, 16, 512], F32, st3)
        TxT = [T() for _ in range(16)]
        hT = sb("h3T", [128, 16, 512], BF16, st3)
        ThT = [T() for _ in range(16)]
        mixT = sb("mixT", [128, 16, 512], F32, st3)
        TmixT = [T() for _ in range(16)]
        FF = sb("FF", [128, 44, 512], BF16, st3)
        TFF = [T() for _ in range(44)]
        sq = [sb("sq3%d" % i, [128, 512], BF16, st3) for i in range(2)]
        Tsq = [T() for _ in range(2)]
        rstd = sb("rstd3", [128, 512], F32, st3)
        Trstd = T()
        gA = sb("gA", [128, 4, 512], F32, st3)
        gB = sb("gB", [128, 4, 512], F32, st3)
        TgA = [T() for _ in range(4)]
        TgB = [T() for _ in range(4)]
        ws3 = WStream(2, st3, "b")
        order3 = []
        for jj in range(4):
            order3 += ["in%d" % (14 + jj), "in%d" % (18 + jj), "ao%d" % jj, "bo%d" % jj]
        order3 += ["wo%d" % i for i in range(4)]
        for jj in range(11):
            order3 += ["fi%d" % jj, "fi%d" % (11 + jj)]
        order3 += ["fo%d" % i for i in range(16)]
        ws3.set_order(order3 * NT)
        haT = FF[:, 0:16, :]
        ThaT = TFF[0:16]
        fbT = FF[:, 16:24, :]
        TfbT = TFF[16:24]
        mixinT = FF[:, 24:40, :]
        TmixinT = TFF[24:40]

        stats3 = Stats(sq, Tsq)

        def post_norm(src, Tsrc, gcol, next_stats):
            stats3.finish(rstd, Trstd)
            for c in range(16):
                P.x("dve", "scalar_tensor_tensor", reads=[Tsrc[c], Trstd, Tvec], writes=[Tsrc[c]], out=src[:, c, :], in0=src[:, c, :],
                    scalar=gcol[:, c:c + 1], in1=rstd[:], op0=ALU.mult, op1=ALU.mult)
                P.x("pool" if c % 2 == 0 else "dve", "tensor_tensor", reads=[TxT[c], Tsrc[c]], writes=[TxT[c]], out=xT[:, c, :], in0=xT[:, c, :],
                    in1=src[:, c, :], op=ALU.add)
                if next_stats:
                    stats3.add(xT[:, c:c + 1, :], [TxT[c]], 1, 0, 512, c == 0, c == 15)

        for it in range(NT):
            tok0 = it * 512
            ws3.prefetch()
            load_xT(x_own, tok0, x_tm, Tx_tm, xT, TxT, stats=stats3)
            for k in range(16):
                P.dma("sp", haT[:, k, :], ha_s[tok0: tok0 + 512, k * 128:(k + 1) * 128],
                      reads=[Tha_s], writes=[ThaT[k]], sem="haT%d" % (k % 4), transpose=True)
            for k in range(8):
                P.dma("sp", fbT[:, k, :], Y_s[tok0: tok0 + 512, k * 128:(k + 1) * 128],
                      reads=[TY_s], writes=[TfbT[k]], sem="fbT%d" % (k % 2), transpose=True)
            stats3.finish(rstd, Trstd)
            fm_apply(xT, TxT, g_pre_mix, hT, ThT, rstd, Trstd)
            for jj in range(4):
                wA, TwA = ws3.get("in%d" % (14 + jj))
                for j4 in range(4):
                    j = jj * 4 + j4
                    ps, tp = ps_next()
                    mm_ii(ps, tp, wA, TwA, j4 * 128, hT, ThT, 16)
                    P.x("act", "activation", reads=[tp, Tvec], writes=[TgA[j4]], out=gA[:, j4, :], in_=ps[:], func=AF.Sigmoid, bias=b_merge[:, j:j + 1])
                wB, TwB = ws3.get("in%d" % (18 + jj))
                for j4 in range(4):
                    j = jj * 4 + j4
                    ps, tp = ps_next()
                    mm_ii(ps, tp, wB, TwB, j4 * 128, hT, ThT, 16)
                    P.x("act", "activation", reads=[tp, Tvec], writes=[TgB[j4]], out=gB[:, j4, :], in_=ps[:], func=AF.Sigmoid, bias=b_merge[:, 16 + j:17 + j])
                wao, Twao = ws3.get("ao%d" % jj)
                for j4 in range(4):
                    ps, tp = ps_next()
                    mm_ii(ps, tp, wao, Twao, j4 * 128, haT, ThaT, 16)
                    P.x("dve", "tensor_tensor", reads=[tp, TgA[j4]], writes=[TgA[j4]], out=gA[:, j4, :], in0=gA[:, j4, :], in1=ps[:], op=ALU.mult)
                wbo, Twbo = ws3.get("bo%d" % jj)
                for j4 in range(4):
                    j = jj * 4 + j4
                    ps, tp = ps_next()
                    mm_ii(ps, tp, wbo, Twbo, j4 * 128, fbT, TfbT, 8)
                    P.x("dve", "tensor_tensor", reads=[tp, TgB[j4]], writes=[TgB[j4]], out=gB[:, j4, :], in0=gB[:, j4, :], in1=ps[:], op=ALU.mult)
                    P.x("pool", "tensor_tensor", reads=[TgA[j4], TgB[j4]], writes=[TmixinT[j]], out=mixinT[:, j, :], in0=gA[:, j4, :], in1=gB[:, j4, :], op=ALU.add)
            if it == 0 and "dbg_ff" in DEBUG_OUT:
                dbg_h = dscr("dbg_h", [128, 16 * 512], BF16)
                P.dma("pool", dbg_h, hT[:].rearrange("p a b -> p (a b)"), reads=ThT, sem="dbg")
                dbg_g = dscr("dbg_g", [128, 8 * 512], F32)
                P.dma("pool", dbg_g[:, 0:2048], gA[:].rearrange("p a b -> p (a b)"), reads=TgA, sem="dbg")
                P.dma("pool", dbg_g[:, 2048:4096], gB[:].rearrange("p a b -> p (a b)"), reads=TgB, sem="dbg")
                dbg_ff = dscr("dbg_ff", [128, 40 * 512], BF16)
                P.dma("pool", dbg_ff, FF[:, 0:40, :].rearrange("p a b -> p (a b)"), reads=TFF[0:40], sem="dbg")
            for jj in range(4):
                wo, Two = ws3.get("wo%d" % jj)
                for j4 in range(4):
                    j = jj * 4 + j4
                    ps, tp = ps_next()
                    mm_ii(ps, tp, wo, Two, j4 * 128, mixinT, TmixinT, 16)
                    P.x("act", "copy", reads=[tp], writes=[TmixT[j]], out=mixT[:, j, :], in_=ps[:])
                    stats3.add(mixT[:, j:j + 1, :], [TmixT[j]], 1, 0, 512, j == 0, j == 15)
            post_norm(mixT, TmixT, g_post_mix, True)
            stats3.finish(rstd, Trstd)
            fm_apply(xT, TxT, g_pre_ffn, hT, ThT, rstd, Trstd)
            for jj in range(11):
                wg_, Twg_ = ws3.get("fi%d" % jj)
                for j4 in range(4):
                    ps, tp = ps_next()
                    mm_ii(ps, tp, wg_, Twg_, j4 * 128, hT, ThT, 16)
                    P.x("act", "activation", reads=[tp], writes=[TgA[j4]], out=gA[:, j4, :], in_=ps[:], func=AF.Silu)
                wu_, Twu_ = ws3.get("fi%d" % (11 + jj))
                for j4 in range(4):
                    j = jj * 4 + j4
                    ps, tp = ps_next()
                    mm_ii(ps, tp, wu_, Twu_, j4 * 128, hT, ThT, 16)
                    P.x("dve", "tensor_tensor", reads=[tp, TgA[j4]], writes=[TFF[j]], out=FF[:, j, :], in0=gA[:, j4, :], in1=ps[:], op=ALU.mult)
            for j in range(16):
                wf, Twf = ws3.get("fo%d" % j)
                ps, tp = ps_next()
                mm_ii(ps, tp, wf, Twf, 0, FF, TFF, 44)
                P.x("act", "copy", reads=[tp], writes=[TmixT[j]], out=mixT[:, j, :], in_=ps[:])
                stats3.add(mixT[:, j:j + 1, :], [TmixT[j]], 1, 0, 512, j == 0, j == 15)
            post_norm(mixT, TmixT, g_post_ffn, False)
            for blk in range(4):
                b2 = 0
                for cg in range(4):
                    ps, tp = ps_next()
                    for i in range(4):
                        c = cg * 4 + i
                        P.x("pe", "transpose", reads=[TxT[c], Tcst], writes=[tp], args=(ps[:, i * 128:(i + 1) * 128], xT[:, c, blk * 128:(blk + 1) * 128], ident_ap,))
                    P.x("act", "copy", reads=[tp], writes=[Tx_tm[b2]], out=x_tm[b2][:, cg * 512:(cg + 1) * 512], in_=ps[:])
                P.dma("pool", y_out[tok0 + blk * 128: tok0 + (blk + 1) * 128, :], x_tm[b2][:], reads=[Tx_tm[b2]], sem="yo%d" % b2)
        P.finish()
        st3.__exit__(None, None, None)
        P.emit()
    return nc


def _fm(v):
    return np.ascontiguousarray(np.asarray(v, np.float32).reshape(-1, 128).T)


def core_inputs(NB, role, x_own, x_oth, halo8, W, seq_T):
    SEG = NB * 128
    hf = role["hf"]
    prompt = role["prompt"]
    x_halo = np.zeros((128, D), np.float32)
    x_halo[0:8] = halo8
    w_in = W["w_in"]
    bg = W["b_gates"]
    if hf == 1:
        w_go = np.concatenate([w_in[:, OFF_G + 0:OFF_G + 8], w_in[:, OFF_G + 8:OFF_G + 16]], axis=1)
        b_go = np.concatenate([bg[0:8], bg[8:16]])
    else:
        w_go = np.concatenate([w_in[:, OFF_G + 16:OFF_G + 24], w_in[:, OFF_G + 24:OFF_G + 32]], axis=1)
        b_go = np.concatenate([bg[16:24], bg[24:32]])
    vecs = np.concatenate([_fm(W["g_pre_mix"]), _fm(W["g_post_mix"]), _fm(W["g_pre_ffn"]), _fm(W["g_post_ffn"]),
                           _fm(W["b_merge"])], axis=1)
    convw = np.ascontiguousarray(W["conv_w"].reshape(5, 16, 128).transpose(2, 1, 0))
    flagP = 1.0 if (prompt and hf == 1) else 0.0
    flagS = 1.0 if (prompt and hf == 0) else 0.0
    rep = np.concatenate([np.tile(W["g_head"][None, :], (128, 1)), np.tile(bg[None, :], (128, 1)),
                          np.tile(b_go[None, :], (128, 1)),
                          np.tile(np.array([[flagP, flagS, -flagP, -flagS]], np.float32), (128, 1))], axis=1).astype(np.float32)
    idx = np.arange(128)
    U = (idx[:, None] <= idx[None, :]).astype(np.float32)
    L = (idx[:, None] >= idx[None, :]).astype(np.float32)
    Mo = (idx[:, None] > idx[None, :]).astype(np.float32) if hf == 1 else (idx[:, None] < idx[None, :]).astype(np.float32)
    cst = np.stack([U, L, Mo, np.ones((128, 128), np.float32), np.eye(128, dtype=np.float32)], axis=1)
    c = np.arange(128, dtype=np.float64)
    ang = 2 * np.pi * np.outer(c, c) / 128.0
    chd = np.concatenate([np.cos(ang), -np.sin(ang)], axis=1)
    N = 2 * SEG
    s = np.arange(128)
    j = np.arange(NB)
    tok = s[:, None] + 128 * j[None, :]
    kp = (tok + SEG * hf) if prompt else 2 * tok
    KB = 2 * NB
    kap = kp[:, 0] % KB
    bpart = np.arange(KB)
    b_true = np.where(bpart < NB, bpart + NB * hf, (bpart - NB) + NB * (1 - hf)).astype(np.float64)
    phi = 2 * np.pi * np.outer(b_true, kap) / float(KB)
    Fr, Fi = np.cos(phi), -np.sin(phi)
    F1a = np.zeros((128, 256))
    F1b = np.zeros((128, 256))
    F1a[:KB] = np.concatenate([Fr, Fi], axis=1)
    F1b[:KB] = np.concatenate([-Fi, Fr], axis=1)
    dftc = np.stack([chd, F1a, F1b], axis=1).astype(np.float32)
    scale = 1.0 / np.sqrt(float(seq_T) * 128.0)
    a = np.arange(128, dtype=np.float64)
    th = 2 * np.pi * (a[:, None, None] * (kp[None, :, :] % N)) / N
    Gr = scale * np.cos(th)
    nGi = scale * np.sin(th)
    gtw = np.stack([Gr, nGi], axis=2).reshape(128, -1).astype(np.float32)
    return {
        "x_own": np.ascontiguousarray(x_own, np.float32), "x_oth": np.ascontiguousarray(x_oth, np.float32), "x_halo": x_halo,
        "w_in": W["w_in"], "w_a_out": W["w_a_out"], "w_b_out": W["w_b_out"], "w_out": W["w_out"],
        "w_ffn_in": W["w_ffn_in"], "w_ffn_out": W["w_ffn_out"], "w_go": np.ascontiguousarray(w_go, np.float32),
        "vecs": np.ascontiguousarray(vecs, np.float32), "convw": convw.astype(np.float32), "rep": rep, "cst": cst,
        "dftc": dftc, "gtw": gtw,
    }


def make_in_maps(NB, seqs, W):
    SEG = NB * 128
    maps, plan = [], []
    z2 = np.zeros((2, D), np.float32)
    for si, xs in enumerate(seqs):
        Tn = xs.shape[0]
        if Tn == SEG:
            halo8 = np.zeros((8, D), np.float32)
            maps.append(core_inputs(NB, dict(hf=0, prompt=False), xs, np.zeros((SEG, D), np.float32), halo8, W, Tn))
            plan.append((si, 0))
        else:
            assert Tn == 2 * SEG
            a, b = xs[:SEG], xs[SEG:]
            h0 = np.concatenate([z2, b[0:2], a[-2:], z2], axis=0)
            h1 = np.concatenate([a[-2:], z2, z2, b[0:2]], axis=0)
            maps.append(core_inputs(NB, dict(hf=0, prompt=True), a, b, h0, W, Tn))
            plan.append((si, 0))
            maps.append(core_inputs(NB, dict(hf=1, prompt=True), b, a, h1, W, Tn))
            plan.append((si, 1))
    return maps, plan


_W_KEYS = ["g_pre_mix", "w_in", "conv_w", "b_gates", "g_head", "w_a_out", "w_b_out", "b_merge", "w_out", "g_post_mix",
           "g_pre_ffn", "w_ffn_in", "w_ffn_out", "g_post_ffn"]


def run_seqs(NB, seqs, weights):
    W = {k: np.ascontiguousarray(np.asarray(weights[k], np.float32)[0]) for k in _W_KEYS}
    maps, plan = make_in_maps(NB, seqs, W)
    nc = build(NB)
    res = run_bass_kernel_spmd(nc, maps, core_ids=list(range(len(maps))))
    SEG = NB * 128
    outs = [np.zeros_like(np.asarray(s, np.float32)) for s in seqs]
    for (si, hf), r in zip(plan, res.results):
        outs[si][hf * SEG:(hf + 1) * SEG] = r["y_out"]
    if DEBUG_OUT:
        return outs, res.results
    return outs


def kernel(**inputs):
    xp = np.asarray(inputs["x_prompt"], np.float32)
    xs = np.asarray(inputs["x_sample"], np.float32)
    seqs = [xp[0], xp[1], xs[0], xs[1], xs[2], xs[3]]
    outs = run_seqs(64, seqs, inputs)
    y_prompt = np.stack(outs[0:2], axis=0)
    y_sample = np.stack(outs[2:6], axis=0)
    return (y_prompt, y_sample)
```

```python
import contextlib
import numpy as np
import concourse.bass as bass
import concourse.mybir as mybir
from concourse.bass_utils import run_bass_kernel_spmd

F32 = mybir.dt.float32
BF16 = mybir.dt.bfloat16
AF = mybir.ActivationFunctionType
ALU = mybir.AluOpType

D = 2048
NH = 8
DK = 128
DV = 256
W_QK = 1024
W_V = 2048
OFF_Q = 0
OFF_K = 1024
OFF_V = 2048
OFF_O = 4096
OFF_G = 6144
OFF_B = 6176
OFF_M = 7200
W_IN = 11296
D_FF = 5632
EPS = 1e-6
LNC = float(np.log(128.0 ** -0.5))

ENGS = ("pe", "act", "dve", "pool", "sp")


class T:
    __slots__ = ("name", "w", "r", "untracked")

    def __init__(self, name="", untracked=False):
        self.name = name
        self.w = None
        self.r = []
        self.untracked = untracked


class Op:
    __slots__ = ("eng", "emit", "waits", "signal", "sidx", "dma")

    def __init__(self, eng, emit):
        self.eng = eng
        self.emit = emit
        self.waits = []
        self.signal = False
        self.sidx = 0
        self.dma = None


class Prog:
    def __init__(self, nc):
        self.nc = nc
        self.ops = {e: [] for e in ENGS}
        self.dma_cnt = {}
        self.waited = {e: {} for e in ENGS}

    def _need(self, op, ev):
        if ev is None:
            return
        e = op.eng
        if ev[0] == "e":
            _, eng2, idx = ev
            if eng2 == "pe" and e == "pe":
                return
            key = ("e", eng2)
            if self.waited[e].get(key, -1) >= idx:
                return
            self.waited[e][key] = idx
            self.ops[eng2][idx].signal = True
            op.waits.append(ev)
        else:
            _, sk, val = ev
            key = ("d", sk)
            if self.waited[e].get(key, -1) >= val:
                return
            self.waited[e][key] = val
            op.waits.append(ev)

    def _track(self, op, ev, reads, writes):
        reads = [t for t in reads if not t.untracked]
        writes = [t for t in writes if not t.untracked]
        for t in reads:
            self._need(op, t.w)
        for t in writes:
            self._need(op, t.w)
            for r in t.r:
                self._need(op, r)
        for t in reads:
            t.r.append(ev)
            if len(t.r) > 64:
                t.r = t.r[-48:]
        for t in writes:
            t.w = ev
            t.r = []

    def op(self, eng, emit, reads=(), writes=()):
        o = Op(eng, emit)
        ev = ("e", eng, len(self.ops[eng]))
        self._track(o, ev, reads, writes)
        self.ops[eng].append(o)
        return ev

    def x(self, eng, name, reads=(), writes=(), args=(), **kw):
        return self.op(eng, lambda e, name=name, args=args, kw=kw: getattr(e, name)(*args, **kw), reads, writes)

    def dma(self, q, out, in_, reads=(), writes=(), sem="dma", **kw):
        o = Op(q, lambda e: e.dma_start(out=out, in_=in_, **kw))
        n = self.dma_cnt.get(sem, 0) + 1
        self.dma_cnt[sem] = n
        ev = ("d", sem, 16 * n)
        o.dma = (sem, 16 * n)
        if n > 1:
            self._need(o, ("d", sem, 16 * (n - 1)))
        self._track(o, ev, reads, writes)
        self.ops[q].append(o)
        return ev

    def barrier(self):
        evs = []
        for e in ENGS:
            for i in range(len(self.ops[e]) - 1, -1, -1):
                o = self.ops[e][i]
                if o.dma is None and o.emit is not None:
                    evs.append(("e", e, i))
                    break
        for sk, n in self.dma_cnt.items():
            evs.append(("d", sk, 16 * n))
        for e in ENGS:
            o = Op(e, None)
            for ev in evs:
                if ev[0] == "e" and ev[1] == e:
                    continue
                if ev[0] == "e" and ev[1] == "pe" and e == "pe":
                    continue
                self._need(o, ev)
            if o.waits:
                self.ops[e].append(o)

    def finish(self):
        o = Op("sp", None)
        for sk, n in self.dma_cnt.items():
            self._need(o, ("d", sk, 16 * n))
        self.ops["sp"].append(o)

    def emit(self):
        nc = self.nc
        for e in ENGS:
            c = 0
            for o in self.ops[e]:
                if o.signal:
                    c += 1
                o.sidx = c
        with contextlib.ExitStack() as st:
            esem = {e: st.enter_context(nc.semaphore("S_" + e)) for e in ENGS}
            dsem = {sk: st.enter_context(nc.semaphore("D_%s" % (sk,))) for sk in self.dma_cnt}
            block = st.enter_context(nc.Block())

            def run(ename):
                def body(eng):
                    for o in self.ops[ename]:
                        for ev in o.waits:
                            if ev[0] == "e":
                                eng.wait_ge(esem[ev[1]], self.ops[ev[1]][ev[2]].sidx)
                            else:
                                eng.wait_ge(dsem[ev[1]], ev[2])
                        if o.emit is None:
                            continue
                        ins = o.emit(eng)
                        if o.dma is not None:
                            ins.then_inc(dsem[o.dma[0]], 16)
                        elif o.signal:
                            ins.then_inc(esem[ename], 1)
                return body

            block.tensor(run("pe"))
            block.scalar(run("act"))
            block.vector(run("dve"))
            block.gpsimd(run("pool"))
            block.sync(run("sp"))


DEBUG_OUT = ()


def build(NB):
    SEG = NB * 128
    NT = NB // 4
    nc = bass.Bass("TRN2", target_bir_lowering=False)
    P = Prog(nc)

    def din(name, shape, dt=F32):
        return nc.dram_tensor(name, list(shape), dt, kind="ExternalInput").ap()

    def dscr(name, shape, dt):
        return nc.dram_tensor(name, list(shape), dt, kind=("ExternalOutput" if name in DEBUG_OUT else "Internal")).ap()

    x_own = din("x_own", [SEG, D])
    x_oth = din("x_oth", [SEG, D])
    x_halo = din("x_halo", [128, D])
    w_in = din("w_in", [D, W_IN])
    w_a_out = din("w_a_out", [W_V, D])
    w_b_out = din("w_b_out", [1024, D])
    w_out = din("w_out", [D, D])
    w_ffn_in = din("w_ffn_in", [D, 2 * D_FF])
    w_ffn_out = din("w_ffn_out", [D_FF, D])
    w_go = din("w_go", [D, 16])
    vecs = din("vecs", [128, 16 * 4 + 32])
    convw = din("convw", [128, 16, 5])
    rep = din("rep", [128, 2048 + 32 + 16 + 4])
    cst = din("cst", [128, 5, 128])
    dftc = din("dftc", [128, 3, 256])
    gtw = din("gtw", [128, 128 * 2 * NB])
    y_out = nc.dram_tensor("y_out", [SEG, D], F32, kind="ExternalOutput").ap()

    WB = {}

    def add_wb(key, src, c0, ncols):
        K = src.shape[0]
        KC = K // 128
        scr = dscr("wb_%s" % key, [128, KC * ncols], BF16)
        WB[key] = dict(src=src, c0=c0, n=ncols, KC=KC, scr=scr, t=T("wb_" + key))

    for i in range(22):
        add_wb("in%d" % i, w_in, 512 * i if i < 12 else 0, 512)
    for i in range(2):
        WB["in%d" % (12 + i)]["c0"] = OFF_B + 512 * i
    for i in range(8):
        WB["in%d" % (14 + i)]["c0"] = OFF_M + 512 * i
    add_wb("gate", w_in, OFF_G, 32)
    add_wb("go", w_go, 0, 16)
    for i in range(4):
        add_wb("ao%d" % i, w_a_out, 512 * i, 512)
        add_wb("bo%d" % i, w_b_out, 512 * i, 512)
        add_wb("wo%d" % i, w_out, 512 * i, 512)
    for i in range(22):
        add_wb("fi%d" % i, w_ffn_in, 512 * i, 512)
    for i in range(16):
        add_wb("fo%d" % i, w_ffn_out, 128 * i, 128)

    q_s = dscr("q_s", [NH, 128, SEG + 512], BF16)
    k_s = dscr("k_s", [NH, 128, SEG + 512], BF16)
    ko_s = dscr("ko_s", [NH, 128, SEG + 512], BF16)
    v_s = dscr("v_s", [SEG, W_V], BF16)
    vo_s = dscr("vo_s", [SEG, W_V], BF16)
    sgo_s = dscr("sgo_s", [SEG, W_V], F32)
    V_s = dscr("V_s", [16, 2 * SEG, 128], BF16)
    hf_s = dscr("hf_s", [SEG, W_V], F32)
    ha_s = dscr("ha_s", [SEG, W_V], BF16)
    Y_s = dscr("Y_s", [SEG, 1024], BF16)
    gtw_s = dscr("gtw_s", [128, 128 * 2 * NB], BF16)
    dft_s = dscr("dft_s", [128, 3 * 256], BF16)
    Tq_s, Tk_s, Tko_s, Tv_s, Tvo_s, Tsgo_s, TV_s, Thf_s, Tha_s, TY_s = [T(untracked=True) for _ in range(10)]
    Tgtw_s, Tdft_s = T(), T()

    st = contextlib.ExitStack()
    with st:
        def sb(name, shape, dt, stack=st):
            return stack.enter_context(nc.sbuf_tensor(name, list(shape), dt))

        psb = [st.enter_context(nc.psum_tensor("ps%d" % i, [128, 512], F32)) for i in range(8)]
        Tps = [T("ps%d" % i) for i in range(8)]
        ps_rr = [0]

        ps_live = [False] * 8
        ps_mod = [7]

        def ps_next():
            i = ps_rr[0] % ps_mod[0]
            ps_rr[0] = (i + 1) % ps_mod[0]
            assert (not ps_live[i]) or len(Tps[i].r) > 0, "PSUM bank %d re-allocated before its consumer was recorded" % i
            ps_live[i] = True
            return psb[i], Tps[i]

        cst_t = sb("cst_t", [128, 5, 128], F32)
        Tcst = T()
        vec_t = sb("vec_t", [128, 96], F32)
        Tvec = T()
        cw_t = sb("cw_t", [128, 16, 5], F32)
        Tcw = T()
        rep_t = sb("rep_t", [128, 2100], F32)
        Trep = T()
        cb_t = sb("cb_t", [128, 4], F32)
        Tcb = T()
        identb = sb("identb", [128, 128], BF16)
        onesb = sb("onesb", [128, 128], BF16)
        Tib = T()
        P.dma("sp", cst_t[:], cst, writes=[Tcst], sem="c0")
        P.dma("sp", vec_t[:], vecs, writes=[Tvec], sem="c0")
        P.dma("sp", cw_t[:], convw, writes=[Tcw], sem="c0")
        P.dma("sp", rep_t[:], rep, writes=[Trep], sem="c0")
        P.x("pool", "memset", writes=[Tcb], args=(cb_t[:, 0:1], EPS,))
        P.x("pool", "memset", reads=[Tcb], writes=[Tcb], args=(cb_t[:, 1:2], LNC,))
        P.x("dve", "tensor_copy", reads=[Tcst], writes=[Tib], out=identb[:], in_=cst_t[:, 4, :])
        P.x("dve", "tensor_copy", reads=[Tcst, Tib], writes=[Tib], out=onesb[:], in_=cst_t[:, 3, :])
        U_ap, L_ap, Mo_ap, ones_ap, ident_ap = [cst_t[:, i, :] for i in range(5)]
        g_pre_mix = vec_t[:, 0:16]
        g_post_mix = vec_t[:, 16:32]
        g_pre_ffn = vec_t[:, 32:48]
        g_post_ffn = vec_t[:, 48:64]
        b_merge = vec_t[:, 64:96]
        ghead_rep = rep_t[:, 0:2048]
        bgate_rep = rep_t[:, 2048:2080]
        bgo_rep = rep_t[:, 2080:2096]
        flags = rep_t[:, 2096:2100]
        eps_ap = cb_t[:, 0:1]
        lnc_ap = cb_t[:, 1:2]

        def cast_wb(key):
            w = WB[key]
            src = w["src"][:, w["c0"]:w["c0"] + w["n"]].rearrange("(kc p) n -> p kc n", p=128)
            dst = w["scr"].rearrange("p (kc n) -> p kc n", n=w["n"])
            P.dma("pool", dst, src, writes=[w["t"]], sem="cast")

        P.dma("pool", dft_s, dftc.rearrange("p a b -> p (a b)"), writes=[Tdft_s], sem="cast")
        order0 = (["gate", "go"] + ["in%d" % i for i in (0, 1, 2, 3, 4, 5, 6, 7, 12, 13)])
        for key in order0:
            cast_wb(key)
        cast_rest = (["in%d" % i for i in (8, 9, 10, 11)] + ["GTW"] + ["in%d" % i for i in range(14, 22)] + ["ao%d" % i for i in range(4)]
                     + ["bo%d" % i for i in range(4)] + ["wo%d" % i for i in range(4)] + ["fi%d" % i for i in range(22)]
                     + ["fo%d" % i for i in range(16)])
        cast_per_tile = -(-len(cast_rest) // max(1, (2 * NT - 1)))

        def cast_some(n):
            for _ in range(n):
                if not cast_rest:
                    return
                key = cast_rest.pop(0)
                if key == "GTW":
                    P.dma("pool", gtw_s, gtw, writes=[Tgtw_s], sem="cast")
                else:
                    cast_wb(key)

        class WStream:
            def __init__(self, nslots, stack, tag):
                self.n = nslots
                self.slots = [sb("wr%s%d" % (tag, i), [128, 8192], BF16, stack) for i in range(nslots)]
                self.T = [T() for _ in range(nslots)]
                self.tag = tag
                self.order = []
                self.issued = 0
                self.taken = 0

            def set_order(self, order):
                self.order = list(order)
                self.issued = 0
                self.taken = 0

            def _issue(self):
                key = self.order[self.issued]
                s = self.issued % self.n
                w = WB[key]
                P.dma("sp", self.slots[s][:, 0:w["KC"] * w["n"]], w["scr"], reads=[w["t"]], writes=[self.T[s]],
                      sem="wr%s%d" % (self.tag, s))
                self.issued += 1

            def get(self, key):
                assert self.order[self.taken] == key, (self.order[self.taken], key)
                while self.issued < len(self.order) and self.issued < self.taken + self.n:
                    self._issue()
                s = self.taken % self.n
                self.taken += 1
                w = WB[key]
                ap = self.slots[s][:, 0:w["KC"] * w["n"]].rearrange("p (kc n) -> p kc n", n=w["n"])
                return ap, self.T[s]

            def prefetch(self):
                while self.issued < len(self.order) and self.issued < self.taken + self.n:
                    self._issue()

        class Stats:
            def __init__(self, sq, Tsq):
                self.sq, self.Tsq = sq, Tsq
                self.n = 0
                self.pend = []

            def _flush(self):
                for f in self.pend:
                    f()
                self.pend = []

            def add(self, src, Tsrc, nchunk, col0, ntok, first, last):
                s2 = self.n % 2
                self.n += 1
                sqv = self.sq[s2][:, 0:nchunk * ntok].rearrange("p (n t) -> p n t", n=nchunk)
                P.x("act", "activation", reads=Tsrc, writes=[self.Tsq[s2]], out=sqv, in_=src, func=AF.Square)
                self._flush()

                def mm(s2=s2, sqv=sqv):
                    for i in range(nchunk):
                        P.x("pe", "matmul", reads=[self.Tsq[s2], Tib], writes=[Tps[7]], out=psb[7][:, col0:col0 + ntok], lhsT=onesb[:], rhs=sqv[:, i, :],
                            start=(first and i == 0), stop=(last and i == nchunk - 1))
                self.pend.append(mm)

            def finish(self, rstd, Trstd, ntok=512):
                self._flush()
                P.x("act", "activation", reads=[Tps[7], Tcb], writes=[Trstd], out=rstd[:, 0:ntok], in_=psb[7][:, 0:ntok], func=AF.Sqrt, scale=1.0 / D, bias=eps_ap)
                P.x("dve", "reciprocal", reads=[Trstd], writes=[Trstd], out=rstd[:, 0:ntok], in_=rstd[:, 0:ntok])

        def load_xT(xsrc, tok0, x_tm, Tx_tm, xT, TxT, nblk=4, stats=None):
            for blk in range(nblk):
                b2 = blk % len(x_tm)
                P.dma("sp", x_tm[b2][:], xsrc[tok0 + blk * 128: tok0 + (blk + 1) * 128, :], writes=[Tx_tm[b2]],
                      sem="xtm%d" % b2)
                for cg in range(4):
                    ps, tp = ps_next()
                    for i in range(4):
                        c = cg * 4 + i
                        P.x("pe", "transpose", reads=[Tx_tm[b2], Tcst], writes=[tp],
                            args=(ps[:, i * 128:(i + 1) * 128], x_tm[b2][:, c * 128:(c + 1) * 128], ident_ap,))
                    dst = xT[:, cg * 4:cg * 4 + 4, blk * 128:(blk + 1) * 128]
                    P.x("act", "copy", reads=[tp], writes=[TxT[cg * 4 + i] for i in range(4)], out=dst, in_=ps[:].rearrange("p (i t) -> p i t", i=4))
                    if stats is not None:
                        stats.add(dst, [TxT[cg * 4 + i] for i in range(4)], 4, blk * 128, 128, cg == 0, cg == 3)

        def fm_apply(src, Tsrc, gcol, dst, Tdst, rstd, Trstd, ntok=512):
            for c in range(16):
                P.x("dve", "scalar_tensor_tensor", reads=[Tsrc[c], Trstd, Tvec], writes=[Tdst[c]], out=dst[:, c, 0:ntok], in0=src[:, c, 0:ntok],
                    scalar=gcol[:, c:c + 1], in1=rstd[:, 0:ntok], op0=ALU.mult, op1=ALU.mult)

        def mm_ii(ps, tp, wap, wT, col0, act, Tact, KC, ntok=512, m=128):
            for k in range(KC):
                def mm(e, k=k):
                    return e.matmul(ps[0:m, 0:ntok], lhsT=wap[:, k, col0:col0 + m], rhs=act[:, k, 0:ntok],
                                    start=(k == 0), stop=(k == KC - 1))
                P.op("pe", mm, reads=[wT, Tact[k]], writes=[tp])

        def mm_i(ps, tp, wap, wT, c0, n, act, Tact, blk, KC):
            for k in range(KC):
                def mm(e, k=k):
                    return e.matmul(ps[:, 0:n], lhsT=act[:, k, blk * 128:(blk + 1) * 128], rhs=wap[:, k, c0:c0 + n],
                                    start=(k == 0), stop=(k == KC - 1))
                P.op("pe", mm, reads=[wT, Tact[k]], writes=[tp])

        st12 = contextlib.ExitStack()
        st12.__enter__()
        gs = sb("gs", [128, NB, 64], F32, st12)
        Tgs = [T() for _ in range(NB)]
        gso = sb("gso", [128, NB, 16], F32, st12)
        Tgso = [T() for _ in range(NB)]
        TC = [T() for _ in range(NH)]
        TCb = [T() for _ in range(NH)]
        Tn = T()
        Tnb = T()
        TCo = T()
        halo_raw = sb("halo_raw", [128, 16, 8], F32, st12)
        Thalo = T()
        pfx = sb("pfx", [128, 8], F32, st12)
        Tpfx = T()

        st1 = contextlib.ExitStack()
        st1.__enter__()
        x_tm = [sb("x_tm%d" % i, [128, D], F32, st1) for i in range(2)]
        Tx_tm = [T() for _ in range(2)]
        xT = sb("xT", [128, 16, 512], F32, st1)
        TxT = [T() for _ in range(16)]
        hT = sb("hT", [128, 16, 512], BF16, st1)
        ThT = [T() for _ in range(16)]
        sq = [sb("sq%d" % i, [128, 512], BF16, st1) for i in range(2)]
        Tsq = [T() for _ in range(2)]
        rstd = sb("rstd", [128, 512], F32, st1)
        Trstd = T()
        wg_t = sb("wg_t", [128, 16, 32], BF16, st1)
        wgo_t = sb("wgo_t", [128, 16, 16], BF16, st1)
        Twg = T()
        dft_t = sb("dft_t", [128, 3, 256], BF16, st1)
        Tdft = T()
        raw = [sb("raw%d" % i, [128, 516], F32, st1) for i in range(2)]
        Traw = [T() for _ in range(2)]
        acc = [sb("acc%d" % i, [128, 512], F32, st1) for i in range(2)]
        Tacc = [T() for _ in range(2)]
        qko = [sb("qko%d" % i, [128, 512], BF16, st1) for i in range(2)]
        Tqko = [T() for _ in range(2)]
        carry = sb("carry", [128, 16, 4], F32, st1)
        Tcarry = [T() for _ in range(16)]
        vst = [sb("vst%d" % i, [128, 4, 512], BF16, st1) for i in range(2)]
        Tvst = [T() for _ in range(2)]
        sgst = sb("sgst", [128, 4, 512], F32, st1)
        Tsgst = T()
        uT = sb("uT", [128, 8, 512], BF16, st1)
        TuT = [T() for _ in range(8)]
        Vst = sb("Vst", [128, 4, 16 * 128], BF16, st1)
        TVst = [T() for _ in range(4)]
        g32 = sb("g32", [128, 32], F32, st1)
        sp16 = sb("sp16", [128, 16], F32, st1)
        t16 = sb("t16", [128, 16], F32, st1)
        t16b = sb("t16b", [128, 16], F32, st1)
        Tg32, Tsp16, Tt16, Tt16b = T(), T(), T(), T()
        c32 = sb("c32", [128, 32], F32, st1)
        Tc32 = T()
        ws1 = WStream(3, st1, "a")

        P.dma("sp", wg_t[:], WB["gate"]["scr"].rearrange("p (k n) -> p k n", n=32), reads=[WB["gate"]["t"]], writes=[Twg], sem="c1")
        P.dma("sp", wgo_t[:], WB["go"]["scr"].rearrange("p (k n) -> p k n", n=16), reads=[WB["go"]["t"]], writes=[Twg], sem="c1")
        P.dma("sp", dft_t[:], dft_s.rearrange("p (a b) -> p a b", a=3), reads=[Tdft_s], writes=[Tdft], sem="c1")
        P.x("pool", "memset", writes=[Tpfx], args=(pfx[:], 0.0,))
        for i in range(2):
            P.x("pool", "memset", writes=[Traw[i]], args=(raw[i][:], 0.0,))

        def conv_emit(ridx, ch, ncol, dst_dram, dcol0, Tdst):
            r = raw[ridx]
            a = acc[ridx]
            o = qko[ridx]
            P.x("dve", "tensor_scalar", reads=[Traw[ridx], Tcw], writes=[Tacc[ridx]], out=a[:, 0:ncol], in0=r[:, 0:ncol], scalar1=cw_t[:, ch, 0:1], scalar2=None,
                                                 op0=ALU.mult)
            for j in range(1, 5):
                P.x("dve", "scalar_tensor_tensor", reads=[Traw[ridx], Tacc[ridx], Tcw], writes=[Tacc[ridx]], out=a[:, 0:ncol], in0=r[:, j:j + ncol], scalar=cw_t[:, ch, j:j + 1],
                                                                 in1=a[:, 0:ncol], op0=ALU.mult, op1=ALU.add)
            P.x("act", "activation", reads=[Tacc[ridx]], writes=[Tqko[ridx]], out=o[:, 0:ncol], in_=a[:, 0:ncol], func=AF.Silu)
            P.dma("pool", dst_dram[:, dcol0:dcol0 + ncol], o[:, 0:ncol], reads=[Tqko[ridx]], writes=[Tdst], sem="qko%d" % ridx)

        ws1.set_order(["in0", "in1", "in2", "in3"])
        stats1 = Stats(sq, Tsq)
        load_xT(x_halo, 0, x_tm, Tx_tm, xT, TxT, nblk=1, stats=stats1)
        stats1.finish(rstd, Trstd, ntok=128)
        fm_apply(xT, TxT, g_pre_mix, hT, ThT, rstd, Trstd, ntok=128)
        for wb in range(4):
            wap, wT = ws1.get("in%d" % wb)
            for hh in range(4):
                ch = wb * 4 + hh
                ps, tp = ps_next()
                mm_ii(ps, tp, wap, wT, hh * 128, hT, ThT, 16, ntok=128)
                P.x("act", "copy", reads=[tp], writes=[Thalo], out=halo_raw[:, ch, :], in_=ps[:, 0:8])

        def phase1_segment(own):
            xsrc = x_own if own else x_oth
            wkeys = (["in%d" % i for i in range(14)] if own else ["in2", "in3", "in4", "in5", "in6", "in7", "in12", "in13"])
            ws1.set_order(wkeys * NT)
            ks = k_s if own else ko_s
            Tks = Tk_s if own else Tko_s
            vs = v_s if own else vo_s
            Tvs = Tv_s if own else Tvo_s
            lo = 0 if own else 4
            for ch in range(16):
                if (not own) and ch < 8:
                    continue
                P.x("pool", "memset", reads=[], writes=[Tcarry[ch]], args=(carry[:, ch, 0:2], 0.0,))
                P.x("pool", "tensor_copy", reads=[Thalo, Tcarry[ch]], writes=[Tcarry[ch]], out=carry[:, ch, 2:4], in_=halo_raw[:, ch, lo:lo + 2])
            rr = [0]
            for it in range(NT):
                tok0 = it * 512
                load_xT(xsrc, tok0, x_tm, Tx_tm, xT, TxT, stats=stats1)
                stats1.finish(rstd, Trstd)
                fm_apply(xT, TxT, g_pre_mix, hT, ThT, rstd, Trstd)
                for wb in ((0, 1, 2, 3) if own else (2, 3)):
                    wap, wT = ws1.get("in%d" % wb)
                    for hh in range(4):
                        ch = wb * 4 + hh
                        ps, tp = ps_next()
                        mm_ii(ps, tp, wap, wT, hh * 128, hT, ThT, 16)
                        ri = rr[0] % 2
                        rr[0] += 1
                        r = raw[ri]
                        P.x("dve", "tensor_copy", reads=[Tcarry[ch]], writes=[Traw[ri]], out=r[:, 0:4], in_=carry[:, ch, :])
                        P.x("act", "copy", reads=[tp, Traw[ri]], writes=[Traw[ri]], out=r[:, 4:516], in_=ps[:])
                        P.x("dve", "tensor_copy", reads=[Traw[ri]], writes=[Tcarry[ch]], out=carry[:, ch, :], in_=r[:, 512:516])
                        dst = (q_s if ch < 8 else ks)[ch % 8]
                        conv_emit(ri, ch, 512, dst, tok0, Tq_s if ch < 8 else Tks)
                for cb in range(4):
                    wap, wT = ws1.get("in%d" % (4 + cb))
                    v2 = cb % 2
                    for blk in range(4):
                        ps, tp = ps_next()
                        mm_i(ps, tp, wap, wT, 0, 512, hT, ThT, blk, 16)
                        P.x("act", "copy", reads=[tp], writes=[Tvst[v2]], out=vst[v2][:, blk, :], in_=ps[:])
                    P.dma("pool", vs[tok0:tok0 + 512, cb * 512:(cb + 1) * 512].rearrange("(b p) f -> p b f", p=128), vst[v2][:],
                          reads=[Tvst[v2]], writes=[Tvs], sem="vst%d" % v2)
                if own:
                    for cb in range(4):
                        wap, wT = ws1.get("in%d" % (8 + cb))
                        for blk in range(4):
                            ps, tp = ps_next()
                            mm_i(ps, tp, wap, wT, 0, 512, hT, ThT, blk, 16)
                            P.x("act", "activation", reads=[tp], writes=[Tsgst], out=sgst[:, blk, :], in_=ps[:], func=AF.Sigmoid)
                            P.x("dve", "tensor_tensor", reads=[Tsgst, Trep], writes=[Tsgst], out=sgst[:, blk, :], in0=sgst[:, blk, :],
                                in1=ghead_rep[:, cb * 512:(cb + 1) * 512], op=ALU.mult)
                        P.dma("pool", sgo_s[tok0:tok0 + 512, cb * 512:(cb + 1) * 512].rearrange("(b p) f -> p b f", p=128), sgst[:],
                              reads=[Tsgst], writes=[Tsgo_s], sem="sgst")
                for ub in range(2):
                    wap, wT = ws1.get("in%d" % (12 + ub))
                    for gg in range(4):
                        g = ub * 4 + gg
                        ps, tp = ps_next()
                        mm_ii(ps, tp, wap, wT, gg * 128, hT, ThT, 16)
                        P.x("act", "copy", reads=[tp], writes=[TuT[g]], out=uT[:, g, :], in_=ps[:])
                        for blk in range(4):
                            ps2, tp2 = ps_next()
                            P.x("pe", "matmul", reads=[TuT[g], Tdft], writes=[tp2], out=ps2[:, 0:256], lhsT=uT[:, g, blk * 128:(blk + 1) * 128],
                                                                               rhs=dft_t[:, 0, :], start=True, stop=True)
                            P.x("dve", "tensor_copy", reads=[tp2], writes=[TVst[blk]],
                                out=Vst[:, blk, g * 256:(g + 1) * 256].rearrange("p (h r c) -> p h r c", h=2, r=2),
                                in_=ps2[:, 0:256].rearrange("p (r h c) -> p h r c", r=2, h=2))
                tv0 = tok0 + (0 if own else SEG)
                for blk in range(4):
                    P.dma("pool", V_s[:, tv0 + blk * 128: tv0 + (blk + 1) * 128, :].rearrange("g p x -> p g x"),
                          Vst[:, blk, :].rearrange("p (g x) -> p g x", x=128), reads=[TVst[blk]], writes=[TV_s], sem="Vst")
                for blk in range(4):
                    c = it * 4 + blk
                    ps, tp = ps_next()
                    if own:
                        for k in range(16):
                            P.x("pe", "matmul", reads=[ThT[k], Twg], writes=[tp], out=ps[:, 0:32], lhsT=hT[:, k, blk * 128:(blk + 1) * 128],
                                                                             rhs=wg_t[:, k, :], start=(k == 0), stop=(k == 15))
                        P.x("dve", "tensor_tensor", reads=[tp, Trep], writes=[Tg32], out=g32[:], in0=ps[:, 0:32], in1=bgate_rep, op=ALU.add)
                        gv = g32[:].rearrange("p (d j h) -> p d j h", d=2, j=2)
                        P.x("act", "activation", reads=[Tg32], writes=[Tsp16], out=sp16[:].rearrange("p (d h) -> p d h", d=2), in_=gv[:, :, 1, :],
                                                                func=AF.Exp, scale=-1.0)
                        P.x("act", "activation", reads=[Tsp16], writes=[Tsp16], out=sp16[:], in_=sp16[:], func=AF.Ln, bias=1.0)
                        ps2, tp2 = ps_next()
                        P.x("pe", "matmul", reads=[Tsp16, Tcst], writes=[tp2], out=ps2[:, 0:8], lhsT=U_ap, rhs=sp16[:, 0:8], start=True, stop=True)
                        P.x("pe", "matmul", reads=[Tsp16, Tcst], writes=[tp2], out=ps2[:, 8:16], lhsT=L_ap, rhs=sp16[:, 8:16], start=True, stop=True)
                        P.x("pe", "matmul", reads=[Tsp16, Tcst], writes=[tp2], out=ps2[:, 16:32], lhsT=ones_ap, rhs=sp16[:, 0:16], start=True, stop=True)
                        P.x("dve", "tensor_copy", reads=[tp2], writes=[Tc32], out=c32[:], in_=ps2[:, 0:32])
                        P.x("dve", "tensor_tensor", reads=[Tg32, Tc32], writes=[Tt16], out=t16[:].rearrange("p (d h) -> p d h", d=2), in0=gv[:, :, 0, :],
                                                                            in1=c32[:, 0:16].rearrange("p (d h) -> p d h", d=2), op=ALU.add)
                        P.x("act", "activation", reads=[Tt16, Tcb], writes=[Tgs[c]], out=gs[:, c, 0:16], in_=t16[:], func=AF.Exp, bias=lnc_ap)
                        P.x("dve", "tensor_tensor", reads=[Tt16, Tc32], writes=[Tt16b], out=t16b[:], in0=t16[:], in1=c32[:, 16:32], op=ALU.subtract)
                        P.x("act", "activation", reads=[Tt16b, Tcb], writes=[Tgs[c]], out=gs[:, c, 16:32], in_=t16b[:], func=AF.Exp, bias=lnc_ap)
                        P.x("act", "activation", reads=[Tc32], writes=[Tgs[c]], out=gs[:, c, 32:48], in_=c32[:, 0:16], func=AF.Exp)
                        P.x("act", "activation", reads=[Tc32], writes=[Tgs[c]], out=gs[:, c, 48:64], in_=c32[:, 16:32], func=AF.Exp, scale=-1.0)
                    else:
                        for k in range(16):
                            P.x("pe", "matmul", reads=[ThT[k], Twg], writes=[tp], out=ps[:, 0:16], lhsT=hT[:, k, blk * 128:(blk + 1) * 128],
                                                                             rhs=wgo_t[:, k, :], start=(k == 0), stop=(k == 15))
                        P.x("dve", "tensor_tensor", reads=[tp, Trep], writes=[Tg32], out=g32[:, 0:16], in0=ps[:, 0:16], in1=bgo_rep, op=ALU.add)
                        P.x("act", "activation", reads=[Tg32], writes=[Tsp16], out=sp16[:, 0:8], in_=g32[:, 8:16], func=AF.Exp, scale=-1.0)
                        P.x("act", "activation", reads=[Tsp16], writes=[Tsp16], out=sp16[:, 0:8], in_=sp16[:, 0:8], func=AF.Ln, bias=1.0)
                        ps2, tp2 = ps_next()
                        P.x("pe", "matmul", reads=[Tsp16, Tcst], writes=[tp2], out=ps2[:, 0:8], lhsT=Mo_ap, rhs=sp16[:, 0:8], start=True, stop=True)
                        P.x("pe", "matmul", reads=[Tsp16, Tcst], writes=[tp2], out=ps2[:, 8:16], lhsT=ones_ap, rhs=sp16[:, 0:8], start=True, stop=True)
                        P.x("dve", "tensor_copy", reads=[tp2], writes=[Tc32], out=c32[:, 0:16], in_=ps2[:, 0:16])
                        P.x("dve", "tensor_tensor", reads=[Tg32, Tc32], writes=[Tt16], out=t16[:, 0:8], in0=g32[:, 0:8], in1=c32[:, 0:8], op=ALU.subtract)
                        P.x("act", "activation", reads=[Tt16, Tcb], writes=[Tt16], out=t16[:, 0:8], in_=t16[:, 0:8], func=AF.Exp, bias=lnc_ap)
                        P.x("act", "activation", reads=[Tpfx, Trep], writes=[Tt16b], out=t16b[:, 0:8], in_=pfx[:], func=AF.Exp, scale=flags[:, 3:4])
                        P.x("dve", "tensor_tensor", reads=[Tt16, Tt16b], writes=[Tgso[c]], out=gso[:, c, 0:8], in0=t16[:, 0:8], in1=t16b[:, 0:8], op=ALU.mult)
                        P.x("act", "activation", reads=[Tc32, Trep], writes=[Tgso[c]], out=gso[:, c, 8:16], in_=c32[:, 8:16], func=AF.Exp, scale=flags[:, 2:3])
                        P.x("dve", "tensor_tensor", reads=[Tpfx, Tc32], writes=[Tpfx], out=pfx[:], in0=pfx[:], in1=c32[:, 8:16], op=ALU.add)
                cast_some(cast_per_tile)
            for ch in range(16):
                if (not own) and ch < 8:
                    continue
                ri = rr[0] % 2
                rr[0] += 1
                r = raw[ri]
                P.x("dve", "tensor_copy", reads=[Tcarry[ch]], writes=[Traw[ri]], out=r[:, 0:4], in_=carry[:, ch, :])
                P.x("dve", "tensor_copy", reads=[Thalo, Traw[ri]], writes=[Traw[ri]], out=r[:, 4:6], in_=halo_raw[:, ch, lo + 2:lo + 4])
                dst = (q_s if ch < 8 else ks)[ch % 8]
                conv_emit(ri, ch, 2, dst, SEG, Tq_s if ch < 8 else Tks)

        phase1_segment(False)
        phase1_segment(True)
        cast_some(len(cast_rest))
        P.barrier()
        st1.__exit__(None, None, None)

        st2 = contextlib.ExitStack()
        st2.__enter__()
        Cst = sb("Cst", [128, NH, 256], F32, st2)
        nst = sb("nst", [128, NH], F32, st2)
        Cbf = sb("Cbf", [128, NH, 256], BF16, st2)
        nbf = sb("nbf", [128, NH], BF16, st2)
        Coth = sb("Coth", [128, NH, 256], F32, st2)
        noth = sb("noth", [128, NH], F32, st2)
        qsc = [sb("qsc%d" % i, [128, NH, 512], BF16, st2) for i in range(2)]
        ksc = [sb("ksc%d" % i, [128, NH, 512], BF16, st2) for i in range(2)]
        vsc = [sb("vsc%d" % i, [128, 4, W_V], BF16, st2) for i in range(2)]
        Tqsc = [T() for _ in range(2)]
        Tksc = [T() for _ in range(2)]
        Tvsc = [T() for _ in range(2)]
        Pm = sb("Pm", [128, NH, 128], BF16, st2)
        TPm = [T() for _ in range(NH)]
        kt = sb("kt", [128, NH, 128], BF16, st2)
        Tkt = [T() for _ in range(NH)]
        hout = [sb("hout%d" % i, [128, W_V], F32, st2) for i in range(3)]
        Thout = [[T(), T()] for _ in range(3)]
        hfl = [sb("hfl%d" % i, [128, W_V], F32, st2) for i in range(3)]
        Thfl = [T() for _ in range(3)]
        sgl = [sb("sgl%d" % i, [128, W_V], F32, st2) for i in range(3)]
        Tsgl = [T() for _ in range(3)]
        hsq = sb("hsq", [128, 256], F32, st2)
        Thsq = T()
        hab = [sb("hab%d" % i, [128, W_V], BF16, st2) for i in range(2)]
        Thab = [T() for _ in range(2)]
        rr8 = sb("rr8", [128, 8], F32, st2)
        Trr8 = T()
        ss8 = sb("ss8", [128, 8], F32, st2)
        Tss8 = T()
        tmpn = sb("tmpn", [128, 8], F32, st2)
        Ttmpn = T()

        def sweep(kind):
            own = kind != "oth"
            order = list(range(NB)) if kind != "bwd" else list(range(NB - 1, -1, -1))
            ksrc, Tksrc = (k_s, Tk_s) if own else (ko_s, Tko_s)
            vsrc, Tvsrc = (v_s, Tv_s) if own else (vo_s, Tvo_s)
            mask = U_ap if kind == "fwd" else L_ap
            if kind == "oth":
                for h in range(NH):
                    P.x("pool", "memset", writes=[TC[h]], args=(Cst[:, h, :], 0.0,))
                    P.x("pool", "memset", writes=[TCb[h]], args=(Cbf[:, h, :], 0.0,))
                P.x("pool", "memset", writes=[Tn], args=(nst[:], 0.0,))
                P.x("pool", "memset", writes=[Tnb], args=(nbf[:], 0.0,))
            else:
                fcol = flags[:, 0:1] if kind == "fwd" else flags[:, 1:2]
                for h in range(NH):
                    P.x("dve", "tensor_scalar", reads=[TCo, Trep], writes=[TC[h]], out=Cst[:, h, :], in0=Coth[:, h, :], scalar1=fcol, scalar2=None, op0=ALU.mult)
                    P.x("act", "copy", reads=[TC[h]], writes=[TCb[h]], out=Cbf[:, h, :], in_=Cst[:, h, :])
                P.x("dve", "tensor_scalar", reads=[TCo, Trep], writes=[Tn], out=nst[:], in0=noth[:], scalar1=fcol, scalar2=None, op0=ALU.mult)
                P.x("act", "copy", reads=[Tn], writes=[Tnb], out=nbf[:], in_=nst[:])
            if own:
                a_i, ap_i, ei_i, eb_i = (0, 16, 32, 48) if kind == "fwd" else (8, 24, 40, 56)
            else:
                ap_i, eb_i = 0, 8

            def load_sc(sci):
                sc = order[sci * 4] // 4
                b = sci % 2
                t0 = sc * 512
                P.dma("sp", ksc[b][:], ksrc[:, :, t0 + 2:t0 + 514].rearrange("h p t -> p h t"), reads=[Tksrc], writes=[Tksc[b]], sem="ksc%d" % b)
                P.dma("sp", vsc[b][:], vsrc[t0:t0 + 512, :].rearrange("(b p) f -> p b f", p=128), reads=[Tvsrc], writes=[Tvsc[b]], sem="vsc%d" % b)
                if own:
                    P.dma("sp", qsc[b][:], q_s[:, :, t0 + 2:t0 + 514].rearrange("h p t -> p h t"), reads=[Tq_s], writes=[Tqsc[b]], sem="qsc%d" % b)

            def prefetch_epi(ci):
                c = order[ci]
                trow = slice(c * 128, (c + 1) * 128)
                P.dma("sp", hfl[ci % 3][:], hf_s[trow, :], reads=[Thf_s], writes=[Thfl[ci % 3]], sem="hfl%d" % (ci % 3))
                P.dma("sp", sgl[ci % 3][:], sgo_s[trow, :], reads=[Tsgo_s], writes=[Tsgl[ci % 3]], sem="sgl%d" % (ci % 3))

            def epilogue(ci):
                c = order[ci]
                ho = hout[ci % 3]
                Tho = Thout[ci % 3]
                trow = slice(c * 128, (c + 1) * 128)
                if kind == "fwd":
                    P.dma("pool", hf_s[trow, :], ho[:], reads=Tho, writes=[Thf_s], sem="hout%d" % (ci % 3))
                elif kind == "bwd":
                    hf_, Thf_ = hfl[ci % 3], Thfl[ci % 3]
                    sg_, Tsg_ = sgl[ci % 3], Tsgl[ci % 3]
                    for h in range(NH):
                        P.x("act", "activation", reads=Tho + [Tss8], writes=[Thsq, Tss8], out=hsq[:], in_=ho[:, h * 256:(h + 1) * 256], func=AF.Square, accum_out=ss8[:, h:h + 1])
                    P.x("act", "activation", reads=[Tss8, Tcb], writes=[Tss8], out=ss8[:], in_=ss8[:], func=AF.Sqrt, scale=1.0 / DV, bias=eps_ap)
                    P.x("dve", "reciprocal", reads=[Tss8], writes=[Tss8], out=ss8[:], in_=ss8[:])
                    hb = hab[ci % 2]
                    Thb = Thab[ci % 2]
                    for h in range(NH):
                        P.x("dve", "scalar_tensor_tensor", reads=Tho + [Tss8, Tsg_], writes=[Thb], out=hb[:, h * 256:(h + 1) * 256], in0=ho[:, h * 256:(h + 1) * 256],
                            scalar=ss8[:, h:h + 1], in1=sg_[:, h * 256:(h + 1) * 256], op0=ALU.mult, op1=ALU.mult)
                    P.dma("pool", ha_s[trow, :], hb[:], reads=[Thb], writes=[Tha_s], sem="hab%d" % (ci % 2))

            nsc_tot = NB // 4
            pending = [None]
            load_sc(0)
            for ci, c in enumerate(order):
                sci = ci // 4
                if ci % 4 == 0 and sci + 1 < nsc_tot:
                    load_sc(sci + 1)
                if kind == "bwd":
                    prefetch_epi(ci)
                b = sci % 2
                cc = c % 4
                tsl = slice(cc * 128, (cc + 1) * 128)
                gsrc, Tg = (gs, Tgs[c]) if own else (gso, Tgso[c])
                ho = hout[ci % 3]
                Tho = Thout[ci % 3]
                psS = [ps_next() for _ in range(2)] if own else None
                if own:
                    for h in range(NH):
                        pS, tS = psS[h // 4]
                        hh = h % 4
                        P.x("pe", "matmul", reads=[Tksc[b], Tqsc[b]], writes=[tS], out=pS[:, hh * 128:(hh + 1) * 128], lhsT=ksc[b][:, h, tsl],
                            rhs=qsc[b][:, h, tsl], start=True, stop=True)
                psK, tK = ps_next()
                psKb = psK[:].bitcast(BF16)
                for h in range(NH):
                    P.x("pe", "transpose", reads=[Tksc[b], Tib], writes=[tK], args=(psKb[:, h * 128:(h + 1) * 128], ksc[b][:, h, tsl], identb[:],))
                if own:
                    for h in range(NH):
                        pS, tS = psS[h // 4]
                        hh = h % 4
                        P.x("dve", "scalar_tensor_tensor", reads=[tS, Tg, Tcst], writes=[TPm[h]], out=Pm[:, h, :], in0=pS[:, hh * 128:(hh + 1) * 128],
                            scalar=gsrc[:, c, a_i + h:a_i + h + 1], in1=mask, op0=ALU.mult, op1=ALU.mult)
                for h in range(NH):
                    if kind == "fwd":
                        P.x("dve", "tensor_scalar", reads=[tK, Tg], writes=[Tkt[h]], out=kt[:, h, :], in0=psKb[:, h * 128:(h + 1) * 128],
                            scalar1=gsrc[:, c, ap_i + h:ap_i + h + 1], scalar2=None, op0=ALU.mult)
                    else:
                        P.x("act", "activation", reads=[tK, Tg], writes=[Tkt[h]], out=kt[:, h, :], in_=psKb[:, h * 128:(h + 1) * 128], func=AF.Copy,
                            scale=gsrc[:, c, ap_i + h:ap_i + h + 1])
                psC = [ps_next() for _ in range(4)]
                psn, tn_ = ps_next()
                for h in range(NH):
                    pC, tC = psC[h // 2]
                    cs = slice((h % 2) * 256, (h % 2) * 256 + 256)
                    P.x("pe", "matmul", reads=[Tkt[h], Tvsc[b]], writes=[tC], out=pC[:, cs], lhsT=kt[:, h, :], rhs=vsc[b][:, cc, h * 256:(h + 1) * 256],
                        start=True, stop=True)
                    P.x("pe", "matmul", reads=[Tkt[h], Tib], writes=[tn_], out=psn[:, 8 + h:9 + h], lhsT=kt[:, h, :], rhs=onesb[:, 0:1], start=True, stop=True)
                for h in range(NH):
                    pC, tC = psC[h // 2]
                    cs = slice((h % 2) * 256, (h % 2) * 256 + 256)
                    P.x("dve", "scalar_tensor_tensor", reads=[TC[h], Tg, tC], writes=[TC[h]], out=Cst[:, h, :], in0=Cst[:, h, :],
                        scalar=gsrc[:, c, eb_i + h:eb_i + h + 1], in1=pC[:, cs], op0=ALU.mult, op1=ALU.add)
                P.x("dve", "tensor_tensor", reads=[Tn, Tg], writes=[Ttmpn], out=tmpn[:], in0=nst[:], in1=gsrc[:, c, eb_i:eb_i + 8], op=ALU.mult)
                P.x("dve", "tensor_tensor", reads=[Ttmpn, tn_, Tn], writes=[Tn], out=nst[:], in0=tmpn[:], in1=psn[:, 8:16], op=ALU.add)
                if own:
                    psR = [ps_next() for _ in range(4)]
                    psr, tr_ = ps_next()
                    for h in range(NH):
                        pR, tR = psR[h // 2]
                        cs = slice((h % 2) * 256, (h % 2) * 256 + 256)
                        P.x("pe", "matmul", reads=[TPm[h], Tvsc[b]], writes=[tR], out=pR[:, cs], lhsT=Pm[:, h, :], rhs=vsc[b][:, cc, h * 256:(h + 1) * 256],
                            start=True, stop=False)
                        P.x("pe", "matmul", reads=[Tqsc[b], TCb[h]], writes=[tR], out=pR[:, cs], lhsT=qsc[b][:, h, tsl], rhs=Cbf[:, h, :], start=False, stop=True)
                        P.x("pe", "matmul", reads=[TPm[h], Tib], writes=[tr_], out=psr[:, h:h + 1], lhsT=Pm[:, h, :], rhs=onesb[:, 0:1], start=True, stop=False)
                        P.x("pe", "matmul", reads=[Tqsc[b], Tnb], writes=[tr_], out=psr[:, h:h + 1], lhsT=qsc[b][:, h, tsl], rhs=nbf[:, h:h + 1], start=False, stop=True)
                if own:
                    cast_eng = (["dve", "dve", "pool", "dve", "dve", "dve", "pool", "dve"] if kind == "fwd"
                                else ["pool", "act", "pool", "act", "pool", "act", "pool", "act"])
                    for h in range(NH):
                        if cast_eng[h] == "act":
                            P.x("act", "copy", reads=[TC[h]], writes=[TCb[h]], out=Cbf[:, h, :], in_=Cst[:, h, :])
                        else:
                            P.x(cast_eng[h], "tensor_copy", reads=[TC[h]], writes=[TCb[h]], out=Cbf[:, h, :], in_=Cst[:, h, :])
                    P.x("dve", "tensor_copy", reads=[Tn, Tnb], writes=[Tnb], out=nbf[:], in_=nst[:])
                def r_evac(c=c, ci=ci, ho=ho, Tho=Tho, gsrc=gsrc, Tg=Tg, psR=(psR if own else None), psr=(psr if own else None), tr_=(tr_ if own else None)):
                    if own:
                        P.x("act", "activation", reads=[tr_], writes=[Trr8], out=rr8[:], in_=psr[:, 0:8], func=AF.Abs)
                        P.x("dve", "tensor_tensor", reads=[Trr8, Tg], writes=[Trr8], out=rr8[:], in0=rr8[:], in1=gsrc[:, c, ei_i:ei_i + 8], op=ALU.max)
                        P.x("dve", "reciprocal", reads=[Trr8], writes=[Trr8], out=rr8[:], in_=rr8[:])
                        for h in range(NH):
                            pR, tR = psR[h // 2]
                            cs = slice((h % 2) * 256, (h % 2) * 256 + 256)
                            if kind == "bwd":
                                P.x("dve", "scalar_tensor_tensor", reads=[tR, Trr8, Thfl[ci % 3]], writes=[Tho[h // 4]], out=ho[:, h * 256:(h + 1) * 256],
                                    in0=pR[:, cs], scalar=rr8[:, h:h + 1], in1=hfl[ci % 3][:, h * 256:(h + 1) * 256], op0=ALU.mult, op1=ALU.add)
                            elif h < 4 or kind == "fwd":
                                P.x("act", "activation", reads=[tR, Trr8], writes=[Tho[0]], out=ho[:, h * 256:(h + 1) * 256], in_=pR[:, cs], func=AF.Copy, scale=rr8[:, h:h + 1])
                            else:
                                P.x("dve", "tensor_scalar", reads=[tR, Trr8], writes=[Tho[1]], out=ho[:, h * 256:(h + 1) * 256], in0=pR[:, cs], scalar1=rr8[:, h:h + 1],
                                    scalar2=None, op0=ALU.mult)
                r_evac()
                if own and ci > 1:
                    epilogue(ci - 2)
            if own:
                epilogue(NB - 2)
                epilogue(NB - 1)
            if kind == "oth":
                for h in range(NH):
                    P.x("dve", "tensor_copy", reads=[TC[h], TCo], writes=[TCo], out=Coth[:, h, :], in_=Cst[:, h, :])
                P.x("dve", "tensor_copy", reads=[Tn, TCo], writes=[TCo], out=noth[:], in_=nst[:])

        ps_mod[0] = 8
        ps_rr[0] = 0
        sweep("oth")
        sweep("fwd")
        P.barrier()
        sweep("bwd")
        P.barrier()
        st2.__exit__(None, None, None)
        st12.__exit__(None, None, None)

        stf = contextlib.ExitStack()
        stf.__enter__()
        KB = 2 * NB
        Vt = [sb("Vt%d" % i, [KB, 128, 128], BF16, stf) for i in range(2)]
        TVt = [T() for _ in range(2)]
        At = sb("At", [128, 128, 2, 64], BF16, stf)
        TAt = T()
        Gt = sb("Gt", [128, 128, 2, NB], BF16, stf)
        TGt = T()
        F1 = sb("F1", [128, 3, 256], BF16, stf)
        TF1 = T()
        Yst = [sb("Yst%d" % i, [NB, 128, 64], BF16, stf) for i in range(2)]
        TYst = [T() for _ in range(2)]
        P.dma("sp", Gt[:], gtw_s.rearrange("p (s r j) -> p s r j", s=128, r=2), reads=[Tgtw_s], writes=[TGt], sem="f0")
        P.dma("sp", F1[:], dft_s.rearrange("p (a b) -> p a b", a=3), reads=[Tdft_s], writes=[TF1], sem="f1")
        Yv = Y_s.rearrange("(j s) c -> j s c", s=128)
        for gh in range(16):
            b = gh % 2
            P.dma("sp", Vt[b][:], V_s[gh].rearrange("(b a) x -> b a x", a=128), reads=[TV_s], writes=[TVt[b]], sem="Vt%d" % b)
            for cp in range(0, 64, 2):
                ps, tp = ps_next()
                for i in range(2):
                    c_ = cp + i
                    P.x("pe", "matmul", reads=[TVt[b], TF1], writes=[tp], out=ps[:, i * 256:(i + 1) * 256], lhsT=Vt[b][:, :, c_], rhs=F1[0:KB, 1, :],
                                                                  start=True, stop=False)
                    P.x("pe", "matmul", reads=[TVt[b], TF1], writes=[tp], out=ps[:, i * 256:(i + 1) * 256], lhsT=Vt[b][:, :, 64 + c_], rhs=F1[0:KB, 2, :],
                                                                  start=False, stop=True)
                eng = "act" if (cp // 2) % 2 == 0 else "dve"
                if eng == "act":
                    P.x("act", "copy", reads=[tp], writes=[TAt], out=At[:, :, :, cp:cp + 2].rearrange("p s r c -> p c r s"),
                                                             in_=ps[:].rearrange("p (c r s) -> p c r s", c=2, r=2))
                else:
                    P.x("dve", "tensor_copy", reads=[tp], writes=[TAt], out=At[:, :, :, cp:cp + 2].rearrange("p s r c -> p c r s"),
                                                                    in_=ps[:].rearrange("p (c r s) -> p c r s", c=2, r=2))
            yb = Yst[gh % 2]
            Tyb = TYst[gh % 2]
            for s0 in range(0, 128, 8):
                ps, tp = ps_next()
                for i in range(8):
                    s = s0 + i
                    P.x("pe", "matmul", reads=[TGt, TAt], writes=[tp], out=ps[0:NB, i * 64:(i + 1) * 64], lhsT=Gt[:, s, 0, :], rhs=At[:, s, 0, :],
                                                                start=True, stop=False)
                    P.x("pe", "matmul", reads=[TGt, TAt], writes=[tp], out=ps[0:NB, i * 64:(i + 1) * 64], lhsT=Gt[:, s, 1, :], rhs=At[:, s, 1, :],
                                                                start=False, stop=True)
                if (s0 // 8) % 2 == 0:
                    P.x("act", "copy", reads=[tp], writes=[Tyb], out=yb[:, s0:s0 + 8, :], in_=ps[0:NB, :].rearrange("p (s c) -> p s c", s=8))
                else:
                    P.x("dve", "tensor_copy", reads=[tp], writes=[Tyb], out=yb[:, s0:s0 + 8, :], in_=ps[0:NB, :].rearrange("p (s c) -> p s c", s=8))
            P.dma("pool", Yv[:, :, gh * 64:(gh + 1) * 64], yb[:], reads=[Tyb], writes=[TY_s], sem="Yst%d" % (gh % 2))
        P.barrier()
        stf.__exit__(None, None, None)

        ps_mod[0] = 7
        ps_rr[0] = 0
        st3 = contextlib.ExitStack()
        st3.__enter__()
        x_tm = [sb("x3_tm%d" % i, [128, D], F32, st3) for i in range(1)]
        Tx_tm = [T() for _ in range(1)]
        xT = sb("x3T", [128, 16, 512], F32, st3)
        TxT = [T() for _ in range(16)]
        hT = sb("h3T", [128, 16, 512], BF16, st3)
        ThT = [T() for _ in range(16)]
        mixT = sb("mixT", [128, 16, 512], F32, st3)
        TmixT = [T() for _ in range(16)]
        FF = sb("FF", [128, 44, 512], BF16, st3)
        TFF = [T() for _ in range(44)]
        sq = [sb("sq3%d" % i, [128, 512], BF16, st3) for i in range(2)]
        Tsq = [T() for _ in range(2)]
        rstd = sb("rstd3", [128, 512], F32, st3)
        Trstd = T()
        gA = sb("gA", [128, 4, 512], F32, st3)
        gB = sb("gB", [128, 4, 512], F32, st3)
        TgA = [T() for _ in range(4)]
        TgB = [T() for _ in range(4)]
        ws3 = WStream(2, st3, "b")
        order3 = []
        for jj in range(4):
            order3 += ["in%d" % (14 + jj), "in%d" % (18 + jj), "ao%d" % jj, "bo%d" % jj]
        order3 += ["wo%d" % i for i in range(4)]
        for jj in range(11):
            order3 += ["fi%d" % jj, "fi%d" % (11 + jj)]
        order3 += ["fo%d" % i for i in range(16)]
        ws3.set_order(order3 * NT)
        haT = FF[:, 0:16, :]
        ThaT = TFF[0:16]
        fbT = FF[:, 16:24, :]
        TfbT = TFF[16:24]
        mixinT = FF[:, 24:40, :]
        TmixinT = TFF[24:40]

        stats3 = Stats(sq, Tsq)

        def post_norm(src, Tsrc, gcol, next_stats):
            stats3.finish(rstd, Trstd)
            for c in range(16):
                P.x("dve", "scalar_tensor_tensor", reads=[Tsrc[c], Trstd, Tvec], writes=[Tsrc[c]], out=src[:, c, :], in0=src[:, c, :],
                    scalar=gcol[:, c:c + 1], in1=rstd[:], op0=ALU.mult, op1=ALU.mult)
                P.x("pool" if c % 2 == 0 else "dve", "tensor_tensor", reads=[TxT[c], Tsrc[c]], writes=[TxT[c]], out=xT[:, c, :], in0=xT[:, c, :],
                    in1=src[:, c, :], op=ALU.add)
                if next_stats:
                    stats3.add(xT[:, c:c + 1, :], [TxT[c]], 1, 0, 512, c == 0, c == 15)

        for it in range(NT):
            tok0 = it * 512
            ws3.prefetch()
            load_xT(x_own, tok0, x_tm, Tx_tm, xT, TxT, stats=stats3)
            for k in range(16):
                P.dma("sp", haT[:, k, :], ha_s[tok0: tok0 + 512, k * 128:(k + 1) * 128],
                      reads=[Tha_s], writes=[ThaT[k]], sem="haT%d" % (k % 4), transpose=True)
            for k in range(8):
                P.dma("sp", fbT[:, k, :], Y_s[tok0: tok0 + 512, k * 128:(k + 1) * 128],
                      reads=[TY_s], writes=[TfbT[k]], sem="fbT%d" % (k % 2), transpose=True)
            stats3.finish(rstd, Trstd)
            fm_apply(xT, TxT, g_pre_mix, hT, ThT, rstd, Trstd)
            for jj in range(4):
                wA, TwA = ws3.get("in%d" % (14 + jj))
                for j4 in range(4):
                    j = jj * 4 + j4
                    ps, tp = ps_next()
                    mm_ii(ps, tp, wA, TwA, j4 * 128, hT, ThT, 16)
                    P.x("act", "activation", reads=[tp, Tvec], writes=[TgA[j4]], out=gA[:, j4, :], in_=ps[:], func=AF.Sigmoid, bias=b_merge[:, j:j + 1])
                wB, TwB = ws3.get("in%d" % (18 + jj))
                for j4 in range(4):
                    j = jj * 4 + j4
                    ps, tp = ps_next()
                    mm_ii(ps, tp, wB, TwB, j4 * 128, hT, ThT, 16)
                    P.x("act", "activation", reads=[tp, Tvec], writes=[TgB[j4]], out=gB[:, j4, :], in_=ps[:], func=AF.Sigmoid, bias=b_merge[:, 16 + j:17 + j])
                wao, Twao = ws3.get("ao%d" % jj)
                for j4 in range(4):
                    ps, tp = ps_next()
                    mm_ii(ps, tp, wao, Twao, j4 * 128, haT, ThaT, 16)
                    P.x("dve", "tensor_tensor", reads=[tp, TgA[j4]], writes=[TgA[j4]], out=gA[:, j4, :], in0=gA[:, j4, :], in1=ps[:], op=ALU.mult)
                wbo, Twbo = ws3.get("bo%d" % jj)
                for j4 in range(4):
                    j = jj * 4 + j4
                    ps, tp = ps_next()
                    mm_ii(ps, tp, wbo, Twbo, j4 * 128, fbT, TfbT, 8)
                    P.x("dve", "tensor_tensor", reads=[tp, TgB[j4]], writes=[TgB[j4]], out=gB[:, j4, :], in0=gB[:, j4, :], in1=ps[:], op=ALU.mult)
                    P.x("pool", "tensor_tensor", reads=[TgA[j4], TgB[j4]], writes=[TmixinT[j]], out=mixinT[:, j, :], in0=gA[:, j4, :], in1=gB[:, j4, :], op=ALU.add)
            if it == 0 and "dbg_ff" in DEBUG_OUT:
                dbg_h = dscr("dbg_h", [128, 16 * 512], BF16)
                P.dma("pool", dbg_h, hT[:].rearrange("p a b -> p (a b)"), reads=ThT, sem="dbg")
                dbg_g = dscr("dbg_g", [128, 8 * 512], F32)
                P.dma("pool", dbg_g[:, 0:2048], gA[:].rearrange("p a b -> p (a b)"), reads=TgA, sem="dbg")
                P.dma("pool", dbg_g[:, 2048:4096], gB[:].rearrange("p a b -> p (a b)"), reads=TgB, sem="dbg")
                dbg_ff = dscr("dbg_ff", [128, 40 * 512], BF16)
                P.dma("pool", dbg_ff, FF[:, 0:40, :].rearrange("p a b -> p (a b)"), reads=TFF[0:40], sem="dbg")
            for jj in range(4):
                wo, Two = ws3.get("wo%d" % jj)
                for j4 in range(4):
                    j = jj * 4 + j4
                    ps, tp = ps_next()
                    mm_ii(ps, tp, wo, Two, j4 * 128, mixinT, TmixinT, 16)
                    P.x("act", "copy", reads=[tp], writes=[TmixT[j]], out=mixT[:, j, :], in_=ps[:])
                    stats3.add(mixT[:, j:j + 1, :], [TmixT[j]], 1, 0, 512, j == 0, j == 15)
            post_norm(mixT, TmixT, g_post_mix, True)
            stats3.finish(rstd, Trstd)
            fm_apply(xT, TxT, g_pre_ffn, hT, ThT, rstd, Trstd)
            for jj in range(11):
                wg_, Twg_ = ws3.get("fi%d" % jj)
                for j4 in range(4):
                    ps, tp = ps_next()
                    mm_ii(ps, tp, wg_, Twg_, j4 * 128, hT, ThT, 16)
                    P.x("act", "activation", reads=[tp], writes=[TgA[j4]], out=gA[:, j4, :], in_=ps[:], func=AF.Silu)
                wu_, Twu_ = ws3.get("fi%d" % (11 + jj))
                for j4 in range(4):
                    j = jj * 4 + j4
                    ps, tp = ps_next()
                    mm_ii(ps, tp, wu_, Twu_, j4 * 128, hT, ThT, 16)
                    P.x("dve", "tensor_tensor", reads=[tp, TgA[j4]], writes=[TFF[j]], out=FF[:, j, :], in0=gA[:, j4, :], in1=ps[:], op=ALU.mult)
            for j in range(16):
                wf, Twf = ws3.get("fo%d" % j)
                ps, tp = ps_next()
                mm_ii(ps, tp, wf, Twf, 0, FF, TFF, 44)
                P.x("act", "copy", reads=[tp], writes=[TmixT[j]], out=mixT[:, j, :], in_=ps[:])
                stats3.add(mixT[:, j:j + 1, :], [TmixT[j]], 1, 0, 512, j == 0, j == 15)
            post_norm(mixT, TmixT, g_post_ffn, False)
            for blk in range(4):
                b2 = 0
                for cg in range(4):
                    ps, tp = ps_next()
                    for i in range(4):
                        c = cg * 4 + i
                        P.x("pe", "transpose", reads=[TxT[c], Tcst], writes=[tp], args=(ps[:, i * 128:(i + 1) * 128], xT[:, c, blk * 128:(blk + 1) * 128], ident_ap,))
                    P.x("act", "copy", reads=[tp], writes=[Tx_tm[b2]], out=x_tm[b2][:, cg * 512:(cg + 1) * 512], in_=ps[:])
                P.dma("pool", y_out[tok0 + blk * 128: tok0 + (blk + 1) * 128, :], x_tm[b2][:], reads=[Tx_tm[b2]], sem="yo%d" % b2)
        P.finish()
        st3.__exit__(None, None, None)
        P.emit()
    return nc


def _fm(v):
    return np.ascontiguousarray(np.asarray(v, np.float32).reshape(-1, 128).T)


def core_inputs(NB, role, x_own, x_oth, halo8, W, seq_T):
    SEG = NB * 128
    hf = role["hf"]
    prompt = role["prompt"]
    x_halo = np.zeros((128, D), np.float32)
    x_halo[0:8] = halo8
    w_in = W["w_in"]
    bg = W["b_gates"]
    if hf == 1:
        w_go = np.concatenate([w_in[:, OFF_G + 0:OFF_G + 8], w_in[:, OFF_G + 8:OFF_G + 16]], axis=1)
        b_go = np.concatenate([bg[0:8], bg[8:16]])
    else:
        w_go = np.concatenate([w_in[:, OFF_G + 16:OFF_G + 24], w_in[:, OFF_G + 24:OFF_G + 32]], axis=1)
        b_go = np.concatenate([bg[16:24], bg[24:32]])
    vecs = np.concatenate([_fm(W["g_pre_mix"]), _fm(W["g_post_mix"]), _fm(W["g_pre_ffn"]), _fm(W["g_post_ffn"]),
                           _fm(W["b_merge"])], axis=1)
    convw = np.ascontiguousarray(W["conv_w"].reshape(5, 16, 128).transpose(2, 1, 0))
    flagP = 1.0 if (prompt and hf == 1) else 0.0
    flagS = 1.0 if (prompt and hf == 0) else 0.0
    rep = np.concatenate([np.tile(W["g_head"][None, :], (128, 1)), np.tile(bg[None, :], (128, 1)),
                          np.tile(b_go[None, :], (128, 1)),
                          np.tile(np.array([[flagP, flagS, -flagP, -flagS]], np.float32), (128, 1))], axis=1).astype(np.float32)
    idx = np.arange(128)
    U = (idx[:, None] <= idx[None, :]).astype(np.float32)
    L = (idx[:, None] >= idx[None, :]).astype(np.float32)
    Mo = (idx[:, None] > idx[None, :]).astype(np.float32) if hf == 1 else (idx[:, None] < idx[None, :]).astype(np.float32)
    cst = np.stack([U, L, Mo, np.ones((128, 128), np.float32), np.eye(128, dtype=np.float32)], axis=1)
    c = np.arange(128, dtype=np.float64)
    ang = 2 * np.pi * np.outer(c, c) / 128.0
    chd = np.concatenate([np.cos(ang), -np.sin(ang)], axis=1)
    N = 2 * SEG
    s = np.arange(128)
    j = np.arange(NB)
    tok = s[:, None] + 128 * j[None, :]
    kp = (tok + SEG * hf) if prompt else 2 * tok
    KB = 2 * NB
    kap = kp[:, 0] % KB
    bpart = np.arange(KB)
    b_true = np.where(bpart < NB, bpart + NB * hf, (bpart - NB) + NB * (1 - hf)).astype(np.float64)
    phi = 2 * np.pi * np.outer(b_true, kap) / float(KB)
    Fr, Fi = np.cos(phi), -np.sin(phi)
    F1a = np.zeros((128, 256))
    F1b = np.zeros((128, 256))
    F1a[:KB] = np.concatenate([Fr, Fi], axis=1)
    F1b[:KB] = np.concatenate([-Fi, Fr], axis=1)
    dftc = np.stack([chd, F1a, F1b], axis=1).astype(np.float32)
    scale = 1.0 / np.sqrt(float(seq_T) * 128.0)
    a = np.arange(128, dtype=np.float64)
    th = 2 * np.pi * (a[:, None, None] * (kp[None, :, :] % N)) / N
    Gr = scale * np.cos(th)
    nGi = scale * np.sin(th)
    gtw = np.stack([Gr, nGi], axis=2).reshape(128, -1).astype(np.float32)
    return {
        "x_own": np.ascontiguousarray(x_own, np.float32), "x_oth": np.ascontiguousarray(x_oth, np.float32), "x_halo": x_halo,
        "w_in": W["w_in"], "w_a_out": W["w_a_out"], "w_b_out": W["w_b_out"], "w_out": W["w_out"],
        "w_ffn_in": W["w_ffn_in"], "w_ffn_out": W["w_ffn_out"], "w_go": np.ascontiguousarray(w_go, np.float32),
        "vecs": np.ascontiguousarray(vecs, np.float32), "convw": convw.astype(np.float32), "rep": rep, "cst": cst,
        "dftc": dftc, "gtw": gtw,
    }


def make_in_maps(NB, seqs, W):
    SEG = NB * 128
    maps, plan = [], []
    z2 = np.zeros((2, D), np.float32)
    for si, xs in enumerate(seqs):
        Tn = xs.shape[0]
        if Tn == SEG:
            halo8 = np.zeros((8, D), np.float32)
            maps.append(core_inputs(NB, dict(hf=0, prompt=False), xs, np.zeros((SEG, D), np.float32), halo8, W, Tn))
            plan.append((si, 0))
        else:
            assert Tn == 2 * SEG
            a, b = xs[:SEG], xs[SEG:]
            h0 = np.concatenate([z2, b[0:2], a[-2:], z2], axis=0)
            h1 = np.concatenate([a[-2:], z2, z2, b[0:2]], axis=0)
            maps.append(core_inputs(NB, dict(hf=0, prompt=True), a, b, h0, W, Tn))
            plan.append((si, 0))
            maps.append(core_inputs(NB, dict(hf=1, prompt=True), b, a, h1, W, Tn))
            plan.append((si, 1))
    return maps, plan


_W_KEYS = ["g_pre_mix", "w_in", "conv_w", "b_gates", "g_head", "w_a_out", "w_b_out", "b_merge", "w_out", "g_post_mix",
           "g_pre_ffn", "w_ffn_in", "w_ffn_out", "g_post_ffn"]


def run_seqs(NB, seqs, weights):
    W = {k: np.ascontiguousarray(np.asarray(weights[k], np.float32)[0]) for k in _W_KEYS}
    maps, plan = make_in_maps(NB, seqs, W)
    nc = build(NB)
    res = run_bass_kernel_spmd(nc, maps, core_ids=list(range(len(maps))))
    SEG = NB * 128
    outs = [np.zeros_like(np.asarray(s, np.float32)) for s in seqs]
    for (si, hf), r in zip(plan, res.results):
        outs[si][hf * SEG:(hf + 1) * SEG] = r["y_out"]
    if DEBUG_OUT:
        return outs, res.results
    return outs


def kernel(**inputs):
    xp = np.asarray(inputs["x_prompt"], np.float32)
    xs = np.asarray(inputs["x_sample"], np.float32)
    seqs = [xp[0], xp[1], xs[0], xs[1], xs[2], xs[3]]
    outs = run_seqs(64, seqs, inputs)
    y_prompt = np.stack(outs[0:2], axis=0)
    y_sample = np.stack(outs[2:6], axis=0)
    return (y_prompt, y_sample)
```
